# Optimizing a Trainium2 kernel written in Bass

```python
import jax, jax.numpy as jnp
from jax import lax
import numpy as np

D_MODEL = 2048
BATCH = 4
SEQ = 2048
DEPTH = 1

RWKV_HEADS = 16
RWKV_HEAD_DIM = 64
RWKV_WIDTH = RWKV_HEADS * RWKV_HEAD_DIM
LORA_DECAY = 96
LORA_ICLR = 96
LORA_GATE = 256
GN_EPS = 64e-5
RWKV_COLS = 3 * RWKV_WIDTH + LORA_DECAY + LORA_ICLR + LORA_GATE
NSA_HEADS = 16
NSA_KV_HEADS = 4
NSA_GROUP = NSA_HEADS // NSA_KV_HEADS
NSA_HEAD_DIM = 64
NSA_Q_WIDTH = NSA_HEADS * NSA_HEAD_DIM
NSA_KV_WIDTH = NSA_KV_HEADS * NSA_HEAD_DIM
CMP_BLOCK = 32
CMP_STRIDE = 16
CMP_HIDDEN = 256
SEL_BLOCK = 64
SEL_TOPN = 16
WINDOW = 512
SEL_QBLK = 64
WIN_QBLK = 128
NSA_COLS = NSA_Q_WIDTH + 6 * NSA_KV_WIDTH + 3 * NSA_HEADS
GATE_COLS = 2 * D_MODEL
IN_COLS = RWKV_COLS + NSA_COLS + GATE_COLS
D_FF = 4 * D_MODEL
NORM_EPS = 1e-5
NEG_INF = -1e30
TINY = 1e-30

kernel_name = "rwkv7_nsa_gated_hybrid_block"


def rmsnorm(x, g):
    x32 = x.astype(jnp.float32)
    y = x32 * lax.rsqrt(jnp.mean(x32 * x32, axis=-1, keepdims=True) + NORM_EPS)
    return (y * g.astype(jnp.float32)).astype(x.dtype)


def masked_softmax(s, mask):
    s = jnp.where(mask, s, NEG_INF)
    s = s - jnp.max(s, axis=-1, keepdims=True)
    p = jnp.exp(s) * mask.astype(jnp.float32)
    return p / jnp.maximum(jnp.sum(p, axis=-1, keepdims=True), TINY)


def alibi_slopes(n):
    return 2.0 ** (-8.0 * jnp.arange(1, n + 1, dtype=jnp.float32) / n)


def to_chunks(t, axis, size):
    shp = t.shape
    t = t.reshape(shp[:axis] + (shp[axis] // size, size) + shp[axis + 1:])
    return jnp.moveaxis(t, axis, 0)


def from_chunks(o):
    o = jnp.moveaxis(o, 0, 3)
    b, g, hg, nc, qb, d = o.shape
    return o.reshape(b, g, hg, nc * qb, d).transpose(0, 3, 1, 2, 4)


def rwkv7_time_mix(z, mu, w0, w_up, a0, a_up, g_up, k_k, k_a, r_k, lnx_w, lnx_b):
    B, T, _ = z.shape
    H, N = RWKV_HEADS, RWKV_HEAD_DIM
    z_prev = jnp.pad(z, ((0, 0), (1, 0), (0, 0)))[:, :-1]
    z = z + (z_prev - z) * mu
    offs = [int(o) for o in np.cumsum([RWKV_WIDTH, RWKV_WIDTH, RWKV_WIDTH, LORA_DECAY, LORA_ICLR])]
    r, k, v, w_lo, a_lo, g_lo = jnp.split(z, offs, axis=-1)
    w_log = -jax.nn.softplus(-(w0 + jnp.tanh(w_lo) @ w_up)) - 0.5
    a = jax.nn.sigmoid(a0 + a_lo @ a_up)
    g = jax.nn.sigmoid(g_lo) @ g_up
    heads = lambda t: t.astype(jnp.float32).reshape(B, T, H, N)
    r, k, v, a = heads(r), heads(k), heads(v), heads(a)
    decay = jnp.exp(-jnp.exp(heads(w_log)))
    kk = k * k_k
    kk = kk * lax.rsqrt(jnp.maximum(jnp.sum(kk * kk, axis=-1, keepdims=True), 1e-24))
    k = k * (1.0 + (a - 1.0) * k_a)

    def step(S, inp):
        r_t, w_t, k_t, v_t, kk_t, a_t = inp
        sa = jnp.einsum('bhij,bhj->bhi', S, -kk_t)
        S = S * w_t[:, :, None, :] + sa[..., None] * (kk_t * a_t)[:, :, None, :] + v_t[..., None] * k_t[:, :, None, :]
        return S, jnp.einsum('bhij,bhj->bhi', S, r_t)

    xs = tuple(jnp.moveaxis(t, 1, 0) for t in (r, decay, k, v, kk, a))
    _, y = lax.scan(step, jnp.zeros((B, H, N, N), jnp.float32), xs)
    y = jnp.moveaxis(y, 0, 1)
    mean = jnp.mean(y, axis=-1, keepdims=True)
    var = jnp.mean(jnp.square(y - mean), axis=-1, keepdims=True)
    y = ((y - mean) * lax.rsqrt(var + GN_EPS)).reshape(B, T, RWKV_WIDTH) * lnx_w + lnx_b
    bonus = (jnp.sum(r * k * r_k, axis=-1, keepdims=True) * v).reshape(B, T, RWKV_WIDTH)
    return ((y + bonus) * g).astype(z.dtype)


def compress(kv, pe, w1, w2):
    B, T, G, d = kv.shape
    n_cmp = (T - CMP_BLOCK) // CMP_STRIDE + 1
    idx = np.arange(n_cmp)[:, None] * CMP_STRIDE + np.arange(CMP_BLOCK)[None, :]
    blk = kv[:, idx] + pe[:, None, :]
    blk = blk.transpose(0, 1, 3, 2, 4).reshape(B, n_cmp, G, CMP_BLOCK * d)
    return jax.nn.gelu(blk @ w1) @ w2


def nsa_attention(q, kc_raw, vc_raw, ks, vs, kw, vw, gate_logits,
                  cmp_pe_k, cmp_w1_k, cmp_w2_k, cmp_pe_v, cmp_w1_v, cmp_w2_v):
    B, T, _ = q.shape
    G, Hg, d = NSA_KV_HEADS, NSA_GROUP, NSA_HEAD_DIM
    scale = d ** -0.5
    kvh = lambda t: t.reshape(B, T, G, d)
    qg = q.reshape(B, T, G, Hg, d)
    slopes = alibi_slopes(NSA_HEADS).reshape(G, Hg)[None, :, :, None, None]
    tpos = jnp.arange(T)

    kc = compress(kvh(kc_raw), cmp_pe_k, cmp_w1_k, cmp_w2_k)
    vc = compress(kvh(vc_raw), cmp_pe_v, cmp_w1_v, cmp_w2_v)
    n_cmp = kc.shape[1]
    ends = jnp.arange(n_cmp) * CMP_STRIDE + (CMP_BLOCK - 1)
    dist_c = (tpos[:, None] - ends[None, :]).astype(jnp.float32)
    s_c = jnp.einsum('btghd,bngd->bghtn', qg, kc).astype(jnp.float32) * scale - slopes * dist_c
    p_cmp = masked_softmax(s_c, dist_c >= 0)
    o_cmp = jnp.einsum('bghtn,bngd->btghd', p_cmp.astype(vc.dtype), vc)

    n_sel = T // SEL_BLOCK
    n_top = min(SEL_TOPN, n_sel)
    cs = np.arange(n_cmp)[:, None] * CMP_STRIDE
    ss = np.arange(n_sel)[None, :] * SEL_BLOCK
    overlap = np.clip(np.minimum(cs + CMP_BLOCK, ss + SEL_BLOCK) - np.maximum(cs, ss), 0, None) / CMP_BLOCK
    imp = jnp.einsum('bghtn,nj->bgtj', p_cmp, jnp.asarray(overlap, jnp.float32))
    blk = jnp.arange(n_sel)[None, :]
    cur = (tpos // SEL_BLOCK)[:, None]
    forced = (blk == 0) | (blk == cur) | (blk == cur - 1)
    score = jnp.where(blk > cur, NEG_INF, jnp.where(forced, -NEG_INF, imp))
    top_val, top_idx = lax.top_k(score, n_top)
    top_ok = top_val > 0.5 * NEG_INF

    qh = qg.transpose(0, 2, 3, 1, 4)

    k_blk = kvh(ks).transpose(0, 2, 1, 3).reshape(B, G, n_sel, SEL_BLOCK, d)
    v_blk = kvh(vs).transpose(0, 2, 1, 3).reshape(B, G, n_sel, SEL_BLOCK, d)
    bi = jnp.arange(B)[:, None, None, None]
    gi = jnp.arange(G)[None, :, None, None]

    def sel_block(args):
        q_c, idx_c, ok_c, t0 = args
        kg = k_blk[bi, gi, idx_c].reshape(B, G, SEL_QBLK, n_top * SEL_BLOCK, d)
        vg = v_blk[bi, gi, idx_c].reshape(B, G, SEL_QBLK, n_top * SEL_BLOCK, d)
        pos = (idx_c[..., None] * SEL_BLOCK + jnp.arange(SEL_BLOCK)).reshape(B, G, SEL_QBLK, n_top * SEL_BLOCK)
        tq = t0 + jnp.arange(SEL_QBLK)
        dist = (tq[:, None] - pos).astype(jnp.float32)
        mask = jnp.repeat(ok_c, SEL_BLOCK, axis=-1) & (dist >= 0)
        s = jnp.einsum('bghqd,bgqkd->bghqk', q_c, kg).astype(jnp.float32) * scale - slopes * dist[:, :, None]
        p = masked_softmax(s, mask[:, :, None])
        return jnp.einsum('bghqk,bgqkd->bghqd', p.astype(vg.dtype), vg)

    o_slc = from_chunks(lax.map(sel_block, (to_chunks(qh, 3, SEL_QBLK), to_chunks(top_idx, 2, SEL_QBLK),
                                            to_chunks(top_ok, 2, SEL_QBLK), jnp.arange(T // SEL_QBLK) * SEL_QBLK)))

    pad = ((0, 0), (0, 0), (WINDOW, 0), (0, 0))
    kwp = jnp.pad(kvh(kw).transpose(0, 2, 1, 3), pad)
    vwp = jnp.pad(kvh(vw).transpose(0, 2, 1, 3), pad)
    span = WINDOW + WIN_QBLK

    def win_block(args):
        q_c, t0 = args
        kb = lax.dynamic_slice_in_dim(kwp, t0, span, axis=2)
        vb = lax.dynamic_slice_in_dim(vwp, t0, span, axis=2)
        tq = t0 + jnp.arange(WIN_QBLK)
        pos = t0 - WINDOW + jnp.arange(span)
        dist = (tq[:, None] - pos[None, :]).astype(jnp.float32)
        mask = (pos[None, :] >= 0) & (dist >= 0) & (dist < WINDOW)
        s = jnp.einsum('bghqd,bgkd->bghqk', q_c, kb).astype(jnp.float32) * scale - slopes * dist
        p = masked_softmax(s, mask)
        return jnp.einsum('bghqk,bgkd->bghqd', p.astype(vb.dtype), vb)

    o_win = from_chunks(lax.map(win_block, (to_chunks(qh, 3, WIN_QBLK), jnp.arange(T // WIN_QBLK) * WIN_QBLK)))

    gates = jax.nn.sigmoid(gate_logits.astype(jnp.float32)).reshape(B, T, 3, G, Hg, 1)
    o = gates[:, :, 0] * o_cmp + gates[:, :, 1] * o_slc + gates[:, :, 2] * o_win
    return o.reshape(B, T, NSA_Q_WIDTH).astype(q.dtype)


def setup_inputs(seed: int = 0) -> dict:
    key = jax.random.key(seed)
    ks = jax.random.split(key, 32)
    nrm = lambda k, shape, sc: sc * jax.random.normal(k, shape, jnp.float32)
    L, H, N, d = DEPTH, RWKV_HEADS, RWKV_HEAD_DIM, NSA_HEAD_DIM
    return {
        "x": nrm(ks[0], (BATCH, SEQ, D_MODEL), 1.0),
        "norm_mix": 1.0 + nrm(ks[1], (L, D_MODEL), 0.02),
        "w_in": nrm(ks[2], (L, D_MODEL, IN_COLS), D_MODEL ** -0.5),
        "rwkv_mu": jax.random.uniform(ks[3], (L, RWKV_COLS), jnp.float32),
        "rwkv_w0": jnp.linspace(-6.0, -0.5, RWKV_WIDTH)[None, :] + nrm(ks[4], (L, RWKV_WIDTH), 0.1),
        "rwkv_w_up": nrm(ks[5], (L, LORA_DECAY, RWKV_WIDTH), 0.1 * LORA_DECAY ** -0.5),
        "rwkv_a0": nrm(ks[6], (L, RWKV_WIDTH), 0.1),
        "rwkv_a_up": nrm(ks[7], (L, LORA_ICLR, RWKV_WIDTH), LORA_ICLR ** -0.5),
        "rwkv_g_up": nrm(ks[8], (L, LORA_GATE, RWKV_WIDTH), LORA_GATE ** -0.5),
        "rwkv_k_k": 0.85 + nrm(ks[9], (L, H, N), 0.02),
        "rwkv_k_a": 1.0 + nrm(ks[10], (L, H, N), 0.02),
        "rwkv_r_k": nrm(ks[11], (L, H, N), 0.1),
        "rwkv_lnx_w": 1.0 + nrm(ks[12], (L, RWKV_WIDTH), 0.02),
        "rwkv_lnx_b": nrm(ks[13], (L, RWKV_WIDTH), 0.02),
        "cmp_pe_k": nrm(ks[14], (L, CMP_BLOCK, d), 0.02),
        "cmp_w1_k": nrm(ks[15], (L, CMP_BLOCK * d, CMP_HIDDEN), (CMP_BLOCK * d) ** -0.5),
        "cmp_w2_k": nrm(ks[16], (L, CMP_HIDDEN, d), CMP_HIDDEN ** -0.5),
        "cmp_pe_v": nrm(ks[17], (L, CMP_BLOCK, d), 0.02),
        "cmp_w1_v": nrm(ks[18], (L, CMP_BLOCK * d, CMP_HIDDEN), (CMP_BLOCK * d) ** -0.5),
        "cmp_w2_v": nrm(ks[19], (L, CMP_HIDDEN, d), CMP_HIDDEN ** -0.5),
        "w_out_rwkv": nrm(ks[20], (L, RWKV_WIDTH, D_MODEL), RWKV_WIDTH ** -0.5),
        "w_out_nsa": nrm(ks[21], (L, NSA_Q_WIDTH, D_MODEL), NSA_Q_WIDTH ** -0.5),
        "w_o": nrm(ks[22], (L, D_MODEL, D_MODEL), D_MODEL ** -0.5),
        "norm_mlp": 1.0 + nrm(ks[23], (L, D_MODEL), 0.02),
        "mlp_w_up": nrm(ks[24], (L, D_MODEL, D_FF), D_MODEL ** -0.5),
        "mlp_w_down": nrm(ks[25], (L, D_FF, D_MODEL), D_FF ** -0.5),
        "norm_final": 1.0 + nrm(ks[26], (D_MODEL,), 0.02),
    }


def reference(x, norm_mix, w_in, rwkv_mu, rwkv_w0, rwkv_w_up, rwkv_a0, rwkv_a_up, rwkv_g_up,
              rwkv_k_k, rwkv_k_a, rwkv_r_k, rwkv_lnx_w, rwkv_lnx_b,
              cmp_pe_k, cmp_w1_k, cmp_w2_k, cmp_pe_v, cmp_w1_v, cmp_w2_v,
              w_out_rwkv, w_out_nsa, w_o, norm_mlp, mlp_w_up, mlp_w_down, norm_final):
    offs = [int(o) for o in np.cumsum([RWKV_COLS, NSA_Q_WIDTH] + [NSA_KV_WIDTH] * 6 + [3 * NSA_HEADS, D_MODEL])]
    h = x
    for l in range(DEPTH):
        xn = rmsnorm(h, norm_mix[l])
        proj = xn @ w_in[l]
        z_rwkv, q, kc, vc, ks, vs, kw, vw, nsa_gate, gate_a, gate_b = jnp.split(proj, offs, axis=-1)
        y_a = rwkv7_time_mix(z_rwkv, rwkv_mu[l], rwkv_w0[l], rwkv_w_up[l], rwkv_a0[l], rwkv_a_up[l],
                             rwkv_g_up[l], rwkv_k_k[l], rwkv_k_a[l], rwkv_r_k[l], rwkv_lnx_w[l], rwkv_lnx_b[l])
        y_b = nsa_attention(q, kc, vc, ks, vs, kw, vw, nsa_gate, cmp_pe_k[l], cmp_w1_k[l], cmp_w2_k[l],
                            cmp_pe_v[l], cmp_w1_v[l], cmp_w2_v[l])
        mixed = jax.nn.sigmoid(gate_a) * (y_a @ w_out_rwkv[l]) + jax.nn.sigmoid(gate_b) * (y_b @ w_out_nsa[l])
        h = h + mixed @ w_o[l]
        hn = rmsnorm(h, norm_mlp[l])
        h = h + jnp.square(jax.nn.relu(hn @ mlp_w_up[l])) @ mlp_w_down[l]
    return rmsnorm(h, norm_final)
```

```python
import numpy as np
import concourse.bass as bass
import concourse.mybir as mybir
from concourse.bass_utils import run_bass_kernel_spmd

F32 = mybir.dt.float32
BF16 = mybir.dt.bfloat16
AF = mybir.ActivationFunctionType
ALU = mybir.AluOpType
AX = mybir.AxisListType

D = 2048
L = 2048
NO = 1024
KC = 16
DFF = 8192
IN_COLS = 10224
EPS = 1e-5


class Prog:
    def __init__(self, nc):
        self.nc = nc
        self.ops = []
        self.lastw = {}
        self.readers = {}
        self.bank_i = 0

    def _deps(self, idx, r, w):
        deps = set()
        for k in r:
            if k in self.lastw:
                deps.add(self.lastw[k])
        for k in w:
            if k in self.lastw:
                deps.add(self.lastw[k])
            deps.update(self.readers.get(k, ()))
        for k in r:
            self.readers.setdefault(k, []).append(idx)
        for k in w:
            self.lastw[k] = idx
            self.readers[k] = []
        deps.discard(idx)
        return deps

    def op(self, eng, fn, r=(), w=()):
        idx = len(self.ops)
        self.ops.append(dict(eng=eng, fn=fn, deps=self._deps(idx, r, w), dma=None, sig=False))

    def dma(self, out, in_, r=(), w=(), grp="d", q="sp"):
        idx = len(self.ops)
        self.ops.append(dict(eng=q, fn=(lambda e: e.dma_start(out=out, in_=in_)),
                             deps=self._deps(idx, r, w), dma=grp, sig=True))

    def barrier(self):
        last = {}
        for i, o in enumerate(self.ops):
            if o["fn"] is None:
                continue
            key = ("dma", o["dma"]) if o["dma"] is not None else ("eng", o["eng"])
            last[key] = i
        alld = set(last.values())
        for e in ["pe", "dve", "act", "pool", "sp"]:
            self.ops.append(dict(eng=e, fn=None, deps=set(alld), dma=None, sig=False))
        self.lastw = {}
        self.readers = {}

    def bank(self):
        b = self.bank_i
        self.bank_i = (b + 1) % 8
        return b

    def emit(self, stack):
        nc = self.nc
        ops = self.ops
        for o in ops:
            for p in o["deps"]:
                po = ops[p]
                if po["dma"] is not None or po["eng"] != o["eng"] or o["eng"] != "pe":
                    po["sig"] = True
        cnt = {}
        sems = {}
        for o in ops:
            if not o["sig"] or o["fn"] is None:
                continue
            key = ("dma", o["dma"]) if o["dma"] is not None else ("eng", o["eng"])
            if key not in sems:
                sems[key] = stack.enter_context(nc.semaphore("s_%s_%s" % key))
                cnt[key] = 0
            cnt[key] += 16 if o["dma"] is not None else 1
            o["sem"] = sems[key]
            o["val"] = cnt[key]
            o["skey"] = key
        block = stack.enter_context(nc.Block())

        def run(engname, e):
            waited = {}
            for o in ops:
                if o["eng"] != engname:
                    continue
                need = {}
                for p in o["deps"]:
                    po = ops[p]
                    if po["dma"] is None and po["eng"] == engname and engname == "pe":
                        continue
                    k = po["skey"]
                    if po["val"] > need.get(k, 0):
                        need[k] = po["val"]
                for k, v in need.items():
                    if v > waited.get(k, 0):
                        e.wait_ge(sems[k], v)
                        waited[k] = v
                if o["fn"] is None:
                    continue
                inst = o["fn"](e)
                if o["sig"]:
                    inst.then_inc(o["sem"], 16 if o["dma"] is not None else 1)

        @block.tensor
        def _(e):
            run("pe", e)

        @block.vector
        def _(e):
            run("dve", e)

        @block.scalar
        def _(e):
            run("act", e)

        @block.gpsimd
        def _(e):
            run("pool", e)

        @block.sync
        def _(e):
            run("sp", e)


class Arena:
    def __init__(self, t, nwords):
        self.t = t
        self.n = nwords
        self.off = 0
        self.marks = []

    def alloc(self, shape, dtype):
        free = int(np.prod(shape[1:]))
        words = free if dtype == F32 else (free + 1) // 2
        assert self.off + words <= self.n, ("arena overflow", self.off, words, self.n)
        v = self.t[:, self.off:self.off + words]
        self.off += words
        if dtype != F32:
            v = v.bitcast(dtype)
            if free % 2:
                v = v[:, 0:free]
        if len(shape) == 3:
            v = v.rearrange("p (a b) -> p a b", a=shape[1])
        elif len(shape) == 4:
            v = v.rearrange("p (a b c) -> p a b c", a=shape[1], b=shape[2])
        return v

    def mark(self):
        self.marks.append(self.off)

    def release(self):
        self.off = self.marks.pop()


def build(dbg=None):
    nc = bass.Bass("TRN2", target_bir_lowering=False)
    import contextlib
    stack = contextlib.ExitStack()

    def din(name, shape, dt=F32):
        if dbg == "A" and name not in ("xs", "ident", "gmixT"):
            return None
        if dbg in ("nsa", "rwkv") and name in ("w_out_rwkv", "w_out_nsa", "w_o", "mlp_w_up", "mlp_w_down", "gmlp_row", "gfin_row"):
            return None
        return nc.dram_tensor(name, list(shape), dt, kind="ExternalInput").ap()

    xs = din("xs", [L, D])
    w_in = din("w_in", [D, IN_COLS])
    w_out_rwkv = din("w_out_rwkv", [1024, D])
    w_out_nsa = din("w_out_nsa", [1024, D])
    w_o = din("w_o", [D, D])
    w_up = din("mlp_w_up", [D, DFF])
    w_down = din("mlp_w_down", [DFF, D])
    ident_d = din("ident", [128, 128])
    gmix_d = din("gmixT", [128, KC])
    gmlp_d = din("gmlp_row", [128, D])
    gfin_d = din("gfin_row", [128, D])
    c_d0 = din("c_d0", [128, 128]); c_dfull = din("c_dfull", [128, 128]); c_d4 = din("c_d4", [128, 128])
    c_dc = din("c_dc", [128, 8, 128]); c_bcol = din("c_bcol", [128, 256]); c_emat = din("c_emat", [128, 16, 128], BF16)
    c_skeep = din("c_skeep", [128, 8, 32]); c_sbias = din("c_sbias", [128, 8, 32]); c_vtok = din("c_vtok", [128, 16])
    c_ccst = din("c_ccst", [128, 33])
    c_w1k = din("cmp_w1_k", [2048, 256]); c_w2k = din("cmp_w2_k", [256, 64]); c_pek = din("c_pekT", [128, 32])
    c_w1v = din("cmp_w1_v", [2048, 256]); c_w2v = din("cmp_w2_v", [256, 64]); c_pev = din("c_pevT", [128, 32])
    r_m1 = din("r_m1", [128, 256]); r_m2 = din("r_m2", [128, 256]); r_m3 = din("r_m3", [128, 128])
    r_identb = din("r_identb", [128, 128], BF16); r_bones = din("r_bones", [128, 128]); r_resetm = din("r_resetm", [128, 512])
    r_ptab = din("r_ptab", [128, 8, 8]); r_lmu = din("r_lmu", [128, 4])
    r_wup = din("rwkv_w_up", [96, 1024]); r_aup = din("rwkv_a_up", [96, 1024]); r_gup = din("rwkv_g_up", [256, 1024])
    r_lnw = din("r_lnw", [8, 128, 128]); r_lnb = din("r_lnb", [8, 128, 128])
    yfake_d = din("yfake", [128, 2, 8, NO], BF16) if dbg == "tail" else None
    out_d = nc.dram_tensor("out", [NO, D], F32, kind="ExternalOutput").ap() if dbg not in ("A", "nsa", "rwkv") else None
    dbg_ya = nc.dram_tensor("dbg_ya", [128, 8, NO], BF16, kind="ExternalOutput").ap() if dbg == "rwkv" else None
    dbg_yb = nc.dram_tensor("dbg_yb", [128, 8, NO], BF16, kind="ExternalOutput").ap() if dbg == "nsa" else None

    NW = 48896
    arena_t = stack.enter_context(nc.sbuf_tensor("arena", [128, NW], F32))
    psum = stack.enter_context(nc.psum_tensor("ps", [128, 8, 512], F32))
    A = Arena(arena_t, NW)
    P = Prog(nc)

    ident = A.alloc([128, 128], F32)
    gmixT = A.alloc([128, KC], F32)
    P.dma(ident, ident_d, w=["ident"], grp="c0")
    P.dma(gmixT, gmix_d, w=["gmixT"], grp="c1")

    NST = 2
    wst = [A.alloc([128, KC, 128], F32) for _ in range(NST)]
    wbf = [A.alloc([128, KC, 128], BF16) for _ in range(NST)]
    wctr = [0]

    def wload(wap, r0, c0, ncols=128, nk=KC, cast_eng=None, segs=None):
        i = wctr[0]
        wctr[0] += 1
        s = i % NST
        if segs is None:
            segs = [(c0, ncols)]
        o0 = 0
        for (cc, nn) in segs:
            src = wap[r0:r0 + nk * 128, cc:cc + nn].rearrange("(k p) c -> p k c", p=128)
            P.dma(wst[s][:, 0:nk, o0:o0 + nn], src, w=[("wst", s)], grp="w%d" % s)
            o0 += nn
        ncols = o0
        eng = cast_eng or ("pool" if i % 2 == 0 else "act")
        o, i_ = wbf[s][:, 0:nk, 0:ncols], wst[s][:, 0:nk, 0:ncols]
        if eng == "act":
            P.op("act", lambda e, o=o, i_=i_: e.activation(out=o, in_=i_, func=AF.Copy), r=[("wst", s)], w=[("wbf", s)])
        else:
            P.op(eng, lambda e, o=o, i_=i_: e.tensor_copy(out=o, in_=i_), r=[("wst", s)], w=[("wbf", s)])
        return wbf[s], ("wbf", s)

    mix_raw = A.alloc([128, 8192], F32)
    mixT = mix_raw.bitcast(BF16).rearrange("p (a b) -> p a b", a=KC)
    xn_off = A.off
    xnT = A.alloc([128, KC, L], BF16)
    phase_off = A.off
    ysc = nc.dram_tensor("ysc", [128, 2, 8, NO], BF16, kind="Internal").ap()

    A.mark()
    xt = [A.alloc([128, D], F32) for _ in range(2)]
    junk = A.alloc([128, D], F32)
    ssq = A.alloc([128, 16], F32)
    rstd = A.alloc([128, 16], F32)
    for tt in range(16):
        s = tt % 2
        P.dma(xt[s], xs[tt * 128:(tt + 1) * 128, :], w=[("xt", s)], grp="x%d" % s)
        P.op("act", lambda e, s=s, tt=tt: e.activation(out=junk, in_=xt[s], func=AF.Square), r=[("xt", s)], w=["junk"])
        P.op("dve", lambda e, tt=tt: e.tensor_reduce(out=ssq[:, tt:tt + 1], in_=junk, axis=AX.X, op=ALU.add), r=["junk"], w=[("ssq", tt)])
        P.op("dve", lambda e, tt=tt: e.tensor_scalar(out=rstd[:, tt:tt + 1], in0=ssq[:, tt:tt + 1], scalar1=1.0 / D, scalar2=EPS,
                                                     op0=ALU.mult, op1=ALU.add), r=[("ssq", tt)], w=[("rstd", tt)])
        P.op("act", lambda e, tt=tt: e.activation(out=rstd[:, tt:tt + 1], in_=rstd[:, tt:tt + 1], func=AF.Sqrt), r=[("rstd", tt)], w=[("rstd", tt)])
        P.op("dve", lambda e, tt=tt: e.reciprocal(out=rstd[:, tt:tt + 1], in_=rstd[:, tt:tt + 1]), r=[("rstd", tt)], w=[("rstd", tt)])
        P.op("pool", lambda e, s=s, tt=tt: e.tensor_scalar(out=xt[s], in0=xt[s], scalar1=rstd[:, tt:tt + 1], scalar2=None,
                                                           op0=ALU.mult), r=[("xt", s), ("rstd", tt)], w=[("xt", s)])
        for kg in range(4):
            b = P.bank()
            for j in range(4):
                kc = kg * 4 + j
                P.op("pe", lambda e, s=s, b=b, j=j, kc=kc: e.transpose(out=psum[:, b, j * 128:(j + 1) * 128],
                                                                     in_=xt[s][:, kc * 128:(kc + 1) * 128], identity=ident),
                     r=[("xt", s), "ident"], w=[("ps", b)])
            P.op("dve", lambda e, b=b, kg=kg, tt=tt: e.tensor_tensor(
                out=xnT[:, kg * 4:(kg + 1) * 4, tt * 128:(tt + 1) * 128],
                in0=psum[:, b, :].rearrange("p (a c) -> p a c", a=4),
                in1=gmixT[:, kg * 4:(kg + 1) * 4].unsqueeze(2).broadcast_to([128, 4, 128]), op=ALU.mult),
                r=[("ps", b), "gmixT"], w=[("xnT", tt)])
    if dbg:
        dbg_xn = nc.dram_tensor("dbg_xn", [128, KC, L], BF16, kind="ExternalOutput").ap()
        for kc in range(KC):
            P.dma(dbg_xn[:, kc, :], xnT[:, kc, :], r=[("xnT", tt) for tt in range(16)], w=[("dbg_xn", kc)], grp="g0")
    if dbg == "A":
        P.op("sp", None, r=[("dbg_xn", kc) for kc in range(KC)])
        P.emit(stack)
        stack.close()
        return nc
    P.barrier()
    A.release()

    def fm_linear(wap, r0, c0, ncols, rhs_fn, rkeys, nk, toks, evac, segs=None):
        wt, wkey = wload(wap, r0, c0, ncols, nk, segs=segs)
        for (t0, n) in toks:
            b = P.bank()
            for k in range(nk):
                P.op("pe", lambda e, b=b, k=k, t0=t0, n=n: e.matmul(psum[0:ncols, b, 0:n], lhsT=wt[:, k, 0:ncols], rhs=rhs_fn(k, t0, n),
                                                                   start=(k == 0), stop=(k == nk - 1)),
                     r=[wkey] + list(rkeys), w=[("ps", b)])
            evac(b, t0, n)

    def tm_linear(wap, r0, c0, lhs_fn, lkeys, nk, ntt, evac, ncols=128):
        wt, wkey = wload(wap, r0, c0, ncols, nk)
        for tg in range(0, ntt, 4):
            b = P.bank()
            for j in range(4):
                tt = tg + j
                for k in range(nk):
                    P.op("pe", lambda e, b=b, k=k, tt=tt, j=j: e.matmul(psum[:, b, j * 128:j * 128 + ncols], lhsT=lhs_fn(k, tt), rhs=wt[:, k, 0:ncols],
                                                                       start=(k == 0), stop=(k == nk - 1)),
                         r=[wkey] + list(lkeys), w=[("ps", b)])
            evac(b, tg, 4)

    Q0, KCR, VCR, KS, VS, KW, VW, NGC = 3520, 4544, 4800, 5056, 5312, 5568, 5824, 6080
    SLOPES = [2.0 ** (-8.0 * (i + 1) / 16.0) for i in range(16)]
    BIG = 30000.0
    TOK4 = [(0, 512), (512, 512), (1024, 512), (1536, 512)]
    TOK2 = [(0, 512), (512, 512)]
    xn_all = lambda k, t0, n: xnT[:, k, t0:t0 + n]
    xn_own = lambda k, t0, n: xnT[:, k, NO + t0:NO + t0 + n]

    def nsa_phase():
        A.off = phase_off
        qT = mix_raw[:, 0:4096].bitcast(BF16).rearrange("p (a b) -> p a b", a=8)
        ksT = mix_raw[:, 4096:8192].bitcast(BF16).rearrange("p (a b) -> p a b", a=4)
        kwT = A.alloc([128, 4, L], BF16)
        vsx = A.alloc([128, 16, 4, 65], BF16)
        vwx = A.alloc([128, 16, 4, 65], BF16)
        d0 = A.alloc([128, 128], F32)
        dfull = A.alloc([128, 128], F32)
        d4 = A.alloc([128, 128], F32)
        dc = A.alloc([128, 8, 128], F32)
        bcol = A.alloc([128, 256], F32)
        emat = A.alloc([128, 16, 128], BF16)
        skeep = A.alloc([128, 8, 32], F32)
        sbias = A.alloc([128, 8, 32], F32)
        vtok = A.alloc([128, 16], F32)
        ccst = A.alloc([128, 33], F32)
        sg = A.alloc([128, 8, 48], F32)
        kcT = A.alloc([128, 4, 128], BF16)
        vcx = A.alloc([128, 4, 98], BF16)
        for (dst, src, nm) in [(d0, c_d0, "d0"), (dfull, c_dfull, "dfull"), (d4, c_d4, "d4"), (dc, c_dc, "dc"), (bcol, c_bcol, "bcol"),
                               (emat, c_emat, "emat"), (skeep, c_skeep, "skeep"), (sbias, c_sbias, "sbias"), (vtok, c_vtok, "vtok"),
                               (ccst, c_ccst, "ccst")]:
            P.dma(dst, src, w=[nm], grp="n_" + nm)

        for hp in range(8):
            def ev_q(b, t0, n, hp=hp):
                P.op("act", lambda e: e.activation(out=qT[:, hp, t0:t0 + n], in_=psum[:, b, 0:n], func=AF.Copy, scale=0.125),
                     r=[("ps", b)], w=["qT"])
            fm_linear(w_in, 0, Q0 + hp * 128, 128, xn_own, ["xnT"], KC, TOK2, ev_q)
        for (c0, dstT, nm) in [(KS, ksT, "ksT"), (KW, kwT, "kwT")]:
            for g in range(4):
                def ev_k(b, t0, n, g=g, dstT=dstT, nm=nm):
                    P.op("act", lambda e: e.activation(out=dstT[:, g, t0:t0 + n], in_=psum[:, b, 0:n], func=AF.Copy),
                         r=[("ps", b)], w=[nm])
                fm_linear(w_in, 0, 0, 128, xn_all, ["xnT"], KC, TOK4, ev_k, segs=[(c0 + g * 64, 64), (c0 + g * 64, 64)])
        for (c0, dstx, nm) in [(VS, vsx, "vsx"), (VW, vwx, "vwx")]:
            for half in range(2):
                def ev_v(b, tg, nt, half=half, dstx=dstx, nm=nm):
                    P.op("act", lambda e: e.activation(out=dstx[:, tg:tg + 4, 2 * half:2 * half + 2, 0:64],
                                                       in_=psum[:, b, :].rearrange("p (a g c) -> p a g c", a=4, g=2), func=AF.Copy),
                         r=[("ps", b)], w=[nm])
                tm_linear(w_in, 0, c0 + half * 128, lambda k, tt: xnT[:, k, tt * 128:(tt + 1) * 128], ["xnT"], KC, 16, ev_v)
            for g in range(4):
                P.op("dve", lambda e, g=g, dstx=dstx: e.tensor_copy(out=dstx[:, :, g, 64], in_=vtok), r=["vtok"], w=[nm])

        def ev_g(b, tg, nt):
            P.op("act", lambda e: e.activation(out=sg[:, tg:tg + 4, :], in_=psum[:, b, :].rearrange("p (a c) -> p a c", a=4)[:, :, 0:48],
                                               func=AF.Sigmoid), r=[("ps", b)], w=["sg"])
        tm_linear(w_in, 0, NGC, lambda k, tt: xnT[:, k, NO + tt * 128:NO + (tt + 1) * 128], ["xnT"], KC, 8, ev_g, ncols=48)

        A.mark()
        rawT = A.alloc([128, 2, L], BF16)
        peT32 = A.alloc([128, 32], F32)
        peT = A.alloc([128, 32], BF16)
        bh = A.alloc([128, 2], F32)
        hacc = A.alloc([128, 4, 2, 128], F32)
        gx = A.alloc([128, 128], F32)
        gx2 = A.alloc([128, 128], F32)
        gT = A.alloc([128, 4, 2, 128], BF16)
        w2s = A.alloc([128, 2, 64], F32)
        w2b = A.alloc([128, 2, 128], BF16)
        for which, (craw, w1_d, w2_d, pe_d) in enumerate([(KCR, c_w1k, c_w2k, c_pek), (VCR, c_w1v, c_w2v, c_pev)]):
            for tl in range(2):
                def ev_r(b, t0, n, tl=tl):
                    P.op("act", lambda e: e.activation(out=rawT[:, tl, t0:t0 + n], in_=psum[:, b, 0:n], func=AF.Copy),
                         r=[("ps", b)], w=["rawT"])
                fm_linear(w_in, 0, craw + tl * 128, 128, xn_all, ["xnT"], KC, TOK4, ev_r)
            P.dma(peT32, pe_d, w=["peT32"], grp="n_pe")
            P.op("dve", lambda e: e.tensor_copy(out=peT, in_=peT32), r=["peT32"], w=["peT"])
            P.dma(w2s, w2_d.rearrange("(k p) c -> p k c", p=128), w=["w2s"], grp="n_w2")
            P.op("dve", lambda e: e.tensor_copy(out=w2b[:, :, 0:64], in_=w2s), r=["w2s"], w=["w2b"])
            P.op("dve", lambda e: e.tensor_copy(out=w2b[:, :, 64:128], in_=w2s), r=["w2s"], w=["w2b"])
            for lg in range(4):
                i = wctr[0]
                wctr[0] += 1
                s_ = i % NST
                wv32 = wst[s_].rearrange("p a b -> p (a b)").rearrange("p (l c) -> p l c", l=8)
                wvb = wbf[s_].rearrange("p a b -> p (a b)").rearrange("p (l c) -> p l c", l=8)
                src = w1_d[lg * 512:(lg + 1) * 512, :].rearrange("(l d) c -> d l c", d=64)
                P.dma(wv32[0:64], src, w=[("wst", s_)], grp="w%d" % s_)
                P.dma(wv32[64:128], src, w=[("wst", s_)], grp="w%d" % s_)
                P.op("pool", lambda e, wvb=wvb, wv32=wv32: e.tensor_copy(out=wvb, in_=wv32), r=[("wst", s_)], w=[("wbf", s_)])
                for g in range(4):
                    p0 = (g % 2) * 64
                    b = P.bank()
                    for hc in range(2):
                        for l8 in range(8):
                            l = lg * 8 + l8
                            P.op("pe", lambda e, b=b, g=g, hc=hc, l8=l8, l=l, p0=p0, wvb=wvb: e.matmul(
                                psum[:, b, hc * 128:hc * 128 + 127], lhsT=wvb[p0:p0 + 64, l8, hc * 128:(hc + 1) * 128],
                                rhs=rawT[p0:p0 + 64, g // 2, l:l + 16 * 126 + 1:16], start=(l8 == 0), stop=(l8 == 7)),
                                r=[("wbf", s_), "rawT"], w=[("ps", b)])
                    hv = hacc[:, g, :, :]
                    pv_ = psum[:, b, 0:256].rearrange("p (a c) -> p a c", a=2)
                    if lg == 0:
                        P.op("act", lambda e, hv=hv, pv_=pv_: e.activation(out=hv[:, :, 0:127], in_=pv_[:, :, 0:127], func=AF.Copy), r=[("ps", b)], w=[("hacc", g)])
                    else:
                        P.op("dve", lambda e, hv=hv, pv_=pv_: e.tensor_tensor(out=hv[:, :, 0:127], in0=pv_[:, :, 0:127], in1=hv[:, :, 0:127], op=ALU.add),
                             r=[("ps", b), ("hacc", g)], w=[("hacc", g)])
                b = P.bank()
                for hc in range(2):
                    for l8 in range(8):
                        l = lg * 8 + l8
                        P.op("pe", lambda e, b=b, hc=hc, l8=l8, l=l, wvb=wvb: e.matmul(
                            psum[:, b, hc * 2:hc * 2 + 1], lhsT=wvb[0:64, l8, hc * 128:(hc + 1) * 128],
                            rhs=peT[0:64, l:l + 1], start=(l8 == 0), stop=(l8 == 7)),
                            r=[("wbf", s_), "peT"], w=[("ps", b)])
                pb_ = psum[:, b, 0:4].rearrange("p (a c) -> p a c", a=2)[:, :, 0]
                if lg == 0:
                    P.op("dve", lambda e, pb_=pb_: e.tensor_copy(out=bh, in_=pb_), r=[("ps", b)], w=["bh"])
                else:
                    P.op("dve", lambda e, pb_=pb_: e.tensor_tensor(out=bh, in0=pb_, in1=bh, op=ALU.add), r=[("ps", b), "bh"], w=["bh"])
            for g in range(4):
                for hc in range(2):
                    src = hacc[:, g, hc, 0:127]
                    P.op("act", lambda e, src=src, hc=hc: e.activation(out=gx[:, 0:127], in_=src, func=AF.Identity, bias=bh[:, hc:hc + 1]),
                         r=[("hacc", g), "bh"], w=["gx"])
                    P.op("dve", lambda e: e.tensor_tensor(out=gx2[:, 0:127], in0=gx[:, 0:127], in1=gx[:, 0:127], op=ALU.mult), r=["gx"], w=["gx2"])
                    P.op("dve", lambda e: e.tensor_scalar(out=gx2[:, 0:127], in0=gx2[:, 0:127], scalar1=0.044715, scalar2=1.0, op0=ALU.mult, op1=ALU.add),
                         r=["gx2"], w=["gx2"])
                    P.op("dve", lambda e: e.tensor_tensor(out=gx2[:, 0:127], in0=gx2[:, 0:127], in1=gx[:, 0:127], op=ALU.mult), r=["gx", "gx2"], w=["gx2"])
                    P.op("act", lambda e: e.activation(out=gx2[:, 0:127], in_=gx2[:, 0:127], func=AF.Tanh, scale=0.7978845608028654), r=["gx2"], w=["gx2"])
                    P.op("dve", lambda e: e.tensor_scalar(out=gx2[:, 0:127], in0=gx2[:, 0:127], scalar1=0.5, scalar2=0.5, op0=ALU.mult, op1=ALU.add),
                         r=["gx2"], w=["gx2"])
                    P.op("dve", lambda e, g=g, hc=hc: e.tensor_tensor(out=gT[:, g, hc, 0:127], in0=gx2[:, 0:127], in1=gx[:, 0:127], op=ALU.mult),
                         r=["gx", "gx2"], w=["gT"])
            for g in range(4):
                b = 7
                if which == 0:
                    for hc in range(2):
                        P.op("pe", lambda e, b=b, g=g, hc=hc: e.matmul(psum[:, b, 0:127], lhsT=w2b[:, hc, :], rhs=gT[:, g, hc, 0:127],
                                                                      start=(hc == 0), stop=(hc == 1)), r=["w2b", "gT"], w=[("ps", b)])
                    P.op("act", lambda e, b=b, g=g: e.activation(out=kcT[:, g, 0:127], in_=psum[:, b, 0:127], func=AF.Copy), r=[("ps", b)], w=["kcT"])
                else:
                    for hc in range(2):
                        P.op("pe", lambda e, b=b, g=g, hc=hc: e.matmul(psum[0:127, b, 0:64], lhsT=gT[:, g, hc, 0:127], rhs=w2b[:, hc, 0:64],
                                                                      start=(hc == 0), stop=(hc == 1)), r=["w2b", "gT"], w=[("ps", b)])
                    P.op("dve", lambda e, b=b, g=g: e.tensor_scalar(out=vcx[0:127, g, 0:64], in0=psum[0:127, b, 0:64], scalar1=ccst[0:127, 0:1], scalar2=None,
                                                                   op0=ALU.mult), r=[("ps", b), "ccst"], w=["vcx"])
                    P.op("dve", lambda e, g=g: e.tensor_copy(out=vcx[0:127, g, 64:97], in_=ccst[0:127, :]), r=["ccst"], w=["vcx"])
        P.barrier()
        A.release()

        ynsa = A.alloc([128, 1024], F32)
        ybt = A.alloc([128, 8, 128], BF16)
        impg = A.alloc([128, 4, 32], F32)
        sc = A.alloc([128, 32], F32)
        cmpt = A.alloc([128, 32, 32], F32)
        rk = A.alloc([128, 32], F32)
        t1 = A.alloc([128, 32], F32)
        selb = A.alloc([128, 4, 32], F32)
        selbT = A.alloc([128, 4, 128], BF16)
        sps = [A.alloc([128, 128], F32) for _ in range(3)]
        pts = [A.alloc([128, 128], BF16) for _ in range(3)]
        rv = A.alloc([128, 8], F32)
        sub = [0]
        acc_i = [0]

        def s_region():
            i = sub[0]
            sub[0] = (i + 1) % 4
            return i, 0

        def a_region():
            i = acc_i[0]
            acc_i[0] = (i + 1) % 3
            return 4 + i, 0
        vis = [0]

        def visit(h, qt, klhsT, nk_rows, extra_mm, dtile, bias_ap, vrhs, acc, first, last, kdeps):
            g, p0 = h // 4, (h % 2) * 64
            sb, so = s_region()
            i3 = vis[0] % 3
            vis[0] += 1
            spt, ptt = sps[i3], pts[i3]
            skey = ("ps", sb)
            P.op("pe", lambda e: e.matmul(psum[0:nk_rows, sb, so:so + 128], lhsT=klhsT, rhs=qT[p0:p0 + 64, h // 2, qt * 128:(qt + 1) * 128],
                                          start=True, stop=(extra_mm is None)), r=["qT"] + kdeps, w=[skey])
            if extra_mm is not None:
                el, er = extra_mm
                P.op("pe", lambda e: e.matmul(psum[0:nk_rows, sb, so:so + 128], lhsT=el, rhs=er, start=False, stop=True),
                     r=["emat", ("selbT", g)], w=[skey])
            P.op("dve", lambda e: e.scalar_tensor_tensor(out=spt[0:nk_rows, :], in0=dtile, scalar=-SLOPES[h], in1=psum[0:nk_rows, sb, so:so + 128],
                                                         op0=ALU.mult, op1=ALU.add), r=[skey, "dtiles"], w=[("sp", i3)])
            if bias_ap is None:
                P.op("act", lambda e: e.activation(out=ptt[0:nk_rows, :], in_=spt[0:nk_rows, :], func=AF.Exp), r=[("sp", i3)], w=[("pt", i3)])
            else:
                P.op("act", lambda e: e.activation(out=ptt[0:nk_rows, :], in_=spt[0:nk_rows, :], func=AF.Exp, bias=bias_ap),
                     r=[("sp", i3), "bcol"], w=[("pt", i3)])
            ab, ao, ncol = acc
            P.op("pe", lambda e: e.matmul(psum[:, ab, ao:ao + ncol], lhsT=ptt[0:nk_rows, :], rhs=vrhs, start=first, stop=last),
                 r=[("pt", i3)] + kdeps, w=[("ps", ab)])

        def finish(h, qt, acc, br, first_branch):
            ab, ao, ncol = acc
            akey = ("ps", ab)
            c = vis[0] % 8
            P.op("dve", lambda e: e.tensor_scalar(out=rv[:, c:c + 1], in0=psum[:, ab, ao + 64:ao + 65], scalar1=1e-30, scalar2=None, op0=ALU.max),
                 r=[akey], w=["rv"])
            P.op("dve", lambda e: e.reciprocal(out=rv[:, c:c + 1], in_=rv[:, c:c + 1]), r=["rv"], w=["rv"])
            if br == 0:
                g = h // 4
                if h % 4 == 0:
                    P.op("dve", lambda e: e.tensor_scalar(out=impg[:, g, :], in0=psum[:, ab, ao + 65:ao + 97], scalar1=rv[:, c:c + 1], scalar2=None,
                                                          op0=ALU.mult), r=[akey, "rv"], w=[("impg", g)])
                else:
                    P.op("dve", lambda e: e.scalar_tensor_tensor(out=impg[:, g, :], in0=psum[:, ab, ao + 65:ao + 97], scalar=rv[:, c:c + 1],
                                                                 in1=impg[:, g, :], op0=ALU.mult, op1=ALU.add),
                         r=[akey, "rv", ("impg", g)], w=[("impg", g)])
            P.op("dve", lambda e: e.tensor_tensor(out=rv[:, c:c + 1], in0=rv[:, c:c + 1], in1=sg[:, qt, br * 16 + h:br * 16 + h + 1], op=ALU.mult),
                 r=["rv", "sg"], w=["rv"])
            yh = ynsa[:, h * 64:(h + 1) * 64]
            if first_branch:
                P.op("dve", lambda e: e.tensor_scalar(out=yh, in0=psum[:, ab, ao:ao + 64], scalar1=rv[:, c:c + 1], scalar2=None, op0=ALU.mult),
                     r=[akey, "rv"], w=[("ynsa", h)])
            else:
                P.op("dve", lambda e: e.scalar_tensor_tensor(out=yh, in0=psum[:, ab, ao:ao + 64], scalar=rv[:, c:c + 1], in1=yh,
                                                             op0=ALU.mult, op1=ALU.add), r=[akey, "rv", ("ynsa", h)], w=[("ynsa", h)])

        for qt in range(8):
            qb = 8 + qt
            for h in range(16):
                g, p0 = h // 4, (h % 2) * 64
                ab, ao = a_region()
                acc = (ab, ao, 97)
                visit(h, qt, kcT[p0:p0 + 64, g, 0:127], 127, None, dc[0:127, qt, :], None, vcx[0:127, g, 0:97], acc, True, True, ["kcT", "vcx"])
                finish(h, qt, acc, 0, True)
            for g in range(4):
                P.op("dve", lambda e, g=g, qt=qt: e.tensor_tensor(out=sc, in0=impg[:, g, :], in1=skeep[:, qt, :], op=ALU.mult), r=[("impg", g), "skeep"], w=["sc"])
                P.op("dve", lambda e, qt=qt: e.tensor_tensor(out=sc, in0=sc, in1=sbias[:, qt, :], op=ALU.add), r=["sc", "sbias"], w=["sc"])
                P.op("dve", lambda e: e.tensor_tensor(out=cmpt, in0=sc.unsqueeze(1).broadcast_to([128, 32, 32]),
                                                      in1=sc.unsqueeze(2).broadcast_to([128, 32, 32]), op=ALU.is_gt), r=["sc"], w=["cmpt"])
                P.op("dve", lambda e: e.tensor_reduce(out=rk, in_=cmpt, axis=AX.X, op=ALU.add), r=["cmpt"], w=["rk"])
                P.op("dve", lambda e: e.tensor_scalar(out=rk, in0=rk, scalar1=15.5, scalar2=None, op0=ALU.is_lt), r=["rk"], w=["rk"])
                P.op("dve", lambda e: e.tensor_scalar(out=t1, in0=sc, scalar1=-1e29, scalar2=None, op0=ALU.is_gt), r=["sc"], w=["t1"])
                P.op("dve", lambda e: e.tensor_tensor(out=rk, in0=rk, in1=t1, op=ALU.mult), r=["rk", "t1"], w=["rk"])
                P.op("dve", lambda e, g=g: e.tensor_scalar(out=selb[:, g, :], in0=rk, scalar1=-1.0, scalar2=BIG, op0=ALU.add, op1=ALU.mult),
                     r=["rk"], w=[("selb", g)])
                sb, so = s_region()
                P.op("pe", lambda e, g=g, sb=sb, so=so: e.transpose(out=psum[0:32, sb, so:so + 128], in_=selb[:, g, :], identity=ident),
                     r=[("selb", g), "ident"], w=[("ps", sb)])
                P.op("act", lambda e, g=g, sb=sb, so=so: e.activation(out=selbT[0:32, g, :], in_=psum[0:32, sb, so:so + 128], func=AF.Copy),
                     r=[("ps", sb)], w=[("selbT", g)])
            for h in range(16):
                g, p0 = h // 4, (h % 2) * 64
                ab, ao = a_region()
                acc = (ab, ao, 65)
                for kb in range(qb + 1):
                    rel = qb - kb
                    dt_ = d0 if rel == 0 else dfull
                    bias_ap = None if rel == 0 else bcol[:, h * 16 + rel:h * 16 + rel + 1]
                    visit(h, qt, ksT[p0:p0 + 64, g, kb * 128:(kb + 1) * 128], 128, (emat[0:32, kb, :], selbT[0:32, g, :]), dt_, bias_ap,
                          vsx[:, kb, g, :], acc, kb == 0, kb == qb, ["ksT", "vsx"])
                finish(h, qt, acc, 1, False)
            for h in range(16):
                g, p0 = h // 4, (h % 2) * 64
                ab, ao = a_region()
                acc = (ab, ao, 65)
                for kb in range(qb - 4, qb + 1):
                    rel = qb - kb
                    dt_ = d0 if rel == 0 else (d4 if rel == 4 else dfull)
                    bias_ap = None if rel == 0 else bcol[:, h * 16 + rel:h * 16 + rel + 1]
                    visit(h, qt, kwT[p0:p0 + 64, g, kb * 128:(kb + 1) * 128], 128, None, dt_, bias_ap,
                          vwx[:, kb, g, :], acc, kb == qb - 4, kb == qb, ["kwT", "vwx"])
                finish(h, qt, acc, 2, False)
            for kg in range(2):
                b = 7
                for j in range(4):
                    kc = kg * 4 + j
                    P.op("pe", lambda e, b=b, j=j, kc=kc: e.transpose(out=psum[:, b, j * 128:(j + 1) * 128], in_=ynsa[:, kc * 128:(kc + 1) * 128], identity=ident),
                         r=[("ynsa", 2 * kc), ("ynsa", 2 * kc + 1), "ident"], w=[("ps", b)])
                P.op("act", lambda e, b=b, kg=kg: e.activation(out=ybt[:, kg * 4:(kg + 1) * 4, :], in_=psum[:, b, :].rearrange("p (a c) -> p a c", a=4), func=AF.Copy),
                     r=[("ps", b)], w=["ybt"])
            P.dma(ysc[:, 1, :, qt * 128:(qt + 1) * 128], ybt, r=["ybt"], w=["ysc"], grp="ysc")
            if dbg == "nsa":
                P.dma(dbg_yb[:, :, qt * 128:(qt + 1) * 128], ybt, r=["ybt"], w=["dbg_yb"], grp="dbgyb")
        P.barrier()

    GN_EPS = 64e-5

    def rwkv_phase():
        A.off = phase_off
        MA = Arena(mix_raw, 8192)
        twT = MA.alloc([128, L], BF16)
        alT = MA.alloc([128, L], BF16)
        sgT = MA.alloc([128, 2, NO], BF16)
        M1 = MA.alloc([128, 256], F32)
        M2 = MA.alloc([128, 256], F32)
        M3 = MA.alloc([128, 128], F32)
        identb = MA.alloc([128, 128], BF16)
        bones = MA.alloc([128, 128], F32)
        resetm = MA.alloc([128, 512], F32)
        ptab = MA.alloc([128, 8, 8], F32)
        omt = MA.alloc([128, 8, 8], F32)
        lmu = MA.alloc([128, 4], F32)
        lomm = MA.alloc([128, 4], F32)
        lnw = MA.alloc([128, 128], F32)
        lnb = MA.alloc([128, 128], F32)
        ybuf = MA.alloc([128, NO], BF16)
        wl32 = MA.alloc([128, 4, 128], F32)
        wlb = MA.alloc([128, 4, 128], BF16)
        carry = MA.alloc([128, 8], F32)
        st = MA.alloc([128, 8], F32)
        eC = MA.alloc([128, 8], F32)
        AG = MA.alloc([128, 2, 128], BF16)
        AU = MA.alloc([128, 2, 128], BF16)
        RbarT = MA.alloc([128, 128], BF16)
        Phi = MA.alloc([128, 128], F32)
        DeltaT = MA.alloc([128, 128], F32)
        Sst = MA.alloc([128, 128], F32)
        Sb = MA.alloc([128, 128], BF16)
        ysb = MA.alloc([128, 128], F32)
        ysq = MA.alloc([128, 128], F32)
        yn = MA.alloc([128, 128], F32)
        ytr = MA.alloc([128, 128], F32)
        t64 = MA.alloc([128, 64], F32)
        TT = [MA.alloc([128, 128], BF16) for _ in range(4)]
        rS = A.alloc([128, L], F32)
        kS = A.alloc([128, L], F32)
        vS = A.alloc([128, L], F32)
        raw = A.alloc([128, 514], F32)
        bon = raw[:, 0:512]
        lw = A.alloc([128, 512], F32)
        at = A.alloc([128, 512], F32)
        kk = A.alloc([128, 512], F32)
        k2 = A.alloc([128, 512], F32)
        cl = A.alloc([128, 512], F32)
        e1 = A.alloc([128, 512], F32)
        e2 = A.alloc([128, 512], F32)
        gq = A.alloc([128, 512], F32)
        ARblk = A.alloc([128, 8, 2, 128], BF16)
        Bblk = A.alloc([128, 8, 128], BF16)
        Kblk = A.alloc([128, 8, 128], BF16)
        B2blk = A.alloc([128, 8, 128], BF16)
        K2blk = A.alloc([128, 8, 128], BF16)
        Vblk = A.alloc([128, 8, 128], BF16)
        tm = [A.alloc([128, 4, 128], BF16) for _ in range(4)]
        LM1 = [A.alloc([128, 128], BF16) for _ in range(4)]
        LM2 = [A.alloc([128, 256], BF16) for _ in range(4)]
        W3 = [[A.alloc([128, 3, 128], BF16) for _ in range(2)] for _ in range(4)]

        for (dst, src, nm) in [(M1, r_m1, "M1"), (M2, r_m2, "M2"), (M3, r_m3, "M3"), (identb, r_identb, "identb"), (bones, r_bones, "bones"),
                               (resetm, r_resetm, "resetm"), (ptab, r_ptab, "ptab"), (lmu, r_lmu, "lmu")]:
            P.dma(dst, src, w=[nm], grp="r_" + nm)
        P.op("dve", lambda e: e.tensor_scalar(out=omt, in0=ptab, scalar1=-1.0, scalar2=1.0, op0=ALU.mult, op1=ALU.add), r=["ptab"], w=["omt"])
        P.op("dve", lambda e: e.tensor_scalar(out=lomm, in0=lmu, scalar1=-1.0, scalar2=1.0, op0=ALU.mult, op1=ALU.add), r=["lmu"], w=["lomm"])
        for blk_ in (ARblk, Bblk, Kblk, B2blk, K2blk, Vblk):
            P.op("pool", lambda e, blk_=blk_: e.memset(blk_, 0.0), w=["blk"])

        def shift_evac(b, t0, n, nrow, mu_col, omm_col, ci, dst):
            if t0 == 0:
                P.op("dve", lambda e: e.memset(raw[0:nrow, 0:1], 0.0), w=["raw"])
            else:
                P.op("dve", lambda e: e.tensor_copy(out=raw[0:nrow, 0:1], in_=carry[0:nrow, ci:ci + 1]), r=[("carry", ci)], w=["raw"])
            P.op("act", lambda e: e.activation(out=raw[0:nrow, 1:n + 1], in_=psum[0:nrow, b, 0:n], func=AF.Copy), r=[("ps", b)], w=["raw"])
            P.op("dve", lambda e: e.tensor_scalar(out=dst, in0=raw[0:nrow, 1:n + 1], scalar1=omm_col[0:nrow], scalar2=None, op0=ALU.mult),
                 r=["raw", "omt", "lomm"], w=["shiftdst"])
            P.op("dve", lambda e: e.scalar_tensor_tensor(out=dst, in0=raw[0:nrow, 0:n], scalar=mu_col[0:nrow], in1=dst,
                                                         op0=ALU.mult, op1=ALU.add), r=["raw", "shiftdst", "ptab", "lmu"], w=["shiftdst"])
            P.op("dve", lambda e: e.tensor_copy(out=carry[0:nrow, ci:ci + 1], in_=raw[0:nrow, n:n + 1]), r=["raw"], w=[("carry", ci)])

        for li, (c0, nrow) in enumerate([(3072, 96), (3168, 96), (3264, 128), (3392, 128)]):
            def ev_l(b, t0, n, li=li, nrow=nrow):
                shift_evac(b, t0, n, nrow, lmu[:, li:li + 1], lomm[:, li:li + 1], 3, e1[0:nrow, 0:n])
                if li == 0:
                    P.op("act", lambda e: e.activation(out=twT[0:nrow, t0:t0 + n], in_=e1[0:nrow, 0:n], func=AF.Tanh), r=["shiftdst"], w=["twT", "shiftdst"])
                elif li == 1:
                    P.op("act", lambda e: e.activation(out=alT[0:nrow, t0:t0 + n], in_=e1[0:nrow, 0:n], func=AF.Copy), r=["shiftdst"], w=["alT", "shiftdst"])
                elif t0 >= NO:
                    P.op("act", lambda e: e.activation(out=sgT[:, li - 2, t0 - NO:t0 - NO + n], in_=e1[0:nrow, 0:n], func=AF.Sigmoid),
                         r=["shiftdst"], w=["sgT", "shiftdst"])
            fm_linear(w_in, 0, c0, nrow, xn_all, ["xnT"], KC, TOK4, ev_l)

        def halves(fn):
            fn(0, 0)
            fn(64, 64)

        def do_pair(hp):
            pc = lambda j: ptab[:, hp, j:j + 1]
            oc = lambda j: omt[:, hp, j:j + 1]
            P.dma(wl32[0:96, 0, :], r_wup[:, hp * 128:(hp + 1) * 128], w=["wl32"], grp="r_wl")
            P.dma(wl32[0:96, 1, :], r_aup[:, hp * 128:(hp + 1) * 128], w=["wl32"], grp="r_wl")
            P.dma(wl32[:, 2:4, :], r_gup[:, hp * 128:(hp + 1) * 128].rearrange("(k p) c -> p k c", p=128), w=["wl32"], grp="r_wl")
            P.op("pool", lambda e: e.tensor_copy(out=wlb[0:96, 0:2, :], in_=wl32[0:96, 0:2, :]), r=["wl32"], w=["wlb"])
            P.op("pool", lambda e: e.tensor_copy(out=wlb[:, 2:4, :], in_=wl32[:, 2:4, :]), r=["wl32"], w=["wlb"])
            P.dma(lnw, r_lnw[hp], w=["lnw"], grp="r_lnw")
            P.dma(lnb, r_lnb[hp], w=["lnb"], grp="r_lnb")
            for qi, (c0, dstS) in enumerate([(0, rS), (1024, kS), (2048, vS)]):
                def ev_s(b, t0, n, qi=qi, dstS=dstS):
                    shift_evac(b, t0, n, 128, pc(qi), oc(qi), qi, dstS[:, t0:t0 + n])
                fm_linear(w_in, 0, c0 + hp * 128, 128, xn_all, ["xnT"], KC, TOK4, ev_s)
            P.op("pool", lambda e: e.memset(Sst, 0.0), w=["S"])
            P.op("pool", lambda e: e.memset(Sb, 0.0), w=["Sb"])
            def do_quarter(qtr):
                T0 = qtr * 512
                own = qtr >= 2
                sl = slice(T0, T0 + 512)
                b = P.bank()
                P.op("pe", lambda e, b=b: e.matmul(psum[:, b, :], lhsT=wlb[0:96, 0, :], rhs=twT[0:96, sl], start=True, stop=True), r=["wlb", "twT"], w=[("ps", b)])
                P.op("act", lambda e, b=b: e.activation(out=lw, in_=psum[:, b, :], func=AF.Sigmoid, bias=pc(3)), r=[("ps", b), "ptab"], w=["lw"])
                P.op("dve", lambda e: e.tensor_scalar(out=lw, in0=lw, scalar1=-0.6065306597126334, scalar2=None, op0=ALU.mult), r=["lw"], w=["lw"])
                b = P.bank()
                P.op("pe", lambda e, b=b: e.matmul(psum[:, b, :], lhsT=wlb[0:96, 1, :], rhs=alT[0:96, sl], start=True, stop=True), r=["wlb", "alT"], w=[("ps", b)])
                P.op("act", lambda e, b=b: e.activation(out=at, in_=psum[:, b, :], func=AF.Sigmoid, bias=pc(4)), r=[("ps", b), "ptab"], w=["at"])
                P.op("dve", lambda e: e.tensor_scalar(out=e1, in0=kS[:, sl], scalar1=pc(5), scalar2=None, op0=ALU.mult), r=["shiftdst", "ptab"], w=["e1"])
                P.op("act", lambda e: e.activation(out=e2, in_=e1, func=AF.Square), r=["e1"], w=["e2"])
                b = P.bank()
                P.op("pe", lambda e, b=b: e.matmul(psum[:, b, :], lhsT=bones, rhs=e2, start=True, stop=True), r=["bones", "e2"], w=[("ps", b)])
                P.op("dve", lambda e, b=b: e.tensor_scalar(out=e2, in0=psum[:, b, :], scalar1=1e-24, scalar2=None, op0=ALU.max), r=[("ps", b)], w=["e2"])
                P.op("act", lambda e: e.activation(out=e2, in_=e2, func=AF.Sqrt), r=["e2"], w=["e2"])
                P.op("dve", lambda e: e.reciprocal(out=e2, in_=e2), r=["e2"], w=["e2"])
                P.op("dve", lambda e: e.tensor_tensor(out=kk, in0=e1, in1=e2, op=ALU.mult), r=["e1", "e2"], w=["kk"])
                P.op("dve", lambda e: e.tensor_scalar(out=e1, in0=at, scalar1=pc(6), scalar2=oc(6), op0=ALU.mult, op1=ALU.add), r=["at", "ptab", "omt"], w=["e1"])
                P.op("dve", lambda e: e.tensor_tensor(out=k2, in0=kS[:, sl], in1=e1, op=ALU.mult), r=["e1", "shiftdst"], w=["k2"])
                if own:
                    P.op("dve", lambda e: e.tensor_tensor(out=e1, in0=rS[:, sl], in1=k2, op=ALU.mult), r=["k2", "shiftdst"], w=["e1"])
                    P.op("dve", lambda e: e.tensor_scalar(out=e1, in0=e1, scalar1=pc(7), scalar2=None, op0=ALU.mult), r=["e1", "ptab"], w=["e1"])
                    b = P.bank()
                    P.op("pe", lambda e, b=b: e.matmul(psum[:, b, :], lhsT=bones, rhs=e1, start=True, stop=True), r=["bones", "e1"], w=[("ps", b)])
                    P.op("dve", lambda e, b=b: e.tensor_tensor(out=bon, in0=psum[:, b, :], in1=vS[:, sl], op=ALU.mult), r=[("ps", b), "shiftdst"], w=["raw"])
                    b = P.bank()
                    for kt in range(2):
                        P.op("pe", lambda e, b=b, kt=kt: e.matmul(psum[:, b, :], lhsT=wlb[:, 2 + kt, :], rhs=sgT[:, kt, T0 - NO:T0 - NO + 512],
                                                                 start=(kt == 0), stop=(kt == 1)), r=["wlb", "sgT"], w=[("ps", b)])
                    P.op("act", lambda e, b=b: e.activation(out=gq, in_=psum[:, b, :], func=AF.Copy), r=[("ps", b)], w=["gq"])
                P.op("dve", lambda e: e.tensor_tensor_scan(out=cl, data0=resetm, data1=lw, initial=0.0, op0=ALU.mult, op1=ALU.add), r=["resetm", "lw"], w=["cl"])
                cl3 = cl.rearrange("p (c t) -> p c t", t=64)
                P.op("act", lambda e: e.activation(out=eC, in_=cl3[:, :, 63], func=AF.Exp), r=["cl"], w=["eC"])

                def v3(ap, p0):
                    return ap[p0:p0 + 64, :].rearrange("p (c t) -> p c t", t=64)
                P.op("act", lambda e: e.activation(out=e1, in_=cl, func=AF.Exp), r=["cl"], w=["e1"])
                halves(lambda p0, c0: P.op("dve", lambda e: e.tensor_tensor(out=ARblk[p0:p0 + 64, :, 1, c0:c0 + 64], in0=v3(rS[:, sl], p0), in1=v3(e1, p0), op=ALU.mult),
                                           r=["e1", "shiftdst"], w=["blk"]))
                P.op("act", lambda e: e.activation(out=e1, in_=cl, func=AF.Exp, scale=-1.0), r=["cl", "blk"], w=["e1"])
                P.op("dve", lambda e: e.tensor_tensor(out=e2, in0=kk, in1=at, op=ALU.mult), r=["kk", "at"], w=["e2"])
                halves(lambda p0, c0: P.op("dve", lambda e: e.tensor_tensor(out=Bblk[p0:p0 + 64, :, c0:c0 + 64], in0=v3(e2, p0), in1=v3(e1, p0), op=ALU.mult),
                                           r=["e1", "e2"], w=["blk"]))
                halves(lambda p0, c0: P.op("dve", lambda e: e.tensor_tensor(out=Kblk[p0:p0 + 64, :, c0:c0 + 64], in0=v3(k2, p0), in1=v3(e1, p0), op=ALU.mult),
                                           r=["e1", "k2"], w=["blk"]))
                P.op("dve", lambda e: e.tensor_tensor(out=e1, in0=cl, in1=lw, op=ALU.subtract), r=["cl", "lw", "blk"], w=["e1"])
                P.op("act", lambda e: e.activation(out=e1, in_=e1, func=AF.Exp), r=["e1"], w=["e1"])
                halves(lambda p0, c0: P.op("dve", lambda e: e.tensor_tensor(out=ARblk[p0:p0 + 64, :, 0, c0:c0 + 64], in0=v3(kk, p0), in1=v3(e1, p0), op=ALU.mult),
                                           r=["e1", "kk"], w=["blk"]))
                P.op("dve", lambda e: e.tensor_tensor(out=e1.rearrange("p (c t) -> p c t", t=64), in0=cl3[:, :, 63:64].broadcast_to([128, 8, 64]), in1=cl3,
                                                      op=ALU.subtract), r=["cl", "blk"], w=["e1"])
                P.op("act", lambda e: e.activation(out=e1, in_=e1, func=AF.Exp), r=["e1"], w=["e1"])
                halves(lambda p0, c0: P.op("dve", lambda e: e.tensor_tensor(out=B2blk[p0:p0 + 64, :, c0:c0 + 64], in0=v3(e2, p0), in1=v3(e1, p0), op=ALU.mult),
                                           r=["e1", "e2"], w=["blk"]))
                halves(lambda p0, c0: P.op("dve", lambda e: e.tensor_tensor(out=K2blk[p0:p0 + 64, :, c0:c0 + 64], in0=v3(k2, p0), in1=v3(e1, p0), op=ALU.mult),
                                           r=["e1", "k2"], w=["blk"]))
                halves(lambda p0, c0: P.op("pool", lambda e: e.tensor_copy(out=Vblk[p0:p0 + 64, :, c0:c0 + 64], in_=v3(vS[:, sl], p0)), r=["shiftdst"], w=["blk"]))

                for g4 in range(2):
                    for j in range(4):
                        cq = g4 * 4 + j
                        b = P.bank()
                        pb = psum[:, b, 0:256].bitcast(BF16)
                        for i, src in enumerate([ARblk[:, cq, 0, :], B2blk[:, cq, :], K2blk[:, cq, :], Vblk[:, cq, :]]):
                            P.op("pe", lambda e, i=i, src=src, pb=pb: e.transpose(out=pb[:, i * 128:(i + 1) * 128], in_=src, identity=identb),
                                 r=["blk", "identb"], w=[("ps", b)])
                        P.op("act", lambda e, j=j, pb=pb: e.activation(out=tm[j].rearrange("p a b -> p (a b)"), in_=pb, func=AF.Copy), r=[("ps", b)], w=[("tm", j)])
                        AR2 = ARblk[:, cq, :, :].rearrange("p a b -> p (a b)")
                        b = P.bank()
                        P.op("pe", lambda e, b=b, cq=cq, AR2=AR2: e.matmul(psum[:, b, 0:256], lhsT=Bblk[:, cq, :], rhs=AR2, start=True, stop=True), r=["blk"], w=[("ps", b)])
                        P.op("dve", lambda e, b=b, j=j: e.tensor_tensor(out=W3[j][0][:, 1, :], in0=psum[:, b, 0:128], in1=M1[:, 0:128], op=ALU.mult),
                             r=[("ps", b), "M1"], w=[("W3", j, 0)])
                        P.op("dve", lambda e, b=b, j=j: e.tensor_tensor(out=LM1[j], in0=psum[:, b, 128:256], in1=M1[:, 128:256], op=ALU.mult),
                             r=[("ps", b), "M1"], w=[("LM1", j)])
                        b = P.bank()
                        P.op("pe", lambda e, b=b, cq=cq, AR2=AR2: e.matmul(psum[:, b, 0:256], lhsT=Kblk[:, cq, :], rhs=AR2, start=True, stop=True), r=["blk"], w=[("ps", b)])
                        P.op("dve", lambda e, b=b, j=j: e.tensor_tensor(out=LM2[j], in0=psum[:, b, 0:256], in1=M2, op=ALU.mult), r=[("ps", b), "M2"], w=[("LM2", j)])
                        b = P.bank()
                        P.op("pe", lambda e, b=b, cq=cq: e.matmul(psum[:, b, 0:128], lhsT=ARblk[:, cq, 0, :], rhs=Bblk[:, cq, :], start=True, stop=True), r=["blk"], w=[("ps", b)])
                        P.op("dve", lambda e, b=b, j=j: e.tensor_tensor(out=W3[j][0][:, 2, :], in0=psum[:, b, 0:128], in1=M3, op=ALU.mult), r=[("ps", b), "M3"], w=[("W3", j, 0)])
                        P.op("pool", lambda e, j=j: e.tensor_copy(out=W3[j][0][:, 0, :], in_=identb), r=["identb"], w=[("W3", j, 0)])
                    for lvl in range(6):
                        last = lvl == 5
                        for j in range(4):
                            cur, nxt = W3[j][lvl % 2], W3[j][(lvl + 1) % 2]
                            b = P.bank()
                            nn = 128 if last else 256
                            P.op("pe", lambda e, b=b, cur=cur, nn=nn: e.matmul(psum[:, b, 0:nn], lhsT=cur[:, 2, :], rhs=cur[:, 0:nn // 128, :].rearrange("p a b -> p (a b)"),
                                                                              start=True, stop=True), r=[("W3", j, lvl % 2)], w=[("ps", b)])
                            if not last:
                                P.op("pe", lambda e, b=b, cur=cur: e.matmul(psum[:, b, 256:384], lhsT=cur[:, 1, :], rhs=cur[:, 2, :], start=True, stop=True),
                                     r=[("W3", j, lvl % 2)], w=[("ps", b)])
                            pdst = TT[j] if last else nxt[:, 0, :]
                            P.op("dve", lambda e, b=b, cur=cur, pdst=pdst: e.tensor_tensor(out=pdst, in0=psum[:, b, 0:128], in1=cur[:, 0, :], op=ALU.add),
                                 r=[("ps", b), ("W3", j, lvl % 2)], w=[("TT", j) if last else ("W3", j, (lvl + 1) % 2)])
                            if not last:
                                P.op("act", lambda e, b=b, nxt=nxt: e.activation(out=nxt[:, 1:3, :].rearrange("p a b -> p (a b)"), in_=psum[:, b, 128:384], func=AF.Copy),
                                     r=[("ps", b)], w=[("W3", j, (lvl + 1) % 2)])
                    for j in range(4):
                        cq = g4 * 4 + j
                        cg = qtr * 8 + cq
                        b = P.bank()
                        P.op("pe", lambda e, b=b, j=j: e.matmul(psum[:, b, 0:128], lhsT=LM2[j][:, 0:128], rhs=tm[j][:, 3, :], start=True, stop=True),
                             r=[("LM2", j), ("tm", j)], w=[("ps", b)])
                        P.op("act", lambda e, b=b: e.activation(out=AG[:, 1, :], in_=psum[:, b, 0:128], func=AF.Copy, scale=-1.0), r=[("ps", b)], w=["AG"])
                        P.op("pool", lambda e, j=j: e.tensor_copy(out=AG[:, 0, :], in_=tm[j][:, 0, :]), r=[("tm", j)], w=["AG"])
                        b = P.bank()
                        P.op("pe", lambda e, b=b, j=j: e.matmul(psum[:, b, 0:256], lhsT=TT[j], rhs=AG.rearrange("p a b -> p (a b)"), start=True, stop=True),
                             r=[("TT", j), "AG"], w=[("ps", b)])
                        P.op("act", lambda e, b=b: e.activation(out=AU.rearrange("p a b -> p (a b)"), in_=psum[:, b, 0:256], func=AF.Copy), r=[("ps", b)], w=["AU"])
                        if cg >= 16:
                            b = P.bank()
                            P.op("pe", lambda e, b=b, j=j: e.matmul(psum[:, b, 0:128], lhsT=AU[:, 0, :], rhs=LM1[j], start=True, stop=True),
                                 r=["AU", ("LM1", j)], w=[("ps", b)])
                            P.op("dve", lambda e, b=b, cq=cq: e.tensor_tensor(out=RbarT, in0=ARblk[:, cq, 1, :], in1=psum[:, b, 0:128], op=ALU.subtract),
                                 r=[("ps", b), "blk"], w=["RbarT"])
                        b = P.bank()
                        P.op("pe", lambda e, b=b, j=j: e.matmul(psum[:, b, 0:128], lhsT=AU[:, 0, :], rhs=tm[j][:, 1, :], start=True, stop=True),
                             r=["AU", ("tm", j)], w=[("ps", b)])
                        P.op("dve", lambda e, b=b, cq=cq: e.scalar_tensor_tensor(out=Phi, in0=ident, scalar=eC[:, cq:cq + 1], in1=psum[:, b, 0:128],
                                                                                op0=ALU.mult, op1=ALU.subtract), r=[("ps", b), "eC", "ident"], w=["Phi"])
                        b = P.bank()
                        P.op("pe", lambda e, b=b, j=j: e.matmul(psum[:, b, 0:128], lhsT=tm[j][:, 1, :], rhs=AU[:, 1, :], start=True, stop=False),
                             r=["AU", ("tm", j)], w=[("ps", b)])
                        P.op("pe", lambda e, b=b, j=j: e.matmul(psum[:, b, 0:128], lhsT=tm[j][:, 2, :], rhs=tm[j][:, 3, :], start=False, stop=True),
                             r=[("tm", j)], w=[("ps", b)])
                        P.op("act", lambda e, b=b: e.activation(out=DeltaT, in_=psum[:, b, 0:128], func=AF.Copy), r=[("ps", b)], w=["DeltaT"])
                        if cg >= 16:
                            to = cg * 64 - NO
                            b = P.bank()
                            P.op("pe", lambda e, b=b, j=j: e.matmul(psum[:, b, 0:128], lhsT=LM1[j], rhs=AU[:, 1, :], start=True, stop=False),
                                 r=["AU", ("LM1", j)], w=[("ps", b)])
                            P.op("pe", lambda e, b=b, j=j: e.matmul(psum[:, b, 0:128], lhsT=LM2[j][:, 128:256], rhs=tm[j][:, 3, :], start=False, stop=False),
                                 r=[("LM2", j), ("tm", j)], w=[("ps", b)])
                            P.op("pe", lambda e, b=b: e.matmul(psum[:, b, 0:128], lhsT=RbarT, rhs=Sb, start=False, stop=True), r=["RbarT", "Sb"], w=[("ps", b)])
                            P.op("act", lambda e, b=b: e.activation(out=ysb, in_=psum[:, b, 0:128], func=AF.Copy), r=[("ps", b)], w=["ysb"])
                            P.op("dve", lambda e: e.tensor_reduce(out=st[:, 0:1], in_=ysb, axis=AX.X, op=ALU.add), r=["ysb"], w=["st"])
                            P.op("act", lambda e: e.activation(out=ysq, in_=ysb, func=AF.Square), r=["ysb"], w=["ysq"])
                            P.op("dve", lambda e: e.tensor_reduce(out=st[:, 1:2], in_=ysq, axis=AX.X, op=ALU.add), r=["ysq"], w=["st"])
                            P.op("dve", lambda e: e.tensor_scalar(out=st[:, 2:3], in0=st[:, 0:1], scalar1=1.0 / 64, scalar2=None, op0=ALU.mult), r=["st"], w=["st"])
                            P.op("dve", lambda e: e.tensor_tensor(out=st[:, 3:4], in0=st[:, 2:3], in1=st[:, 2:3], op=ALU.mult), r=["st"], w=["st"])
                            P.op("dve", lambda e: e.scalar_tensor_tensor(out=st[:, 4:5], in0=st[:, 1:2], scalar=1.0 / 64, in1=st[:, 3:4], op0=ALU.mult, op1=ALU.subtract),
                                 r=["st"], w=["st"])
                            P.op("dve", lambda e: e.tensor_scalar(out=st[:, 4:5], in0=st[:, 4:5], scalar1=GN_EPS, scalar2=None, op0=ALU.add), r=["st"], w=["st"])
                            P.op("act", lambda e: e.activation(out=st[:, 4:5], in_=st[:, 4:5], func=AF.Sqrt), r=["st"], w=["st"])
                            P.op("dve", lambda e: e.reciprocal(out=st[:, 5:6], in_=st[:, 4:5]), r=["st"], w=["st"])
                            P.op("dve", lambda e: e.tensor_scalar(out=yn, in0=ysb, scalar1=st[:, 2:3], scalar2=st[:, 5:6], op0=ALU.subtract, op1=ALU.mult),
                                 r=["ysb", "st"], w=["yn"])
                            P.op("dve", lambda e: e.tensor_tensor(out=yn, in0=yn, in1=lnw, op=ALU.mult), r=["yn", "lnw"], w=["yn"])
                            P.op("dve", lambda e: e.tensor_tensor(out=yn, in0=yn, in1=lnb, op=ALU.add), r=["yn", "lnb"], w=["yn"])
                            b = P.bank()
                            P.op("pe", lambda e, b=b: e.transpose(out=psum[:, b, 0:128], in_=yn, identity=ident), r=["yn", "ident"], w=[("ps", b)])
                            P.op("act", lambda e, b=b: e.activation(out=ytr, in_=psum[:, b, 0:128], func=AF.Copy), r=[("ps", b)], w=["ytr"])
                            P.op("dve", lambda e: e.tensor_tensor(out=t64, in0=ytr[:, 0:64], in1=ytr[:, 64:128], op=ALU.add), r=["ytr"], w=["t64"])
                            P.op("dve", lambda e, cq=cq: e.tensor_tensor(out=t64, in0=t64, in1=bon[:, cq * 64:(cq + 1) * 64], op=ALU.add), r=["t64", "raw"], w=["t64"])
                            P.op("dve", lambda e, cq=cq, to=to: e.tensor_tensor(out=ybuf[:, to:to + 64], in0=t64, in1=gq[:, cq * 64:(cq + 1) * 64], op=ALU.mult),
                                 r=["t64", "gq"], w=["ybuf"])
                        b = P.bank()
                        P.op("pe", lambda e, b=b: e.matmul(psum[:, b, 0:128], lhsT=Phi, rhs=Sst, start=True, stop=True), r=["Phi", "S"], w=[("ps", b)])
                        P.op("dve", lambda e, b=b: e.tensor_tensor(out=Sst, in0=psum[:, b, 0:128], in1=DeltaT, op=ALU.add), r=[("ps", b), "DeltaT"], w=["S"])
                        P.op("act", lambda e: e.activation(out=Sb, in_=Sst, func=AF.Copy), r=["S"], w=["Sb"])
            for qtr in range(4):
                do_quarter(qtr)
            P.dma(ysc[:, 0, hp, :], ybuf, r=["ybuf"], w=["ysc"], grp="ysc")
            if dbg == "rwkv":
                P.dma(dbg_ya[:, hp, :], ybuf, r=["ybuf"], w=["dbg_ya"], grp="dbgya")
        for hp in range(8):
            do_pair(hp)
        P.barrier()

    if dbg not in ("tail", "nsa"):
        rwkv_phase()
    if dbg == "rwkv":
        P.op("sp", None, r=["dbg_ya"])
        P.emit(stack)
        stack.close()
        return nc
    if dbg not in ("tail", "rwkv"):
        nsa_phase()
    if dbg == "nsa":
        P.op("sp", None, r=["dbg_yb"])
        P.emit(stack)
        stack.close()
        return nc

    A.off = phase_off
    yT = A.alloc([128, 2, 8, NO], BF16)
    for wh in range(2):
        for kc in range(8):
            P.dma(yT[:, wh, kc, :], (yfake_d if dbg == "tail" else ysc)[:, wh, kc, :], r=["ysc"], w=["yT"], grp="c2")

    TOK2 = [(0, 512), (512, 512)]
    A.mark()
    tga = A.alloc([128, NO], F32)
    tgb = A.alloc([128, NO], F32)
    tpa = A.alloc([128, NO], F32)
    GA0 = 3520 + 1024 + 6 * 256 + 48
    for dc in range(KC):
        def ev_sig(dst, key):
            def f(b, t0, n):
                P.op("act", lambda e: e.activation(out=dst[:, t0:t0 + n], in_=psum[:, b, 0:n], func=AF.Sigmoid),
                     r=[("ps", b)], w=[key])
            return f
        xn_own = lambda k, t0, n: xnT[:, k, NO + t0:NO + t0 + n]
        fm_linear(w_in, 0, GA0 + dc * 128, 128, xn_own, ["xnT"], KC, TOK2, ev_sig(tga, "tga"))
        fm_linear(w_in, 0, GA0 + D + dc * 128, 128, xn_own, ["xnT"], KC, TOK2, ev_sig(tgb, "tgb"))

        def ev_pa(b, t0, n):
            P.op("dve", lambda e: e.tensor_tensor(out=tpa[:, t0:t0 + n], in0=psum[:, b, 0:n], in1=tga[:, t0:t0 + n], op=ALU.mult),
                 r=[("ps", b), "tga"], w=["tpa"])
        fm_linear(w_out_rwkv, 0, dc * 128, 128, lambda k, t0, n: yT[:, 0, k, t0:t0 + n], ["yT"], 8, TOK2, ev_pa)

        def ev_pb(b, t0, n, dc=dc):
            P.op("dve", lambda e: e.tensor_tensor(out=tgb[:, t0:t0 + n], in0=psum[:, b, 0:n], in1=tgb[:, t0:t0 + n], op=ALU.mult),
                 r=[("ps", b), "tgb"], w=["tgb"])
            P.op("pool", lambda e: e.tensor_tensor(out=mixT[:, dc, t0:t0 + n], in0=tgb[:, t0:t0 + n], in1=tpa[:, t0:t0 + n], op=ALU.add),
                 r=["tgb", "tpa"], w=[("mixT", dc)])
        fm_linear(w_out_nsa, 0, dc * 128, 128, lambda k, t0, n: yT[:, 1, k, t0:t0 + n], ["yT"], 8, TOK2, ev_pb)
    P.barrier()
    A.release()
    mix_keys = [("mixT", dc) for dc in range(KC)]
    A.off = xn_off
    h = A.alloc([128, 8, D], F32)
    for tt in range(8):
        P.dma(h[:, tt, :], xs[NO + tt * 128:NO + (tt + 1) * 128, :], w=[("h", tt)], grp="h%d" % tt)
    for dc in range(KC):
        def ev_h(b, tg, nt, dc=dc):
            for j in range(nt):
                tt = tg + j
                P.op("dve", lambda e, tt=tt, j=j: e.tensor_tensor(out=h[:, tt, dc * 128:(dc + 1) * 128], in0=psum[:, b, j * 128:(j + 1) * 128],
                                                                  in1=h[:, tt, dc * 128:(dc + 1) * 128], op=ALU.add),
                     r=[("ps", b), ("h", tt)], w=[("h", tt)])
        tm_linear(w_o, 0, dc * 128, lambda k, tt: mixT[:, k, tt * 128:(tt + 1) * 128], mix_keys, KC, 8, ev_h)

    if dbg:
        dbg_mix = nc.dram_tensor("dbg_mix", [128, KC, NO], BF16, kind="ExternalOutput").ap()
        for kc in range(KC):
            P.dma(dbg_mix[:, kc, :], mixT[:, kc, :], r=mix_keys, w=[("dbg_mix", kc)], grp="g1")
        dbg_h = nc.dram_tensor("dbg_h", [128, 8, D], F32, kind="ExternalOutput").ap()
        for tt in range(8):
            P.dma(dbg_h[:, tt, :], h[:, tt, :], r=[("h", tt)], w=[("dbg_h", tt)], grp="g2")
    grow = A.alloc([128, D], F32)
    hnT = A.alloc([128, KC, NO], BF16)
    hn = A.alloc([128, D], F32)
    jraw = A.alloc([128, 1024], F32)
    junk2 = jraw.bitcast(BF16)
    ss2 = A.alloc([128, 16], F32)
    rs2 = A.alloc([128, 16], F32)
    P.dma(grow, gmlp_d, w=["grow"], grp="c3")

    def rms_tile(tt, col, src, gkey, dst_fn):
        P.op("act", lambda e: e.activation(out=hn, in_=src, func=AF.Square), r=[("h", tt)], w=["hn"])
        P.op("dve", lambda e: e.tensor_reduce(out=ss2[:, col:col + 1], in_=hn, axis=AX.X, op=ALU.add), r=["hn"], w=[("ss2", col)])
        P.op("dve", lambda e: e.tensor_scalar(out=rs2[:, col:col + 1], in0=ss2[:, col:col + 1], scalar1=1.0 / D, scalar2=EPS,
                                              op0=ALU.mult, op1=ALU.add), r=[("ss2", col)], w=[("rs2", col)])
        P.op("act", lambda e: e.activation(out=rs2[:, col:col + 1], in_=rs2[:, col:col + 1], func=AF.Sqrt), r=[("rs2", col)], w=[("rs2", col)])
        P.op("dve", lambda e: e.reciprocal(out=rs2[:, col:col + 1], in_=rs2[:, col:col + 1]), r=[("rs2", col)], w=[("rs2", col)])
        dst_fn()

    for tt in range(8):
        def mk(tt=tt):
            P.op("dve", lambda e: e.scalar_tensor_tensor(out=hn, in0=h[:, tt, :], scalar=rs2[:, tt:tt + 1], in1=grow,
                                                         op0=ALU.mult, op1=ALU.mult),
                 r=[("h", tt), ("rs2", tt), "grow"], w=["hn"])
            for kg in range(4):
                b = P.bank()
                for j in range(4):
                    kc = kg * 4 + j
                    P.op("pe", lambda e, b=b, j=j, kc=kc: e.transpose(out=psum[:, b, j * 128:(j + 1) * 128],
                                                                     in_=hn[:, kc * 128:(kc + 1) * 128], identity=ident),
                         r=["hn", "ident"], w=[("ps", b)])
                P.op("act", lambda e, b=b, kg=kg: e.activation(out=hnT[:, kg * 4:(kg + 1) * 4, tt * 128:(tt + 1) * 128],
                                                              in_=psum[:, b, :].rearrange("p (a c) -> p a c", a=4), func=AF.Copy),
                     r=[("ps", b)], w=[("hnT", tt)])
        rms_tile(tt, tt, h[:, tt, :], "grow", mk)
    hn_keys = [("hnT", tt) for tt in range(8)]

    P.barrier()
    aT = mixT
    for g in range(4):
        for fl in range(KC):
            def ev_a(b, t0, n, fl=fl):
                sl = (t0 // 512) % 2
                tmp = jraw[:, sl * 512:sl * 512 + n]
                P.op("act", lambda e: e.activation(out=tmp, in_=psum[:, b, 0:n], func=AF.Relu), r=[("ps", b)], w=[("jr", sl)])
                P.op("pool", lambda e: e.tensor_tensor(out=aT[:, fl, t0:t0 + n], in0=tmp, in1=tmp, op=ALU.mult),
                     r=[("jr", sl)], w=[("aT", fl)])
            fm_linear(w_up, 0, (g * KC + fl) * 128, 128, lambda k, t0, n: hnT[:, k, t0:t0 + n], hn_keys, KC, TOK2, ev_a)
        a_keys = [("aT", fl) for fl in range(KC)]
        for dc in range(KC):
            def ev_h2(b, tg, nt, dc=dc):
                for j in range(nt):
                    tt = tg + j
                    P.op("dve", lambda e, tt=tt, j=j: e.tensor_tensor(out=h[:, tt, dc * 128:(dc + 1) * 128], in0=psum[:, b, j * 128:(j + 1) * 128],
                                                                      in1=h[:, tt, dc * 128:(dc + 1) * 128], op=ALU.add),
                         r=[("ps", b), ("h", tt)], w=[("h", tt)])
            tm_linear(w_down, g * 2048, dc * 128, lambda k, tt: aT[:, k, tt * 128:(tt + 1) * 128], a_keys, KC, 8, ev_h2)

    P.dma(grow, gfin_d, r=[], w=["grow"], grp="c3")
    for tt in range(8):
        def mk(tt=tt):
            P.op("dve", lambda e: e.scalar_tensor_tensor(out=h[:, tt, :], in0=h[:, tt, :], scalar=rs2[:, 8 + tt:9 + tt], in1=grow,
                                                         op0=ALU.mult, op1=ALU.mult),
                 r=[("h", tt), ("rs2", 8 + tt), "grow"], w=[("h", tt)])
            P.dma(out_d[tt * 128:(tt + 1) * 128, :], h[:, tt, :], r=[("h", tt)], w=[("out", tt)], grp="o%d" % (tt % 2))
        rms_tile(tt, 8 + tt, h[:, tt, :], "grow", mk)
    P.op("sp", None, r=[("out", tt) for tt in range(8)])
    P.emit(stack)
    stack.close()
    return nc


def host_inputs(inputs, dbg=None):
    f = lambda a: np.ascontiguousarray(np.asarray(a, dtype=np.float32))
    x = f(inputs["x"])
    shared = {
        "w_in": f(inputs["w_in"][0]),
        "w_out_rwkv": f(inputs["w_out_rwkv"][0]),
        "w_out_nsa": f(inputs["w_out_nsa"][0]),
        "w_o": f(inputs["w_o"][0]),
        "mlp_w_up": f(inputs["mlp_w_up"][0]),
        "mlp_w_down": f(inputs["mlp_w_down"][0]),
        "ident": np.eye(128, dtype=np.float32),
        "gmixT": f(np.asarray(inputs["norm_mix"][0]).reshape(KC, 128).T),
        "gmlp_row": f(np.broadcast_to(np.asarray(inputs["norm_mlp"][0])[None, :], (128, D))),
        "gfin_row": f(np.broadcast_to(np.asarray(inputs["norm_final"])[None, :], (128, D))),
    }
    BIG = 30000.0
    kk_, qq_ = np.meshgrid(np.arange(128), np.arange(128), indexing="ij")
    dq = (qq_ - kk_).astype(np.float32)
    shared["c_dfull"] = f(dq)
    shared["c_d0"] = f(np.where(qq_ >= kk_, dq, BIG))
    shared["c_d4"] = f(np.where(qq_ < kk_, dq, BIG))
    slopes = 2.0 ** (-8.0 * np.arange(1, 17) / 16.0)
    bc = np.zeros((128, 256), np.float32)
    for h_ in range(16):
        for rel in range(16):
            bc[:, h_ * 16 + rel] = -slopes[h_] * 128.0 * rel
    shared["c_bcol"] = bc
    em = np.zeros((128, 16, 128), np.float32)
    for kb in range(16):
        for k_ in range(128):
            em[2 * kb + k_ // 64, kb, k_] = 1.0
    import ml_dtypes
    shared["c_emat"] = em.astype(ml_dtypes.bfloat16)
    for nm in ["cmp_w1_k", "cmp_w2_k", "cmp_w1_v", "cmp_w2_v"]:
        shared[nm] = f(inputs[nm][0])
    for nm, src in [("c_pekT", "cmp_pe_k"), ("c_pevT", "cmp_pe_v")]:
        pt = np.zeros((128, 32), np.float32)
        pt[0:64, :] = np.asarray(inputs[src][0]).T
        shared[nm] = pt
    hh_ = np.arange(128) // 64
    same = (hh_[:, None] == hh_[None, :])
    rr_, cc_ = np.meshgrid(np.arange(128) % 64, np.arange(128) % 64, indexing="ij")
    msu = (same & (rr_ < cc_)).astype(np.float32)
    mu_ = (same & (rr_ <= cc_)).astype(np.float32)
    msl = (same & (cc_ < rr_)).astype(np.float32)
    shared["r_m1"] = f(np.concatenate([-msu, mu_], axis=1))
    shared["r_m2"] = f(np.concatenate([msu, mu_], axis=1))
    shared["r_m3"] = f(-msl)
    shared["r_identb"] = np.eye(128, dtype=np.float32).astype(ml_dtypes.bfloat16)
    shared["r_bones"] = f(same.astype(np.float32))
    rm = np.ones((128, 512), np.float32)
    rm[:, ::64] = 0.0
    shared["r_resetm"] = rm
    mu_all = np.asarray(inputs["rwkv_mu"][0], np.float32)
    pt_ = np.zeros((128, 8, 8), np.float32)
    flat = lambda a: np.asarray(a, np.float32).reshape(-1)
    srcs = [mu_all[0:1024], mu_all[1024:2048], mu_all[2048:3072], flat(inputs["rwkv_w0"][0]), flat(inputs["rwkv_a0"][0]),
            flat(inputs["rwkv_k_k"][0]), flat(inputs["rwkv_k_a"][0]), flat(inputs["rwkv_r_k"][0])]
    for j_, a_ in enumerate(srcs):
        pt_[:, :, j_] = a_.reshape(8, 128).T
    shared["r_ptab"] = pt_
    lm = np.zeros((128, 4), np.float32)
    lm[:96, 0] = mu_all[3072:3168]
    lm[:96, 1] = mu_all[3168:3264]
    lm[:, 2] = mu_all[3264:3392]
    lm[:, 3] = mu_all[3392:3520]
    shared["r_lmu"] = lm
    for nm in ["rwkv_w_up", "rwkv_a_up", "rwkv_g_up"]:
        shared[nm] = f(inputs[nm][0])
    lw_ = flat(inputs["rwkv_lnx_w"][0]).reshape(8, 2, 64)
    lb_ = flat(inputs["rwkv_lnx_b"][0]).reshape(8, 2, 64)
    lnw_blk = np.zeros((8, 128, 128), np.float32)
    lnb_blk = np.zeros((8, 128, 128), np.float32)
    for hp_ in range(8):
        for h2 in range(2):
            lnw_blk[hp_, h2 * 64:(h2 + 1) * 64, h2 * 64:(h2 + 1) * 64] = lw_[hp_, h2][None, :]
            lnb_blk[hp_, h2 * 64:(h2 + 1) * 64, h2 * 64:(h2 + 1) * 64] = lb_[hp_, h2][None, :]
    shared["r_lnw"] = lnw_blk
    shared["r_lnb"] = lnb_blk
    n_ = np.arange(127)
    cs = n_[:, None] * 16
    ss = np.arange(32)[None, :] * 64
    overlap = np.clip(np.minimum(cs + 32, ss + 64) - np.maximum(cs, ss), 0, None) / 32.0
    percore = []
    for hf in range(2):
        off = 0 if hf == 1 else 16
        tokv = np.ones(2048, np.float32)
        if hf == 0:
            tokv[:1024] = 0.0
        vn = np.ones(127, np.float32)
        if hf == 0:
            vn[:64] = 0.0
        cc = np.zeros((128, 33), np.float32)
        cc[:127, 0] = vn
        cc[:127, 1:] = overlap * vn[:, None]
        dcm = np.full((128, 8, 128), BIG, np.float32)
        sk = np.zeros((128, 8, 32), np.float32)
        sb_ = np.zeros((128, 8, 32), np.float32)
        for qt in range(8):
            t = 1024 + qt * 128 + np.arange(128)
            dist = t[None, :] - (16 * n_[:, None] + 31)
            ok = (dist >= 0) & (vn[:, None] > 0)
            dcm[:127, qt, :] = np.where(ok, dist, BIG)
            cur = t // 64
            j = np.arange(32)[None, :]
            excl = (j > cur[:, None]) | (j < off)
            forced = (~excl) & ((j == off) | (j == cur[:, None]) | (j == cur[:, None] - 1))
            sk[:, qt, :] = np.where(excl | forced, 0.0, 1.0)
            sb_[:, qt, :] = np.where(excl, -1e30, np.where(forced, 1e30, 0.0))
        percore.append({"c_vtok": f(tokv.reshape(16, 128).T), "c_ccst": cc, "c_dc": dcm, "c_skeep": sk, "c_sbias": sb_})
    maps = []
    for c in range(8):
        b, hf = c // 2, c % 2
        if hf == 1:
            xs_ = x[b]
        else:
            xs_ = np.concatenate([np.zeros((NO, D), np.float32), x[b, :NO]], axis=0)
        m = dict(shared)
        m.update(percore[hf])
        m["xs"] = f(xs_)
        maps.append(m)
    return maps


_NC = {}


def kernel(**inputs):
    if "nc" not in _NC:
        _NC["nc"] = build()
    maps = host_inputs(inputs)
    res = run_bass_kernel_spmd(_NC["nc"], maps, core_ids=list(range(8)))
    out = np.zeros((4, 2048, D), np.float32)
    for c in range(8):
        b, hf = c // 2, c % 2
        out[b, hf * NO:(hf + 1) * NO] = res.results[c]["out"]
    return out
```

```python
import numpy as np
import concourse.bass as bass
import concourse.mybir as mybir
from concourse.bass_utils import run_bass_kernel_spmd

F32 = mybir.dt.float32
BF16 = mybir.dt.bfloat16
AF = mybir.ActivationFunctionType
ALU = mybir.AluOpType
AX = mybir.AxisListType

D = 2048
L = 2048
NO = 1024
KC = 16
DFF = 8192
IN_COLS = 10224
EPS = 1e-5


class Prog:
    def __init__(self, nc):
        self.nc = nc
        self.ops = []
        self.lastw = {}
        self.readers = {}
        self.bank_i = 0

    def _deps(self, idx, r, w):
        deps = set()
        for k in r:
            if k in self.lastw:
                deps.add(self.lastw[k])
        for k in w:
            if k in self.lastw:
                deps.add(self.lastw[k])
            deps.update(self.readers.get(k, ()))
        for k in r:
            self.readers.setdefault(k, []).append(idx)
        for k in w:
            self.lastw[k] = idx
            self.readers[k] = []
        deps.discard(idx)
        return deps

    def op(self, eng, fn, r=(), w=()):
        idx = len(self.ops)
        self.ops.append(dict(eng=eng, fn=fn, deps=self._deps(idx, r, w), dma=None, sig=False))

    def dma(self, out, in_, r=(), w=(), grp="d", q="sp"):
        idx = len(self.ops)
        self.ops.append(dict(eng=q, fn=(lambda e: e.dma_start(out=out, in_=in_)),
                             deps=self._deps(idx, r, w), dma=grp, sig=True))

    def barrier(self):
        last = {}
        for i, o in enumerate(self.ops):
            if o["fn"] is None:
                continue
            key = ("dma", o["dma"]) if o["dma"] is not None else ("eng", o["eng"])
            last[key] = i
        alld = set(last.values())
        for e in ["pe", "dve", "act", "pool", "sp"]:
            self.ops.append(dict(eng=e, fn=None, deps=set(alld), dma=None, sig=False))
        self.lastw = {}
        self.readers = {}

    def bank(self):
        b = self.bank_i
        self.bank_i = (b + 1) % 8
        return b

    def emit(self, stack):
        nc = self.nc
        ops = self.ops
        for o in ops:
            for p in o["deps"]:
                po = ops[p]
                if po["dma"] is not None or po["eng"] != o["eng"] or o["eng"] != "pe":
                    po["sig"] = True
        cnt = {}
        sems = {}
        for o in ops:
            if not o["sig"] or o["fn"] is None:
                continue
            key = ("dma", o["dma"]) if o["dma"] is not None else ("eng", o["eng"])
            if key not in sems:
                sems[key] = stack.enter_context(nc.semaphore("s_%s_%s" % key))
                cnt[key] = 0
            cnt[key] += 16 if o["dma"] is not None else 1
            o["sem"] = sems[key]
            o["val"] = cnt[key]
            o["skey"] = key
        block = stack.enter_context(nc.Block())

        def run(engname, e):
            waited = {}
            for o in ops:
                if o["eng"] != engname:
                    continue
                need = {}
                for p in o["deps"]:
                    po = ops[p]
                    if po["dma"] is None and po["eng"] == engname and engname == "pe":
                        continue
                    k = po["skey"]
                    if po["val"] > need.get(k, 0):
                        need[k] = po["val"]
                for k, v in need.items():
                    if v > waited.get(k, 0):
                        e.wait_ge(sems[k], v)
                        waited[k] = v
                if o["fn"] is None:
                    continue
                inst = o["fn"](e)
                if o["sig"]:
                    inst.then_inc(o["sem"], 16 if o["dma"] is not None else 1)

        @block.tensor
        def _(e):
            run("pe", e)

        @block.vector
        def _(e):
            run("dve", e)

        @block.scalar
        def _(e):
            run("act", e)

        @block.gpsimd
        def _(e):
            run("pool", e)

        @block.sync
        def _(e):
            run("sp", e)


class Arena:
    def __init__(self, t, nwords):
        self.t = t
        self.n = nwords
        self.off = 0
        self.marks = []

    def alloc(self, shape, dtype):
        free = int(np.prod(shape[1:]))
        words = free if dtype == F32 else (free + 1) // 2
        assert self.off + words <= self.n, ("arena overflow", self.off, words, self.n)
        v = self.t[:, self.off:self.off + words]
        self.off += words
        if dtype != F32:
            v = v.bitcast(dtype)
            if free % 2:
                v = v[:, 0:free]
        if len(shape) == 3:
            v = v.rearrange("p (a b) -> p a b", a=shape[1])
        elif len(shape) == 4:
            v = v.rearrange("p (a b c) -> p a b c", a=shape[1], b=shape[2])
        return v

    def mark(self):
        self.marks.append(self.off)

    def release(self):
        self.off = self.marks.pop()


def build(dbg=None):
    nc = bass.Bass("TRN2", target_bir_lowering=False)
    import contextlib
    stack = contextlib.ExitStack()

    def din(name, shape, dt=F32):
        if dbg == "A" and name not in ("xs", "ident", "gmixT"):
            return None
        if dbg in ("nsa", "rwkv") and name in ("w_out_rwkv", "w_out_nsa", "w_o", "mlp_w_up", "mlp_w_down", "gmlp_row", "gfin_row"):
            return None
        return nc.dram_tensor(name, list(shape), dt, kind="ExternalInput").ap()

    xs = din("xs", [L, D])
    w_in = din("w_in", [D, IN_COLS])
    w_out_rwkv = din("w_out_rwkv", [1024, D])
    w_out_nsa = din("w_out_nsa", [1024, D])
    w_o = din("w_o", [D, D])
    w_up = din("mlp_w_up", [D, DFF])
    w_down = din("mlp_w_down", [DFF, D])
    ident_d = din("ident", [128, 128])
    gmix_d = din("gmixT", [128, KC])
    gmlp_d = din("gmlp_row", [128, D])
    gfin_d = din("gfin_row", [128, D])
    c_d0 = din("c_d0", [128, 128]); c_dfull = din("c_dfull", [128, 128]); c_d4 = din("c_d4", [128, 128])
    c_dc = din("c_dc", [128, 8, 128]); c_bcol = din("c_bcol", [128, 256]); c_emat = din("c_emat", [128, 16, 128], BF16)
    c_skeep = din("c_skeep", [128, 8, 32]); c_sbias = din("c_sbias", [128, 8, 32]); c_vtok = din("c_vtok", [128, 16])
    c_ccst = din("c_ccst", [128, 33])
    c_w1k = din("cmp_w1_k", [2048, 256]); c_w2k = din("cmp_w2_k", [256, 64]); c_pek = din("c_pekT", [128, 32])
    c_w1v = din("cmp_w1_v", [2048, 256]); c_w2v = din("cmp_w2_v", [256, 64]); c_pev = din("c_pevT", [128, 32])
    r_m1 = din("r_m1", [128, 256]); r_m2 = din("r_m2", [128, 256]); r_m3 = din("r_m3", [128, 128])
    r_identb = din("r_identb", [128, 128], BF16); r_bones = din("r_bones", [128, 128]); r_resetm = din("r_resetm", [128, 512])
    r_ptab = din("r_ptab", [128, 8, 8]); r_lmu = din("r_lmu", [128, 4])
    r_wup = din("rwkv_w_up", [96, 1024]); r_aup = din("rwkv_a_up", [96, 1024]); r_gup = din("rwkv_g_up", [256, 1024])
    r_lnw = din("r_lnw", [8, 128, 128]); r_lnb = din("r_lnb", [8, 128, 128])
    yfake_d = din("yfake", [128, 2, 8, NO], BF16) if dbg == "tail" else None
    out_d = nc.dram_tensor("out", [NO, D], F32, kind="ExternalOutput").ap() if dbg not in ("A", "nsa", "rwkv") else None
    dbg_ya = nc.dram_tensor("dbg_ya", [128, 8, NO], BF16, kind="ExternalOutput").ap() if dbg == "rwkv" else None
    dbg_yb = nc.dram_tensor("dbg_yb", [128, 8, NO], BF16, kind="ExternalOutput").ap() if dbg == "nsa" else None

    NW = 48896
    arena_t = stack.enter_context(nc.sbuf_tensor("arena", [128, NW], F32))
    psum = stack.enter_context(nc.psum_tensor("ps", [128, 8, 512], F32))
    A = Arena(arena_t, NW)
    P = Prog(nc)

    ident = A.alloc([128, 128], F32)
    gmixT = A.alloc([128, KC], F32)
    P.dma(ident, ident_d, w=["ident"], grp="c0")
    P.dma(gmixT, gmix_d, w=["gmixT"], grp="c1")

    NST = 2
    wst = [A.alloc([128, KC, 128], F32) for _ in range(NST)]
    wbf = [A.alloc([128, KC, 128], BF16) for _ in range(NST)]
    wctr = [0]

    def wload(wap, r0, c0, ncols=128, nk=KC, cast_eng=None, segs=None):
        i = wctr[0]
        wctr[0] += 1
        s = i % NST
        if segs is None:
            segs = [(c0, ncols)]
        o0 = 0
        for (cc, nn) in segs:
            src = wap[r0:r0 + nk * 128, cc:cc + nn].rearrange("(k p) c -> p k c", p=128)
            P.dma(wst[s][:, 0:nk, o0:o0 + nn], src, w=[("wst", s)], grp="w%d" % s)
            o0 += nn
        ncols = o0
        eng = cast_eng or ("pool" if i % 2 == 0 else "act")
        o, i_ = wbf[s][:, 0:nk, 0:ncols], wst[s][:, 0:nk, 0:ncols]
        if eng == "act":
            P.op("act", lambda e, o=o, i_=i_: e.activation(out=o, in_=i_, func=AF.Copy), r=[("wst", s)], w=[("wbf", s)])
        else:
            P.op(eng, lambda e, o=o, i_=i_: e.tensor_copy(out=o, in_=i_), r=[("wst", s)], w=[("wbf", s)])
        return wbf[s], ("wbf", s)

    mix_raw = A.alloc([128, 8192], F32)
    mixT = mix_raw.bitcast(BF16).rearrange("p (a b) -> p a b", a=KC)
    xn_off = A.off
    xnT = A.alloc([128, KC, L], BF16)
    phase_off = A.off
    ysc = nc.dram_tensor("ysc", [128, 2, 8, NO], BF16, kind="Internal").ap()

    A.mark()
    xt = [A.alloc([128, D], F32) for _ in range(2)]
    junk = A.alloc([128, D], F32)
    ssq = A.alloc([128, 16], F32)
    rstd = A.alloc([128, 16], F32)
    for tt in range(16):
        s = tt % 2
        P.dma(xt[s], xs[tt * 128:(tt + 1) * 128, :], w=[("xt", s)], grp="x%d" % s)
        P.op("act", lambda e, s=s, tt=tt: e.activation(out=junk, in_=xt[s], func=AF.Square), r=[("xt", s)], w=["junk"])
        P.op("dve", lambda e, tt=tt: e.tensor_reduce(out=ssq[:, tt:tt + 1], in_=junk, axis=AX.X, op=ALU.add), r=["junk"], w=[("ssq", tt)])
        P.op("dve", lambda e, tt=tt: e.tensor_scalar(out=rstd[:, tt:tt + 1], in0=ssq[:, tt:tt + 1], scalar1=1.0 / D, scalar2=EPS,
                                                     op0=ALU.mult, op1=ALU.add), r=[("ssq", tt)], w=[("rstd", tt)])
        P.op("act", lambda e, tt=tt: e.activation(out=rstd[:, tt:tt + 1], in_=rstd[:, tt:tt + 1], func=AF.Sqrt), r=[("rstd", tt)], w=[("rstd", tt)])
        P.op("dve", lambda e, tt=tt: e.reciprocal(out=rstd[:, tt:tt + 1], in_=rstd[:, tt:tt + 1]), r=[("rstd", tt)], w=[("rstd", tt)])
        P.op("pool", lambda e, s=s, tt=tt: e.tensor_scalar(out=xt[s], in0=xt[s], scalar1=rstd[:, tt:tt + 1], scalar2=None,
                                                           op0=ALU.mult), r=[("xt", s), ("rstd", tt)], w=[("xt", s)])
        for kg in range(4):
            b = P.bank()
            for j in range(4):
                kc = kg * 4 + j
                P.op("pe", lambda e, s=s, b=b, j=j, kc=kc: e.transpose(out=psum[:, b, j * 128:(j + 1) * 128],
                                                                     in_=xt[s][:, kc * 128:(kc + 1) * 128], identity=ident),
                     r=[("xt", s), "ident"], w=[("ps", b)])
            P.op("dve", lambda e, b=b, kg=kg, tt=tt: e.tensor_tensor(
                out=xnT[:, kg * 4:(kg + 1) * 4, tt * 128:(tt + 1) * 128],
                in0=psum[:, b, :].rearrange("p (a c) -> p a c", a=4),
                in1=gmixT[:, kg * 4:(kg + 1) * 4].unsqueeze(2).broadcast_to([128, 4, 128]), op=ALU.mult),
                r=[("ps", b), "gmixT"], w=[("xnT", tt)])
    if dbg:
        dbg_xn = nc.dram_tensor("dbg_xn", [128, KC, L], BF16, kind="ExternalOutput").ap()
        for kc in range(KC):
            P.dma(dbg_xn[:, kc, :], xnT[:, kc, :], r=[("xnT", tt) for tt in range(16)], w=[("dbg_xn", kc)], grp="g0")
    if dbg == "A":
        P.op("sp", None, r=[("dbg_xn", kc) for kc in range(KC)])
        P.emit(stack)
        stack.close()
        return nc
    P.barrier()
    A.release()

    def fm_linear(wap, r0, c0, ncols, rhs_fn, rkeys, nk, toks, evac, segs=None):
        wt, wkey = wload(wap, r0, c0, ncols, nk, segs=segs)
        for (t0, n) in toks:
            b = P.bank()
            for k in range(nk):
                P.op("pe", lambda e, b=b, k=k, t0=t0, n=n: e.matmul(psum[0:ncols, b, 0:n], lhsT=wt[:, k, 0:ncols], rhs=rhs_fn(k, t0, n),
                                                                   start=(k == 0), stop=(k == nk - 1)),
                     r=[wkey] + list(rkeys), w=[("ps", b)])
            evac(b, t0, n)

    def tm_linear(wap, r0, c0, lhs_fn, lkeys, nk, ntt, evac, ncols=128):
        wt, wkey = wload(wap, r0, c0, ncols, nk)
        for tg in range(0, ntt, 4):
            b = P.bank()
            for j in range(4):
                tt = tg + j
                for k in range(nk):
                    P.op("pe", lambda e, b=b, k=k, tt=tt, j=j: e.matmul(psum[:, b, j * 128:j * 128 + ncols], lhsT=lhs_fn(k, tt), rhs=wt[:, k, 0:ncols],
                                                                       start=(k == 0), stop=(k == nk - 1)),
                         r=[wkey] + list(lkeys), w=[("ps", b)])
            evac(b, tg, 4)

    Q0, KCR, VCR, KS, VS, KW, VW, NGC = 3520, 4544, 4800, 5056, 5312, 5568, 5824, 6080
    SLOPES = [2.0 ** (-8.0 * (i + 1) / 16.0) for i in range(16)]
    BIG = 30000.0
    TOK4 = [(0, 512), (512, 512), (1024, 512), (1536, 512)]
    TOK2 = [(0, 512), (512, 512)]
    xn_all = lambda k, t0, n: xnT[:, k, t0:t0 + n]
    xn_own = lambda k, t0, n: xnT[:, k, NO + t0:NO + t0 + n]

    def nsa_phase():
        A.off = phase_off
        qT = mix_raw[:, 0:4096].bitcast(BF16).rearrange("p (a b) -> p a b", a=8)
        ksT = mix_raw[:, 4096:8192].bitcast(BF16).rearrange("p (a b) -> p a b", a=4)
        kwT = A.alloc([128, 4, L], BF16)
        vsx = A.alloc([128, 16, 4, 65], BF16)
        vwx = A.alloc([128, 16, 4, 65], BF16)
        d0 = A.alloc([128, 128], F32)
        dfull = A.alloc([128, 128], F32)
        d4 = A.alloc([128, 128], F32)
        dc = A.alloc([128, 8, 128], F32)
        bcol = A.alloc([128, 256], F32)
        emat = A.alloc([128, 16, 128], BF16)
        skeep = A.alloc([128, 8, 32], F32)
        sbias = A.alloc([128, 8, 32], F32)
        vtok = A.alloc([128, 16], F32)
        ccst = A.alloc([128, 33], F32)
        sg = A.alloc([128, 8, 48], F32)
        kcT = A.alloc([128, 4, 128], BF16)
        vcx = A.alloc([128, 4, 98], BF16)
        for (dst, src, nm) in [(d0, c_d0, "d0"), (dfull, c_dfull, "dfull"), (d4, c_d4, "d4"), (dc, c_dc, "dc"), (bcol, c_bcol, "bcol"),
                               (emat, c_emat, "emat"), (skeep, c_skeep, "skeep"), (sbias, c_sbias, "sbias"), (vtok, c_vtok, "vtok"),
                               (ccst, c_ccst, "ccst")]:
            P.dma(dst, src, w=[nm], grp="n_" + nm)

        for hp in range(8):
            def ev_q(b, t0, n, hp=hp):
                P.op("act", lambda e: e.activation(out=qT[:, hp, t0:t0 + n], in_=psum[:, b, 0:n], func=AF.Copy, scale=0.125),
                     r=[("ps", b)], w=["qT"])
            fm_linear(w_in, 0, Q0 + hp * 128, 128, xn_own, ["xnT"], KC, TOK2, ev_q)
        for (c0, dstT, nm) in [(KS, ksT, "ksT"), (KW, kwT, "kwT")]:
            for g in range(4):
                def ev_k(b, t0, n, g=g, dstT=dstT, nm=nm):
                    P.op("act", lambda e: e.activation(out=dstT[:, g, t0:t0 + n], in_=psum[:, b, 0:n], func=AF.Copy),
                         r=[("ps", b)], w=[nm])
                fm_linear(w_in, 0, 0, 128, xn_all, ["xnT"], KC, TOK4, ev_k, segs=[(c0 + g * 64, 64), (c0 + g * 64, 64)])
        for (c0, dstx, nm) in [(VS, vsx, "vsx"), (VW, vwx, "vwx")]:
            for half in range(2):
                def ev_v(b, tg, nt, half=half, dstx=dstx, nm=nm):
                    P.op("act", lambda e: e.activation(out=dstx[:, tg:tg + 4, 2 * half:2 * half + 2, 0:64],
                                                       in_=psum[:, b, :].rearrange("p (a g c) -> p a g c", a=4, g=2), func=AF.Copy),
                         r=[("ps", b)], w=[nm])
                tm_linear(w_in, 0, c0 + half * 128, lambda k, tt: xnT[:, k, tt * 128:(tt + 1) * 128], ["xnT"], KC, 16, ev_v)
            for g in range(4):
                P.op("dve", lambda e, g=g, dstx=dstx: e.tensor_copy(out=dstx[:, :, g, 64], in_=vtok), r=["vtok"], w=[nm])

        def ev_g(b, tg, nt):
            P.op("act", lambda e: e.activation(out=sg[:, tg:tg + 4, :], in_=psum[:, b, :].rearrange("p (a c) -> p a c", a=4)[:, :, 0:48],
                                               func=AF.Sigmoid), r=[("ps", b)], w=["sg"])
        tm_linear(w_in, 0, NGC, lambda k, tt: xnT[:, k, NO + tt * 128:NO + (tt + 1) * 128], ["xnT"], KC, 8, ev_g, ncols=48)

        A.mark()
        rawT = A.alloc([128, 2, L], BF16)
        peT32 = A.alloc([128, 32], F32)
        peT = A.alloc([128, 32], BF16)
        bh = A.alloc([128, 2], F32)
        hacc = A.alloc([128, 4, 2, 128], F32)
        gx = A.alloc([128, 128], F32)
        gx2 = A.alloc([128, 128], F32)
        gT = A.alloc([128, 4, 2, 128], BF16)
        w2s = A.alloc([128, 2, 64], F32)
        w2b = A.alloc([128, 2, 128], BF16)
        for which, (craw, w1_d, w2_d, pe_d) in enumerate([(KCR, c_w1k, c_w2k, c_pek), (VCR, c_w1v, c_w2v, c_pev)]):
            for tl in range(2):
                def ev_r(b, t0, n, tl=tl):
                    P.op("act", lambda e: e.activation(out=rawT[:, tl, t0:t0 + n], in_=psum[:, b, 0:n], func=AF.Copy),
                         r=[("ps", b)], w=["rawT"])
                fm_linear(w_in, 0, craw + tl * 128, 128, xn_all, ["xnT"], KC, TOK4, ev_r)
            P.dma(peT32, pe_d, w=["peT32"], grp="n_pe")
            P.op("dve", lambda e: e.tensor_copy(out=peT, in_=peT32), r=["peT32"], w=["peT"])
            P.dma(w2s, w2_d.rearrange("(k p) c -> p k c", p=128), w=["w2s"], grp="n_w2")
            P.op("dve", lambda e: e.tensor_copy(out=w2b[:, :, 0:64], in_=w2s), r=["w2s"], w=["w2b"])
            P.op("dve", lambda e: e.tensor_copy(out=w2b[:, :, 64:128], in_=w2s), r=["w2s"], w=["w2b"])
            for lg in range(4):
                i = wctr[0]
                wctr[0] += 1
                s_ = i % NST
                wv32 = wst[s_].rearrange("p a b -> p (a b)").rearrange("p (l c) -> p l c", l=8)
                wvb = wbf[s_].rearrange("p a b -> p (a b)").rearrange("p (l c) -> p l c", l=8)
                src = w1_d[lg * 512:(lg + 1) * 512, :].rearrange("(l d) c -> d l c", d=64)
                P.dma(wv32[0:64], src, w=[("wst", s_)], grp="w%d" % s_)
                P.dma(wv32[64:128], src, w=[("wst", s_)], grp="w%d" % s_)
                P.op("pool", lambda e, wvb=wvb, wv32=wv32: e.tensor_copy(out=wvb, in_=wv32), r=[("wst", s_)], w=[("wbf", s_)])
                for g in range(4):
                    p0 = (g % 2) * 64
                    b = P.bank()
                    for hc in range(2):
                        for l8 in range(8):
                            l = lg * 8 + l8
                            P.op("pe", lambda e, b=b, g=g, hc=hc, l8=l8, l=l, p0=p0, wvb=wvb: e.matmul(
                                psum[:, b, hc * 128:hc * 128 + 127], lhsT=wvb[p0:p0 + 64, l8, hc * 128:(hc + 1) * 128],
                                rhs=rawT[p0:p0 + 64, g // 2, l:l + 16 * 126 + 1:16], start=(l8 == 0), stop=(l8 == 7)),
                                r=[("wbf", s_), "rawT"], w=[("ps", b)])
                    hv = hacc[:, g, :, :]
                    pv_ = psum[:, b, 0:256].rearrange("p (a c) -> p a c", a=2)
                    if lg == 0:
                        P.op("act", lambda e, hv=hv, pv_=pv_: e.activation(out=hv[:, :, 0:127], in_=pv_[:, :, 0:127], func=AF.Copy), r=[("ps", b)], w=[("hacc", g)])
                    else:
                        P.op("dve", lambda e, hv=hv, pv_=pv_: e.tensor_tensor(out=hv[:, :, 0:127], in0=pv_[:, :, 0:127], in1=hv[:, :, 0:127], op=ALU.add),
                             r=[("ps", b), ("hacc", g)], w=[("hacc", g)])
                b = P.bank()
                for hc in range(2):
                    for l8 in range(8):
                        l = lg * 8 + l8
                        P.op("pe", lambda e, b=b, hc=hc, l8=l8, l=l, wvb=wvb: e.matmul(
                            psum[:, b, hc * 2:hc * 2 + 1], lhsT=wvb[0:64, l8, hc * 128:(hc + 1) * 128],
                            rhs=peT[0:64, l:l + 1], start=(l8 == 0), stop=(l8 == 7)),
                            r=[("wbf", s_), "peT"], w=[("ps", b)])
                pb_ = psum[:, b, 0:4].rearrange("p (a c) -> p a c", a=2)[:, :, 0]
                if lg == 0:
                    P.op("dve", lambda e, pb_=pb_: e.tensor_copy(out=bh, in_=pb_), r=[("ps", b)], w=["bh"])
                else:
                    P.op("dve", lambda e, pb_=pb_: e.tensor_tensor(out=bh, in0=pb_, in1=bh, op=ALU.add), r=[("ps", b), "bh"], w=["bh"])
            for g in range(4):
                for hc in range(2):
                    src = hacc[:, g, hc, 0:127]
                    P.op("act", lambda e, src=src, hc=hc: e.activation(out=gx[:, 0:127], in_=src, func=AF.Identity, bias=bh[:, hc:hc + 1]),
                         r=[("hacc", g), "bh"], w=["gx"])
                    P.op("dve", lambda e: e.tensor_tensor(out=gx2[:, 0:127], in0=gx[:, 0:127], in1=gx[:, 0:127], op=ALU.mult), r=["gx"], w=["gx2"])
                    P.op("dve", lambda e: e.tensor_scalar(out=gx2[:, 0:127], in0=gx2[:, 0:127], scalar1=0.044715, scalar2=1.0, op0=ALU.mult, op1=ALU.add),
                         r=["gx2"], w=["gx2"])
                    P.op("dve", lambda e: e.tensor_tensor(out=gx2[:, 0:127], in0=gx2[:, 0:127], in1=gx[:, 0:127], op=ALU.mult), r=["gx", "gx2"], w=["gx2"])
                    P.op("act", lambda e: e.activation(out=gx2[:, 0:127], in_=gx2[:, 0:127], func=AF.Tanh, scale=0.7978845608028654), r=["gx2"], w=["gx2"])
                    P.op("dve", lambda e: e.tensor_scalar(out=gx2[:, 0:127], in0=gx2[:, 0:127], scalar1=0.5, scalar2=0.5, op0=ALU.mult, op1=ALU.add),
                         r=["gx2"], w=["gx2"])
                    P.op("dve", lambda e, g=g, hc=hc: e.tensor_tensor(out=gT[:, g, hc, 0:127], in0=gx2[:, 0:127], in1=gx[:, 0:127], op=ALU.mult),
                         r=["gx", "gx2"], w=["gT"])
            for g in range(4):
                b = 7
                if which == 0:
                    for hc in range(2):
                        P.op("pe", lambda e, b=b, g=g, hc=hc: e.matmul(psum[:, b, 0:127], lhsT=w2b[:, hc, :], rhs=gT[:, g, hc, 0:127],
                                                                      start=(hc == 0), stop=(hc == 1)), r=["w2b", "gT"], w=[("ps", b)])
                    P.op("act", lambda e, b=b, g=g: e.activation(out=kcT[:, g, 0:127], in_=psum[:, b, 0:127], func=AF.Copy), r=[("ps", b)], w=["kcT"])
                else:
                    for hc in range(2):
                        P.op("pe", lambda e, b=b, g=g, hc=hc: e.matmul(psum[0:127, b, 0:64], lhsT=gT[:, g, hc, 0:127], rhs=w2b[:, hc, 0:64],
                                                                      start=(hc == 0), stop=(hc == 1)), r=["w2b", "gT"], w=[("ps", b)])
                    P.op("dve", lambda e, b=b, g=g: e.tensor_scalar(out=vcx[0:127, g, 0:64], in0=psum[0:127, b, 0:64], scalar1=ccst[0:127, 0:1], scalar2=None,
                                                                   op0=ALU.mult), r=[("ps", b), "ccst"], w=["vcx"])
                    P.op("dve", lambda e, g=g: e.tensor_copy(out=vcx[0:127, g, 64:97], in_=ccst[0:127, :]), r=["ccst"], w=["vcx"])
        P.barrier()
        A.release()

        ynsa = A.alloc([128, 1024], F32)
        ybt = A.alloc([128, 8, 128], BF16)
        impg = A.alloc([128, 4, 32], F32)
        sc = A.alloc([128, 32], F32)
        cmpt = A.alloc([128, 32, 32], F32)
        rk = A.alloc([128, 32], F32)
        t1 = A.alloc([128, 32], F32)
        selb = A.alloc([128, 4, 32], F32)
        selbT = A.alloc([128, 4, 128], BF16)
        sps = [A.alloc([128, 128], F32) for _ in range(4)]
        pts = [A.alloc([128, 128], BF16) for _ in range(4)]
        rv = A.alloc([128, 8], F32)
        sub = [0]
        acc_i = [0]

        def s_region():
            i = sub[0]
            sub[0] = (i + 1) % 4
            return i, 0

        def a_region():
            i = acc_i[0]
            acc_i[0] = (i + 1) % 3
            return 4 + i, 0
        vis = [0]

        pending = []
        LAG = 2

        def flush(n=0):
            while len(pending) > n:
                pending.pop(0)()

        def visit(h, qt, klhsT, nk_rows, extra_mm, dtile, bias_ap, vrhs, acc, first, last, kdeps, fin=None):
            g, p0 = h // 4, (h % 2) * 64
            sb, so = s_region()
            i3 = vis[0] % 4
            vis[0] += 1
            spt, ptt = sps[i3], pts[i3]
            skey = ("ps", sb)
            P.op("pe", lambda e: e.matmul(psum[0:nk_rows, sb, so:so + 128], lhsT=klhsT, rhs=qT[p0:p0 + 64, h // 2, qt * 128:(qt + 1) * 128],
                                          start=True, stop=(extra_mm is None)), r=["qT"] + kdeps, w=[skey])
            if extra_mm is not None:
                el, er = extra_mm
                P.op("pe", lambda e: e.matmul(psum[0:nk_rows, sb, so:so + 128], lhsT=el, rhs=er, start=False, stop=True),
                     r=["emat", ("selbT", g)], w=[skey])
            P.op("dve", lambda e: e.scalar_tensor_tensor(out=spt[0:nk_rows, :], in0=dtile, scalar=-SLOPES[h], in1=psum[0:nk_rows, sb, so:so + 128],
                                                         op0=ALU.mult, op1=ALU.add), r=[skey, "dtiles"], w=[("sp", i3)])
            if bias_ap is None:
                P.op("act", lambda e: e.activation(out=ptt[0:nk_rows, :], in_=spt[0:nk_rows, :], func=AF.Exp), r=[("sp", i3)], w=[("pt", i3)])
            else:
                P.op("act", lambda e: e.activation(out=ptt[0:nk_rows, :], in_=spt[0:nk_rows, :], func=AF.Exp, bias=bias_ap),
                     r=[("sp", i3), "bcol"], w=[("pt", i3)])
            ab, ao, ncol = acc

            def back():
                P.op("pe", lambda e: e.matmul(psum[:, ab, ao:ao + ncol], lhsT=ptt[0:nk_rows, :], rhs=vrhs, start=first, stop=last),
                     r=[("pt", i3)] + kdeps, w=[("ps", ab)])
                if fin is not None:
                    fin()
            pending.append(back)
            flush(LAG)

        def finish(h, qt, acc, br, first_branch):
            ab, ao, ncol = acc
            akey = ("ps", ab)
            c = vis[0] % 8
            P.op("dve", lambda e: e.tensor_scalar(out=rv[:, c:c + 1], in0=psum[:, ab, ao + 64:ao + 65], scalar1=1e-30, scalar2=None, op0=ALU.max),
                 r=[akey], w=["rv"])
            P.op("dve", lambda e: e.reciprocal(out=rv[:, c:c + 1], in_=rv[:, c:c + 1]), r=["rv"], w=["rv"])
            if br == 0:
                g = h // 4
                if h % 4 == 0:
                    P.op("dve", lambda e: e.tensor_scalar(out=impg[:, g, :], in0=psum[:, ab, ao + 65:ao + 97], scalar1=rv[:, c:c + 1], scalar2=None,
                                                          op0=ALU.mult), r=[akey, "rv"], w=[("impg", g)])
                else:
                    P.op("dve", lambda e: e.scalar_tensor_tensor(out=impg[:, g, :], in0=psum[:, ab, ao + 65:ao + 97], scalar=rv[:, c:c + 1],
                                                                 in1=impg[:, g, :], op0=ALU.mult, op1=ALU.add),
                         r=[akey, "rv", ("impg", g)], w=[("impg", g)])
            P.op("dve", lambda e: e.tensor_tensor(out=rv[:, c:c + 1], in0=rv[:, c:c + 1], in1=sg[:, qt, br * 16 + h:br * 16 + h + 1], op=ALU.mult),
                 r=["rv", "sg"], w=["rv"])
            yh = ynsa[:, h * 64:(h + 1) * 64]
            if first_branch:
                P.op("dve", lambda e: e.tensor_scalar(out=yh, in0=psum[:, ab, ao:ao + 64], scalar1=rv[:, c:c + 1], scalar2=None, op0=ALU.mult),
                     r=[akey, "rv"], w=[("ynsa", h)])
            else:
                P.op("dve", lambda e: e.scalar_tensor_tensor(out=yh, in0=psum[:, ab, ao:ao + 64], scalar=rv[:, c:c + 1], in1=yh,
                                                             op0=ALU.mult, op1=ALU.add), r=[akey, "rv", ("ynsa", h)], w=[("ynsa", h)])

        for qt in range(8):
            qb = 8 + qt
            for h in range(16):
                g, p0 = h // 4, (h % 2) * 64
                ab, ao = a_region()
                acc = (ab, ao, 97)
                visit(h, qt, kcT[p0:p0 + 64, g, 0:127], 127, None, dc[0:127, qt, :], None, vcx[0:127, g, 0:97], acc, True, True, ["kcT", "vcx"],
                      fin=(lambda h=h, acc=acc: finish(h, qt, acc, 0, True)))
            flush()
            for g in range(4):
                P.op("dve", lambda e, g=g, qt=qt: e.tensor_tensor(out=sc, in0=impg[:, g, :], in1=skeep[:, qt, :], op=ALU.mult), r=[("impg", g), "skeep"], w=["sc"])
                P.op("dve", lambda e, qt=qt: e.tensor_tensor(out=sc, in0=sc, in1=sbias[:, qt, :], op=ALU.add), r=["sc", "sbias"], w=["sc"])
                P.op("dve", lambda e: e.tensor_tensor(out=cmpt, in0=sc.unsqueeze(1).broadcast_to([128, 32, 32]),
                                                      in1=sc.unsqueeze(2).broadcast_to([128, 32, 32]), op=ALU.is_gt), r=["sc"], w=["cmpt"])
                P.op("dve", lambda e: e.tensor_reduce(out=rk, in_=cmpt, axis=AX.X, op=ALU.add), r=["cmpt"], w=["rk"])
                P.op("dve", lambda e: e.tensor_scalar(out=rk, in0=rk, scalar1=15.5, scalar2=None, op0=ALU.is_lt), r=["rk"], w=["rk"])
                P.op("dve", lambda e: e.tensor_scalar(out=t1, in0=sc, scalar1=-1e29, scalar2=None, op0=ALU.is_gt), r=["sc"], w=["t1"])
                P.op("dve", lambda e: e.tensor_tensor(out=rk, in0=rk, in1=t1, op=ALU.mult), r=["rk", "t1"], w=["rk"])
                P.op("dve", lambda e, g=g: e.tensor_scalar(out=selb[:, g, :], in0=rk, scalar1=-1.0, scalar2=BIG, op0=ALU.add, op1=ALU.mult),
                     r=["rk"], w=[("selb", g)])
                sb, so = s_region()
                P.op("pe", lambda e, g=g, sb=sb, so=so: e.transpose(out=psum[0:32, sb, so:so + 128], in_=selb[:, g, :], identity=ident),
                     r=[("selb", g), "ident"], w=[("ps", sb)])
                P.op("act", lambda e, g=g, sb=sb, so=so: e.activation(out=selbT[0:32, g, :], in_=psum[0:32, sb, so:so + 128], func=AF.Copy),
                     r=[("ps", sb)], w=[("selbT", g)])
            for h in range(16):
                g, p0 = h // 4, (h % 2) * 64
                ab, ao = a_region()
                acc = (ab, ao, 65)
                for kb in range(qb + 1):
                    rel = qb - kb
                    dt_ = d0 if rel == 0 else dfull
                    bias_ap = None if rel == 0 else bcol[:, h * 16 + rel:h * 16 + rel + 1]
                    visit(h, qt, ksT[p0:p0 + 64, g, kb * 128:(kb + 1) * 128], 128, (emat[0:32, kb, :], selbT[0:32, g, :]), dt_, bias_ap,
                          vsx[:, kb, g, :], acc, kb == 0, kb == qb, ["ksT", "vsx"],
                          fin=((lambda h=h, acc=acc: finish(h, qt, acc, 1, False)) if kb == qb else None))
            for h in range(16):
                g, p0 = h // 4, (h % 2) * 64
                ab, ao = a_region()
                acc = (ab, ao, 65)
                for kb in range(qb - 4, qb + 1):
                    rel = qb - kb
                    dt_ = d0 if rel == 0 else (d4 if rel == 4 else dfull)
                    bias_ap = None if rel == 0 else bcol[:, h * 16 + rel:h * 16 + rel + 1]
                    visit(h, qt, kwT[p0:p0 + 64, g, kb * 128:(kb + 1) * 128], 128, None, dt_, bias_ap,
                          vwx[:, kb, g, :], acc, kb == qb - 4, kb == qb, ["kwT", "vwx"],
                          fin=((lambda h=h, acc=acc: finish(h, qt, acc, 2, False)) if kb == qb else None))
            flush()
            for kg in range(2):
                b = 7
                for j in range(4):
                    kc = kg * 4 + j
                    P.op("pe", lambda e, b=b, j=j, kc=kc: e.transpose(out=psum[:, b, j * 128:(j + 1) * 128], in_=ynsa[:, kc * 128:(kc + 1) * 128], identity=ident),
                         r=[("ynsa", 2 * kc), ("ynsa", 2 * kc + 1), "ident"], w=[("ps", b)])
                P.op("act", lambda e, b=b, kg=kg: e.activation(out=ybt[:, kg * 4:(kg + 1) * 4, :], in_=psum[:, b, :].rearrange("p (a c) -> p a c", a=4), func=AF.Copy),
                     r=[("ps", b)], w=["ybt"])
            P.dma(ysc[:, 1, :, qt * 128:(qt + 1) * 128], ybt, r=["ybt"], w=["ysc"], grp="ysc")
            if dbg == "nsa":
                P.dma(dbg_yb[:, :, qt * 128:(qt + 1) * 128], ybt, r=["ybt"], w=["dbg_yb"], grp="dbgyb")
        P.barrier()

    GN_EPS = 64e-5

    def rwkv_phase():
        A.off = phase_off
        MA = Arena(mix_raw, 8192)
        twT = MA.alloc([128, L], BF16)
        alT = MA.alloc([128, L], BF16)
        sgT = MA.alloc([128, 2, NO], BF16)
        M1 = MA.alloc([128, 256], F32)
        M2 = MA.alloc([128, 256], F32)
        M3 = MA.alloc([128, 128], F32)
        identb = MA.alloc([128, 128], BF16)
        bones = MA.alloc([128, 128], F32)
        resetm = MA.alloc([128, 512], F32)
        ptab = MA.alloc([128, 8, 8], F32)
        omt = MA.alloc([128, 8, 8], F32)
        lmu = MA.alloc([128, 4], F32)
        lomm = MA.alloc([128, 4], F32)
        lnw = MA.alloc([128, 128], F32)
        lnb = MA.alloc([128, 128], F32)
        ybuf = MA.alloc([128, NO], BF16)
        wl32 = MA.alloc([128, 4, 128], F32)
        wlb = MA.alloc([128, 4, 128], BF16)
        carry = MA.alloc([128, 8], F32)
        st = MA.alloc([128, 8], F32)
        eC = MA.alloc([128, 8], F32)
        AG = MA.alloc([128, 2, 128], BF16)
        AU = MA.alloc([128, 2, 128], BF16)
        RbarT = MA.alloc([128, 128], BF16)
        Phi = MA.alloc([128, 128], F32)
        DeltaT = MA.alloc([128, 128], F32)
        Sst = MA.alloc([128, 128], F32)
        Sb = MA.alloc([128, 128], BF16)
        ysb = MA.alloc([128, 128], F32)
        ysq = MA.alloc([128, 128], F32)
        yn = MA.alloc([128, 128], F32)
        ytr = MA.alloc([128, 128], F32)
        t64 = MA.alloc([128, 64], F32)
        TT = [MA.alloc([128, 128], BF16) for _ in range(4)]
        rS = A.alloc([128, L], F32)
        kS = A.alloc([128, L], F32)
        vS = A.alloc([128, L], F32)
        raw = A.alloc([128, 514], F32)
        bon = raw[:, 0:512]
        lw = A.alloc([128, 512], F32)
        at = A.alloc([128, 512], F32)
        kk = A.alloc([128, 512], F32)
        k2 = A.alloc([128, 512], F32)
        cl = A.alloc([128, 512], F32)
        e1 = A.alloc([128, 512], F32)
        e2 = A.alloc([128, 512], F32)
        gq = A.alloc([128, 512], F32)
        ARblk = A.alloc([128, 8, 2, 128], BF16)
        Bblk = A.alloc([128, 8, 128], BF16)
        Kblk = A.alloc([128, 8, 128], BF16)
        B2blk = A.alloc([128, 8, 128], BF16)
        K2blk = A.alloc([128, 8, 128], BF16)
        Vblk = A.alloc([128, 8, 128], BF16)
        tm = [A.alloc([128, 4, 128], BF16) for _ in range(4)]
        LM1 = [A.alloc([128, 128], BF16) for _ in range(4)]
        LM2 = [A.alloc([128, 256], BF16) for _ in range(4)]
        W3 = [[A.alloc([128, 3, 128], BF16) for _ in range(2)] for _ in range(4)]

        for (dst, src, nm) in [(M1, r_m1, "M1"), (M2, r_m2, "M2"), (M3, r_m3, "M3"), (identb, r_identb, "identb"), (bones, r_bones, "bones"),
                               (resetm, r_resetm, "resetm"), (ptab, r_ptab, "ptab"), (lmu, r_lmu, "lmu")]:
            P.dma(dst, src, w=[nm], grp="r_" + nm)
        P.op("dve", lambda e: e.tensor_scalar(out=omt, in0=ptab, scalar1=-1.0, scalar2=1.0, op0=ALU.mult, op1=ALU.add), r=["ptab"], w=["omt"])
        P.op("dve", lambda e: e.tensor_scalar(out=lomm, in0=lmu, scalar1=-1.0, scalar2=1.0, op0=ALU.mult, op1=ALU.add), r=["lmu"], w=["lomm"])
        for blk_ in (ARblk, Bblk, Kblk, B2blk, K2blk, Vblk):
            P.op("pool", lambda e, blk_=blk_: e.memset(blk_, 0.0), w=["blk"])

        def shift_evac(b, t0, n, nrow, mu_col, omm_col, ci, dst):
            if t0 == 0:
                P.op("dve", lambda e: e.memset(raw[0:nrow, 0:1], 0.0), w=["raw"])
            else:
                P.op("dve", lambda e: e.tensor_copy(out=raw[0:nrow, 0:1], in_=carry[0:nrow, ci:ci + 1]), r=[("carry", ci)], w=["raw"])
            P.op("act", lambda e: e.activation(out=raw[0:nrow, 1:n + 1], in_=psum[0:nrow, b, 0:n], func=AF.Copy), r=[("ps", b)], w=["raw"])
            P.op("dve", lambda e: e.tensor_scalar(out=dst, in0=raw[0:nrow, 1:n + 1], scalar1=omm_col[0:nrow], scalar2=None, op0=ALU.mult),
                 r=["raw", "omt", "lomm"], w=["shiftdst"])
            P.op("dve", lambda e: e.scalar_tensor_tensor(out=dst, in0=raw[0:nrow, 0:n], scalar=mu_col[0:nrow], in1=dst,
                                                         op0=ALU.mult, op1=ALU.add), r=["raw", "shiftdst", "ptab", "lmu"], w=["shiftdst"])
            P.op("dve", lambda e: e.tensor_copy(out=carry[0:nrow, ci:ci + 1], in_=raw[0:nrow, n:n + 1]), r=["raw"], w=[("carry", ci)])

        for li, (c0, nrow) in enumerate([(3072, 96), (3168, 96), (3264, 128), (3392, 128)]):
            def ev_l(b, t0, n, li=li, nrow=nrow):
                shift_evac(b, t0, n, nrow, lmu[:, li:li + 1], lomm[:, li:li + 1], 3, e1[0:nrow, 0:n])
                if li == 0:
                    P.op("act", lambda e: e.activation(out=twT[0:nrow, t0:t0 + n], in_=e1[0:nrow, 0:n], func=AF.Tanh), r=["shiftdst"], w=["twT", "shiftdst"])
                elif li == 1:
                    P.op("act", lambda e: e.activation(out=alT[0:nrow, t0:t0 + n], in_=e1[0:nrow, 0:n], func=AF.Copy), r=["shiftdst"], w=["alT", "shiftdst"])
                elif t0 >= NO:
                    P.op("act", lambda e: e.activation(out=sgT[:, li - 2, t0 - NO:t0 - NO + n], in_=e1[0:nrow, 0:n], func=AF.Sigmoid),
                         r=["shiftdst"], w=["sgT", "shiftdst"])
            fm_linear(w_in, 0, c0, nrow, xn_all, ["xnT"], KC, TOK4, ev_l)

        def halves(fn):
            fn(0, 0)
            fn(64, 64)

        def do_pair(hp):
            pc = lambda j: ptab[:, hp, j:j + 1]
            oc = lambda j: omt[:, hp, j:j + 1]
            P.dma(wl32[0:96, 0, :], r_wup[:, hp * 128:(hp + 1) * 128], w=["wl32"], grp="r_wl")
            P.dma(wl32[0:96, 1, :], r_aup[:, hp * 128:(hp + 1) * 128], w=["wl32"], grp="r_wl")
            P.dma(wl32[:, 2:4, :], r_gup[:, hp * 128:(hp + 1) * 128].rearrange("(k p) c -> p k c", p=128), w=["wl32"], grp="r_wl")
            P.op("pool", lambda e: e.tensor_copy(out=wlb[0:96, 0:2, :], in_=wl32[0:96, 0:2, :]), r=["wl32"], w=["wlb"])
            P.op("pool", lambda e: e.tensor_copy(out=wlb[:, 2:4, :], in_=wl32[:, 2:4, :]), r=["wl32"], w=["wlb"])
            P.dma(lnw, r_lnw[hp], w=["lnw"], grp="r_lnw")
            P.dma(lnb, r_lnb[hp], w=["lnb"], grp="r_lnb")
            for qi, (c0, dstS) in enumerate([(0, rS), (1024, kS), (2048, vS)]):
                def ev_s(b, t0, n, qi=qi, dstS=dstS):
                    shift_evac(b, t0, n, 128, pc(qi), oc(qi), qi, dstS[:, t0:t0 + n])
                fm_linear(w_in, 0, c0 + hp * 128, 128, xn_all, ["xnT"], KC, TOK4, ev_s)
            P.op("pool", lambda e: e.memset(Sst, 0.0), w=["S"])
            P.op("pool", lambda e: e.memset(Sb, 0.0), w=["Sb"])
            def do_quarter(qtr):
                T0 = qtr * 512
                own = qtr >= 2
                sl = slice(T0, T0 + 512)
                b = P.bank()
                P.op("pe", lambda e, b=b: e.matmul(psum[:, b, :], lhsT=wlb[0:96, 0, :], rhs=twT[0:96, sl], start=True, stop=True), r=["wlb", "twT"], w=[("ps", b)])
                P.op("act", lambda e, b=b: e.activation(out=lw, in_=psum[:, b, :], func=AF.Sigmoid, bias=pc(3)), r=[("ps", b), "ptab"], w=["lw"])
                P.op("dve", lambda e: e.tensor_scalar(out=lw, in0=lw, scalar1=-0.6065306597126334, scalar2=None, op0=ALU.mult), r=["lw"], w=["lw"])
                b = P.bank()
                P.op("pe", lambda e, b=b: e.matmul(psum[:, b, :], lhsT=wlb[0:96, 1, :], rhs=alT[0:96, sl], start=True, stop=True), r=["wlb", "alT"], w=[("ps", b)])
                P.op("act", lambda e, b=b: e.activation(out=at, in_=psum[:, b, :], func=AF.Sigmoid, bias=pc(4)), r=[("ps", b), "ptab"], w=["at"])
                P.op("dve", lambda e: e.tensor_scalar(out=e1, in0=kS[:, sl], scalar1=pc(5), scalar2=None, op0=ALU.mult), r=["shiftdst", "ptab"], w=["e1"])
                P.op("act", lambda e: e.activation(out=e2, in_=e1, func=AF.Square), r=["e1"], w=["e2"])
                b = P.bank()
                P.op("pe", lambda e, b=b: e.matmul(psum[:, b, :], lhsT=bones, rhs=e2, start=True, stop=True), r=["bones", "e2"], w=[("ps", b)])
                P.op("dve", lambda e, b=b: e.tensor_scalar(out=e2, in0=psum[:, b, :], scalar1=1e-24, scalar2=None, op0=ALU.max), r=[("ps", b)], w=["e2"])
                P.op("act", lambda e: e.activation(out=e2, in_=e2, func=AF.Sqrt), r=["e2"], w=["e2"])
                P.op("dve", lambda e: e.reciprocal(out=e2, in_=e2), r=["e2"], w=["e2"])
                P.op("dve", lambda e: e.tensor_tensor(out=kk, in0=e1, in1=e2, op=ALU.mult), r=["e1", "e2"], w=["kk"])
                P.op("dve", lambda e: e.tensor_scalar(out=e1, in0=at, scalar1=pc(6), scalar2=oc(6), op0=ALU.mult, op1=ALU.add), r=["at", "ptab", "omt"], w=["e1"])
                P.op("dve", lambda e: e.tensor_tensor(out=k2, in0=kS[:, sl], in1=e1, op=ALU.mult), r=["e1", "shiftdst"], w=["k2"])
                if own:
                    P.op("dve", lambda e: e.tensor_tensor(out=e1, in0=rS[:, sl], in1=k2, op=ALU.mult), r=["k2", "shiftdst"], w=["e1"])
                    P.op("dve", lambda e: e.tensor_scalar(out=e1, in0=e1, scalar1=pc(7), scalar2=None, op0=ALU.mult), r=["e1", "ptab"], w=["e1"])
                    b = P.bank()
                    P.op("pe", lambda e, b=b: e.matmul(psum[:, b, :], lhsT=bones, rhs=e1, start=True, stop=True), r=["bones", "e1"], w=[("ps", b)])
                    P.op("dve", lambda e, b=b: e.tensor_tensor(out=bon, in0=psum[:, b, :], in1=vS[:, sl], op=ALU.mult), r=[("ps", b), "shiftdst"], w=["raw"])
                    b = P.bank()
                    for kt in range(2):
                        P.op("pe", lambda e, b=b, kt=kt: e.matmul(psum[:, b, :], lhsT=wlb[:, 2 + kt, :], rhs=sgT[:, kt, T0 - NO:T0 - NO + 512],
                                                                 start=(kt == 0), stop=(kt == 1)), r=["wlb", "sgT"], w=[("ps", b)])
                    P.op("act", lambda e, b=b: e.activation(out=gq, in_=psum[:, b, :], func=AF.Copy), r=[("ps", b)], w=["gq"])
                P.op("dve", lambda e: e.tensor_tensor_scan(out=cl, data0=resetm, data1=lw, initial=0.0, op0=ALU.mult, op1=ALU.add), r=["resetm", "lw"], w=["cl"])
                cl3 = cl.rearrange("p (c t) -> p c t", t=64)
                P.op("act", lambda e: e.activation(out=eC, in_=cl3[:, :, 63], func=AF.Exp), r=["cl"], w=["eC"])

                def v3(ap, p0):
                    return ap[p0:p0 + 64, :].rearrange("p (c t) -> p c t", t=64)
                P.op("act", lambda e: e.activation(out=e1, in_=cl, func=AF.Exp), r=["cl"], w=["e1"])
                halves(lambda p0, c0: P.op("dve", lambda e: e.tensor_tensor(out=ARblk[p0:p0 + 64, :, 1, c0:c0 + 64], in0=v3(rS[:, sl], p0), in1=v3(e1, p0), op=ALU.mult),
                                           r=["e1", "shiftdst"], w=["blk"]))
                P.op("act", lambda e: e.activation(out=e1, in_=cl, func=AF.Exp, scale=-1.0), r=["cl", "blk"], w=["e1"])
                P.op("dve", lambda e: e.tensor_tensor(out=e2, in0=kk, in1=at, op=ALU.mult), r=["kk", "at"], w=["e2"])
                halves(lambda p0, c0: P.op("dve", lambda e: e.tensor_tensor(out=Bblk[p0:p0 + 64, :, c0:c0 + 64], in0=v3(e2, p0), in1=v3(e1, p0), op=ALU.mult),
                                           r=["e1", "e2"], w=["blk"]))
                halves(lambda p0, c0: P.op("dve", lambda e: e.tensor_tensor(out=Kblk[p0:p0 + 64, :, c0:c0 + 64], in0=v3(k2, p0), in1=v3(e1, p0), op=ALU.mult),
                                           r=["e1", "k2"], w=["blk"]))
                P.op("dve", lambda e: e.tensor_tensor(out=e1, in0=cl, in1=lw, op=ALU.subtract), r=["cl", "lw", "blk"], w=["e1"])
                P.op("act", lambda e: e.activation(out=e1, in_=e1, func=AF.Exp), r=["e1"], w=["e1"])
                halves(lambda p0, c0: P.op("dve", lambda e: e.tensor_tensor(out=ARblk[p0:p0 + 64, :, 0, c0:c0 + 64], in0=v3(kk, p0), in1=v3(e1, p0), op=ALU.mult),
                                           r=["e1", "kk"], w=["blk"]))
                P.op("dve", lambda e: e.tensor_tensor(out=e1.rearrange("p (c t) -> p c t", t=64), in0=cl3[:, :, 63:64].broadcast_to([128, 8, 64]), in1=cl3,
                                                      op=ALU.subtract), r=["cl", "blk"], w=["e1"])
                P.op("act", lambda e: e.activation(out=e1, in_=e1, func=AF.Exp), r=["e1"], w=["e1"])
                halves(lambda p0, c0: P.op("dve", lambda e: e.tensor_tensor(out=B2blk[p0:p0 + 64, :, c0:c0 + 64], in0=v3(e2, p0), in1=v3(e1, p0), op=ALU.mult),
                                           r=["e1", "e2"], w=["blk"]))
                halves(lambda p0, c0: P.op("dve", lambda e: e.tensor_tensor(out=K2blk[p0:p0 + 64, :, c0:c0 + 64], in0=v3(k2, p0), in1=v3(e1, p0), op=ALU.mult),
                                           r=["e1", "k2"], w=["blk"]))
                halves(lambda p0, c0: P.op("pool", lambda e: e.tensor_copy(out=Vblk[p0:p0 + 64, :, c0:c0 + 64], in_=v3(vS[:, sl], p0)), r=["shiftdst"], w=["blk"]))

                for g4 in range(2):
                    for j in range(4):
                        cq = g4 * 4 + j
                        b = P.bank()
                        pb = psum[:, b, 0:256].bitcast(BF16)
                        for i, src in enumerate([ARblk[:, cq, 0, :], B2blk[:, cq, :], K2blk[:, cq, :], Vblk[:, cq, :]]):
                            P.op("pe", lambda e, i=i, src=src, pb=pb: e.transpose(out=pb[:, i * 128:(i + 1) * 128], in_=src, identity=identb),
                                 r=["blk", "identb"], w=[("ps", b)])
                        P.op("act", lambda e, j=j, pb=pb: e.activation(out=tm[j].rearrange("p a b -> p (a b)"), in_=pb, func=AF.Copy), r=[("ps", b)], w=[("tm", j)])
                        AR2 = ARblk[:, cq, :, :].rearrange("p a b -> p (a b)")
                        b = P.bank()
                        P.op("pe", lambda e, b=b, cq=cq, AR2=AR2: e.matmul(psum[:, b, 0:256], lhsT=Bblk[:, cq, :], rhs=AR2, start=True, stop=True), r=["blk"], w=[("ps", b)])
                        P.op("dve", lambda e, b=b, j=j: e.tensor_tensor(out=W3[j][0][:, 1, :], in0=psum[:, b, 0:128], in1=M1[:, 0:128], op=ALU.mult),
                             r=[("ps", b), "M1"], w=[("W3", j, 0)])
                        P.op("dve", lambda e, b=b, j=j: e.tensor_tensor(out=LM1[j], in0=psum[:, b, 128:256], in1=M1[:, 128:256], op=ALU.mult),
                             r=[("ps", b), "M1"], w=[("LM1", j)])
                        b = P.bank()
                        P.op("pe", lambda e, b=b, cq=cq, AR2=AR2: e.matmul(psum[:, b, 0:256], lhsT=Kblk[:, cq, :], rhs=AR2, start=True, stop=True), r=["blk"], w=[("ps", b)])
                        P.op("dve", lambda e, b=b, j=j: e.tensor_tensor(out=LM2[j], in0=psum[:, b, 0:256], in1=M2, op=ALU.mult), r=[("ps", b), "M2"], w=[("LM2", j)])
                        b = P.bank()
                        P.op("pe", lambda e, b=b, cq=cq: e.matmul(psum[:, b, 0:128], lhsT=ARblk[:, cq, 0, :], rhs=Bblk[:, cq, :], start=True, stop=True), r=["blk"], w=[("ps", b)])
                        P.op("dve", lambda e, b=b, j=j: e.tensor_tensor(out=W3[j][0][:, 2, :], in0=psum[:, b, 0:128], in1=M3, op=ALU.mult), r=[("ps", b), "M3"], w=[("W3", j, 0)])
                        P.op("pool", lambda e, j=j: e.tensor_copy(out=W3[j][0][:, 0, :], in_=identb), r=["identb"], w=[("W3", j, 0)])
                    for lvl in range(6):
                        last = lvl == 5
                        for j in range(4):
                            cur, nxt = W3[j][lvl % 2], W3[j][(lvl + 1) % 2]
                            b = P.bank()
                            nn = 128 if last else 256
                            P.op("pe", lambda e, b=b, cur=cur, nn=nn: e.matmul(psum[:, b, 0:nn], lhsT=cur[:, 2, :], rhs=cur[:, 0:nn // 128, :].rearrange("p a b -> p (a b)"),
                                                                              start=True, stop=True), r=[("W3", j, lvl % 2)], w=[("ps", b)])
                            if not last:
                                P.op("pe", lambda e, b=b, cur=cur: e.matmul(psum[:, b, 256:384], lhsT=cur[:, 1, :], rhs=cur[:, 2, :], start=True, stop=True),
                                     r=[("W3", j, lvl % 2)], w=[("ps", b)])
                            pdst = TT[j] if last else nxt[:, 0, :]
                            P.op("dve", lambda e, b=b, cur=cur, pdst=pdst: e.tensor_tensor(out=pdst, in0=psum[:, b, 0:128], in1=cur[:, 0, :], op=ALU.add),
                                 r=[("ps", b), ("W3", j, lvl % 2)], w=[("TT", j) if last else ("W3", j, (lvl + 1) % 2)])
                            if not last:
                                P.op("act", lambda e, b=b, nxt=nxt: e.activation(out=nxt[:, 1:3, :].rearrange("p a b -> p (a b)"), in_=psum[:, b, 128:384], func=AF.Copy),
                                     r=[("ps", b)], w=[("W3", j, (lvl + 1) % 2)])
                    for j in range(4):
                        cq = g4 * 4 + j
                        cg = qtr * 8 + cq
                        b = P.bank()
                        P.op("pe", lambda e, b=b, j=j: e.matmul(psum[:, b, 0:128], lhsT=LM2[j][:, 0:128], rhs=tm[j][:, 3, :], start=True, stop=True),
                             r=[("LM2", j), ("tm", j)], w=[("ps", b)])
                        P.op("act", lambda e, b=b: e.activation(out=AG[:, 1, :], in_=psum[:, b, 0:128], func=AF.Copy, scale=-1.0), r=[("ps", b)], w=["AG"])
                        P.op("pool", lambda e, j=j: e.tensor_copy(out=AG[:, 0, :], in_=tm[j][:, 0, :]), r=[("tm", j)], w=["AG"])
                        b = P.bank()
                        P.op("pe", lambda e, b=b, j=j: e.matmul(psum[:, b, 0:256], lhsT=TT[j], rhs=AG.rearrange("p a b -> p (a b)"), start=True, stop=True),
                             r=[("TT", j), "AG"], w=[("ps", b)])
                        P.op("act", lambda e, b=b: e.activation(out=AU.rearrange("p a b -> p (a b)"), in_=psum[:, b, 0:256], func=AF.Copy), r=[("ps", b)], w=["AU"])
                        if cg >= 16:
                            b = P.bank()
                            P.op("pe", lambda e, b=b, j=j: e.matmul(psum[:, b, 0:128], lhsT=AU[:, 0, :], rhs=LM1[j], start=True, stop=True),
                                 r=["AU", ("LM1", j)], w=[("ps", b)])
                            P.op("dve", lambda e, b=b, cq=cq: e.tensor_tensor(out=RbarT, in0=ARblk[:, cq, 1, :], in1=psum[:, b, 0:128], op=ALU.subtract),
                                 r=[("ps", b), "blk"], w=["RbarT"])
                        b = P.bank()
                        P.op("pe", lambda e, b=b, j=j: e.matmul(psum[:, b, 0:128], lhsT=AU[:, 0, :], rhs=tm[j][:, 1, :], start=True, stop=True),
                             r=["AU", ("tm", j)], w=[("ps", b)])
                        P.op("dve", lambda e, b=b, cq=cq: e.scalar_tensor_tensor(out=Phi, in0=ident, scalar=eC[:, cq:cq + 1], in1=psum[:, b, 0:128],
                                                                                op0=ALU.mult, op1=ALU.subtract), r=[("ps", b), "eC", "ident"], w=["Phi"])
                        b = P.bank()
                        P.op("pe", lambda e, b=b, j=j: e.matmul(psum[:, b, 0:128], lhsT=tm[j][:, 1, :], rhs=AU[:, 1, :], start=True, stop=False),
                             r=["AU", ("tm", j)], w=[("ps", b)])
                        P.op("pe", lambda e, b=b, j=j: e.matmul(psum[:, b, 0:128], lhsT=tm[j][:, 2, :], rhs=tm[j][:, 3, :], start=False, stop=True),
                             r=[("tm", j)], w=[("ps", b)])
                        P.op("act", lambda e, b=b: e.activation(out=DeltaT, in_=psum[:, b, 0:128], func=AF.Copy), r=[("ps", b)], w=["DeltaT"])
                        if cg >= 16:
                            to = cg * 64 - NO
                            b = P.bank()
                            P.op("pe", lambda e, b=b, j=j: e.matmul(psum[:, b, 0:128], lhsT=LM1[j], rhs=AU[:, 1, :], start=True, stop=False),
                                 r=["AU", ("LM1", j)], w=[("ps", b)])
                            P.op("pe", lambda e, b=b, j=j: e.matmul(psum[:, b, 0:128], lhsT=LM2[j][:, 128:256], rhs=tm[j][:, 3, :], start=False, stop=False),
                                 r=[("LM2", j), ("tm", j)], w=[("ps", b)])
                            P.op("pe", lambda e, b=b: e.matmul(psum[:, b, 0:128], lhsT=RbarT, rhs=Sb, start=False, stop=True), r=["RbarT", "Sb"], w=[("ps", b)])
                            P.op("act", lambda e, b=b: e.activation(out=ysb, in_=psum[:, b, 0:128], func=AF.Copy), r=[("ps", b)], w=["ysb"])
                            P.op("dve", lambda e: e.tensor_reduce(out=st[:, 0:1], in_=ysb, axis=AX.X, op=ALU.add), r=["ysb"], w=["st"])
                            P.op("act", lambda e: e.activation(out=ysq, in_=ysb, func=AF.Square), r=["ysb"], w=["ysq"])
                            P.op("dve", lambda e: e.tensor_reduce(out=st[:, 1:2], in_=ysq, axis=AX.X, op=ALU.add), r=["ysq"], w=["st"])
                            P.op("dve", lambda e: e.tensor_scalar(out=st[:, 2:3], in0=st[:, 0:1], scalar1=1.0 / 64, scalar2=None, op0=ALU.mult), r=["st"], w=["st"])
                            P.op("dve", lambda e: e.tensor_tensor(out=st[:, 3:4], in0=st[:, 2:3], in1=st[:, 2:3], op=ALU.mult), r=["st"], w=["st"])
                            P.op("dve", lambda e: e.scalar_tensor_tensor(out=st[:, 4:5], in0=st[:, 1:2], scalar=1.0 / 64, in1=st[:, 3:4], op0=ALU.mult, op1=ALU.subtract),
                                 r=["st"], w=["st"])
                            P.op("dve", lambda e: e.tensor_scalar(out=st[:, 4:5], in0=st[:, 4:5], scalar1=GN_EPS, scalar2=None, op0=ALU.add), r=["st"], w=["st"])
                            P.op("act", lambda e: e.activation(out=st[:, 4:5], in_=st[:, 4:5], func=AF.Sqrt), r=["st"], w=["st"])
                            P.op("dve", lambda e: e.reciprocal(out=st[:, 5:6], in_=st[:, 4:5]), r=["st"], w=["st"])
                            P.op("dve", lambda e: e.tensor_scalar(out=yn, in0=ysb, scalar1=st[:, 2:3], scalar2=st[:, 5:6], op0=ALU.subtract, op1=ALU.mult),
                                 r=["ysb", "st"], w=["yn"])
                            P.op("dve", lambda e: e.tensor_tensor(out=yn, in0=yn, in1=lnw, op=ALU.mult), r=["yn", "lnw"], w=["yn"])
                            P.op("dve", lambda e: e.tensor_tensor(out=yn, in0=yn, in1=lnb, op=ALU.add), r=["yn", "lnb"], w=["yn"])
                            b = P.bank()
                            P.op("pe", lambda e, b=b: e.transpose(out=psum[:, b, 0:128], in_=yn, identity=ident), r=["yn", "ident"], w=[("ps", b)])
                            P.op("act", lambda e, b=b: e.activation(out=ytr, in_=psum[:, b, 0:128], func=AF.Copy), r=[("ps", b)], w=["ytr"])
                            P.op("dve", lambda e: e.tensor_tensor(out=t64, in0=ytr[:, 0:64], in1=ytr[:, 64:128], op=ALU.add), r=["ytr"], w=["t64"])
                            P.op("dve", lambda e, cq=cq: e.tensor_tensor(out=t64, in0=t64, in1=bon[:, cq * 64:(cq + 1) * 64], op=ALU.add), r=["t64", "raw"], w=["t64"])
                            P.op("dve", lambda e, cq=cq, to=to: e.tensor_tensor(out=ybuf[:, to:to + 64], in0=t64, in1=gq[:, cq * 64:(cq + 1) * 64], op=ALU.mult),
                                 r=["t64", "gq"], w=["ybuf"])
                        b = P.bank()
                        P.op("pe", lambda e, b=b: e.matmul(psum[:, b, 0:128], lhsT=Phi, rhs=Sst, start=True, stop=True), r=["Phi", "S"], w=[("ps", b)])
                        P.op("dve", lambda e, b=b: e.tensor_tensor(out=Sst, in0=psum[:, b, 0:128], in1=DeltaT, op=ALU.add), r=[("ps", b), "DeltaT"], w=["S"])
                        P.op("act", lambda e: e.activation(out=Sb, in_=Sst, func=AF.Copy), r=["S"], w=["Sb"])
            for qtr in range(4):
                do_quarter(qtr)
            P.dma(ysc[:, 0, hp, :], ybuf, r=["ybuf"], w=["ysc"], grp="ysc")
            if dbg == "rwkv":
                P.dma(dbg_ya[:, hp, :], ybuf, r=["ybuf"], w=["dbg_ya"], grp="dbgya")
        for hp in range(8):
            do_pair(hp)
        P.barrier()

    if dbg not in ("tail", "nsa"):
        rwkv_phase()
    if dbg == "rwkv":
        P.op("sp", None, r=["dbg_ya"])
        P.emit(stack)
        stack.close()
        return nc
    if dbg not in ("tail", "rwkv"):
        nsa_phase()
    if dbg == "nsa":
        P.op("sp", None, r=["dbg_yb"])
        P.emit(stack)
        stack.close()
        return nc

    A.off = phase_off
    yT = A.alloc([128, 2, 8, NO], BF16)
    for wh in range(2):
        for kc in range(8):
            P.dma(yT[:, wh, kc, :], (yfake_d if dbg == "tail" else ysc)[:, wh, kc, :], r=["ysc"], w=["yT"], grp="c2")

    TOK2 = [(0, 512), (512, 512)]
    A.mark()
    tga = A.alloc([128, NO], F32)
    tgb = A.alloc([128, NO], F32)
    tpa = A.alloc([128, NO], F32)
    GA0 = 3520 + 1024 + 6 * 256 + 48
    for dc in range(KC):
        def ev_sig(dst, key):
            def f(b, t0, n):
                P.op("act", lambda e: e.activation(out=dst[:, t0:t0 + n], in_=psum[:, b, 0:n], func=AF.Sigmoid),
                     r=[("ps", b)], w=[key])
            return f
        xn_own = lambda k, t0, n: xnT[:, k, NO + t0:NO + t0 + n]
        fm_linear(w_in, 0, GA0 + dc * 128, 128, xn_own, ["xnT"], KC, TOK2, ev_sig(tga, "tga"))
        fm_linear(w_in, 0, GA0 + D + dc * 128, 128, xn_own, ["xnT"], KC, TOK2, ev_sig(tgb, "tgb"))

        def ev_pa(b, t0, n):
            P.op("dve", lambda e: e.tensor_tensor(out=tpa[:, t0:t0 + n], in0=psum[:, b, 0:n], in1=tga[:, t0:t0 + n], op=ALU.mult),
                 r=[("ps", b), "tga"], w=["tpa"])
        fm_linear(w_out_rwkv, 0, dc * 128, 128, lambda k, t0, n: yT[:, 0, k, t0:t0 + n], ["yT"], 8, TOK2, ev_pa)

        def ev_pb(b, t0, n, dc=dc):
            P.op("dve", lambda e: e.tensor_tensor(out=tgb[:, t0:t0 + n], in0=psum[:, b, 0:n], in1=tgb[:, t0:t0 + n], op=ALU.mult),
                 r=[("ps", b), "tgb"], w=["tgb"])
            P.op("pool", lambda e: e.tensor_tensor(out=mixT[:, dc, t0:t0 + n], in0=tgb[:, t0:t0 + n], in1=tpa[:, t0:t0 + n], op=ALU.add),
                 r=["tgb", "tpa"], w=[("mixT", dc)])
        fm_linear(w_out_nsa, 0, dc * 128, 128, lambda k, t0, n: yT[:, 1, k, t0:t0 + n], ["yT"], 8, TOK2, ev_pb)
    P.barrier()
    A.release()
    mix_keys = [("mixT", dc) for dc in range(KC)]
    A.off = xn_off
    h = A.alloc([128, 8, D], F32)
    for tt in range(8):
        P.dma(h[:, tt, :], xs[NO + tt * 128:NO + (tt + 1) * 128, :], w=[("h", tt)], grp="h%d" % tt)
    for dc in range(KC):
        def ev_h(b, tg, nt, dc=dc):
            for j in range(nt):
                tt = tg + j
                P.op("dve", lambda e, tt=tt, j=j: e.tensor_tensor(out=h[:, tt, dc * 128:(dc + 1) * 128], in0=psum[:, b, j * 128:(j + 1) * 128],
                                                                  in1=h[:, tt, dc * 128:(dc + 1) * 128], op=ALU.add),
                     r=[("ps", b), ("h", tt)], w=[("h", tt)])
        tm_linear(w_o, 0, dc * 128, lambda k, tt: mixT[:, k, tt * 128:(tt + 1) * 128], mix_keys, KC, 8, ev_h)

    if dbg:
        dbg_mix = nc.dram_tensor("dbg_mix", [128, KC, NO], BF16, kind="ExternalOutput").ap()
        for kc in range(KC):
            P.dma(dbg_mix[:, kc, :], mixT[:, kc, :], r=mix_keys, w=[("dbg_mix", kc)], grp="g1")
        dbg_h = nc.dram_tensor("dbg_h", [128, 8, D], F32, kind="ExternalOutput").ap()
        for tt in range(8):
            P.dma(dbg_h[:, tt, :], h[:, tt, :], r=[("h", tt)], w=[("dbg_h", tt)], grp="g2")
    grow = A.alloc([128, D], F32)
    hnT = A.alloc([128, KC, NO], BF16)
    hn = A.alloc([128, D], F32)
    jraw = A.alloc([128, 1024], F32)
    junk2 = jraw.bitcast(BF16)
    ss2 = A.alloc([128, 16], F32)
    rs2 = A.alloc([128, 16], F32)
    P.dma(grow, gmlp_d, w=["grow"], grp="c3")

    def rms_tile(tt, col, src, gkey, dst_fn):
        P.op("act", lambda e: e.activation(out=hn, in_=src, func=AF.Square), r=[("h", tt)], w=["hn"])
        P.op("dve", lambda e: e.tensor_reduce(out=ss2[:, col:col + 1], in_=hn, axis=AX.X, op=ALU.add), r=["hn"], w=[("ss2", col)])
        P.op("dve", lambda e: e.tensor_scalar(out=rs2[:, col:col + 1], in0=ss2[:, col:col + 1], scalar1=1.0 / D, scalar2=EPS,
                                              op0=ALU.mult, op1=ALU.add), r=[("ss2", col)], w=[("rs2", col)])
        P.op("act", lambda e: e.activation(out=rs2[:, col:col + 1], in_=rs2[:, col:col + 1], func=AF.Sqrt), r=[("rs2", col)], w=[("rs2", col)])
        P.op("dve", lambda e: e.reciprocal(out=rs2[:, col:col + 1], in_=rs2[:, col:col + 1]), r=[("rs2", col)], w=[("rs2", col)])
        dst_fn()

    for tt in range(8):
        def mk(tt=tt):
            P.op("dve", lambda e: e.scalar_tensor_tensor(out=hn, in0=h[:, tt, :], scalar=rs2[:, tt:tt + 1], in1=grow,
                                                         op0=ALU.mult, op1=ALU.mult),
                 r=[("h", tt), ("rs2", tt), "grow"], w=["hn"])
            for kg in range(4):
                b = P.bank()
                for j in range(4):
                    kc = kg * 4 + j
                    P.op("pe", lambda e, b=b, j=j, kc=kc: e.transpose(out=psum[:, b, j * 128:(j + 1) * 128],
                                                                     in_=hn[:, kc * 128:(kc + 1) * 128], identity=ident),
                         r=["hn", "ident"], w=[("ps", b)])
                P.op("act", lambda e, b=b, kg=kg: e.activation(out=hnT[:, kg * 4:(kg + 1) * 4, tt * 128:(tt + 1) * 128],
                                                              in_=psum[:, b, :].rearrange("p (a c) -> p a c", a=4), func=AF.Copy),
                     r=[("ps", b)], w=[("hnT", tt)])
        rms_tile(tt, tt, h[:, tt, :], "grow", mk)
    hn_keys = [("hnT", tt) for tt in range(8)]

    P.barrier()
    aT = mixT
    for g in range(4):
        for fl in range(KC):
            def ev_a(b, t0, n, fl=fl):
                sl = (t0 // 512) % 2
                tmp = jraw[:, sl * 512:sl * 512 + n]
                P.op("act", lambda e: e.activation(out=tmp, in_=psum[:, b, 0:n], func=AF.Relu), r=[("ps", b)], w=[("jr", sl)])
                P.op("pool", lambda e: e.tensor_tensor(out=aT[:, fl, t0:t0 + n], in0=tmp, in1=tmp, op=ALU.mult),
                     r=[("jr", sl)], w=[("aT", fl)])
            fm_linear(w_up, 0, (g * KC + fl) * 128, 128, lambda k, t0, n: hnT[:, k, t0:t0 + n], hn_keys, KC, TOK2, ev_a)
        a_keys = [("aT", fl) for fl in range(KC)]
        for dc in range(KC):
            def ev_h2(b, tg, nt, dc=dc):
                for j in range(nt):
                    tt = tg + j
                    P.op("dve", lambda e, tt=tt, j=j: e.tensor_tensor(out=h[:, tt, dc * 128:(dc + 1) * 128], in0=psum[:, b, j * 128:(j + 1) * 128],
                                                                      in1=h[:, tt, dc * 128:(dc + 1) * 128], op=ALU.add),
                         r=[("ps", b), ("h", tt)], w=[("h", tt)])
            tm_linear(w_down, g * 2048, dc * 128, lambda k, tt: aT[:, k, tt * 128:(tt + 1) * 128], a_keys, KC, 8, ev_h2)

    P.dma(grow, gfin_d, r=[], w=["grow"], grp="c3")
    for tt in range(8):
        def mk(tt=tt):
            P.op("dve", lambda e: e.scalar_tensor_tensor(out=h[:, tt, :], in0=h[:, tt, :], scalar=rs2[:, 8 + tt:9 + tt], in1=grow,
                                                         op0=ALU.mult, op1=ALU.mult),
                 r=[("h", tt), ("rs2", 8 + tt), "grow"], w=[("h", tt)])
            P.dma(out_d[tt * 128:(tt + 1) * 128, :], h[:, tt, :], r=[("h", tt)], w=[("out", tt)], grp="o%d" % (tt % 2))
        rms_tile(tt, 8 + tt, h[:, tt, :], "grow", mk)
    P.op("sp", None, r=[("out", tt) for tt in range(8)])
    P.emit(stack)
    stack.close()
    return nc


def host_inputs(inputs, dbg=None):
    f = lambda a: np.ascontiguousarray(np.asarray(a, dtype=np.float32))
    x = f(inputs["x"])
    shared = {
        "w_in": f(inputs["w_in"][0]),
        "w_out_rwkv": f(inputs["w_out_rwkv"][0]),
        "w_out_nsa": f(inputs["w_out_nsa"][0]),
        "w_o": f(inputs["w_o"][0]),
        "mlp_w_up": f(inputs["mlp_w_up"][0]),
        "mlp_w_down": f(inputs["mlp_w_down"][0]),
        "ident": np.eye(128, dtype=np.float32),
        "gmixT": f(np.asarray(inputs["norm_mix"][0]).reshape(KC, 128).T),
        "gmlp_row": f(np.broadcast_to(np.asarray(inputs["norm_mlp"][0])[None, :], (128, D))),
        "gfin_row": f(np.broadcast_to(np.asarray(inputs["norm_final"])[None, :], (128, D))),
    }
    BIG = 30000.0
    kk_, qq_ = np.meshgrid(np.arange(128), np.arange(128), indexing="ij")
    dq = (qq_ - kk_).astype(np.float32)
    shared["c_dfull"] = f(dq)
    shared["c_d0"] = f(np.where(qq_ >= kk_, dq, BIG))
    shared["c_d4"] = f(np.where(qq_ < kk_, dq, BIG))
    slopes = 2.0 ** (-8.0 * np.arange(1, 17) / 16.0)
    bc = np.zeros((128, 256), np.float32)
    for h_ in range(16):
        for rel in range(16):
            bc[:, h_ * 16 + rel] = -slopes[h_] * 128.0 * rel
    shared["c_bcol"] = bc
    em = np.zeros((128, 16, 128), np.float32)
    for kb in range(16):
        for k_ in range(128):
            em[2 * kb + k_ // 64, kb, k_] = 1.0
    import ml_dtypes
    shared["c_emat"] = em.astype(ml_dtypes.bfloat16)
    for nm in ["cmp_w1_k", "cmp_w2_k", "cmp_w1_v", "cmp_w2_v"]:
        shared[nm] = f(inputs[nm][0])
    for nm, src in [("c_pekT", "cmp_pe_k"), ("c_pevT", "cmp_pe_v")]:
        pt = np.zeros((128, 32), np.float32)
        pt[0:64, :] = np.asarray(inputs[src][0]).T
        shared[nm] = pt
    hh_ = np.arange(128) // 64
    same = (hh_[:, None] == hh_[None, :])
    rr_, cc_ = np.meshgrid(np.arange(128) % 64, np.arange(128) % 64, indexing="ij")
    msu = (same & (rr_ < cc_)).astype(np.float32)
    mu_ = (same & (rr_ <= cc_)).astype(np.float32)
    msl = (same & (cc_ < rr_)).astype(np.float32)
    shared["r_m1"] = f(np.concatenate([-msu, mu_], axis=1))
    shared["r_m2"] = f(np.concatenate([msu, mu_], axis=1))
    shared["r_m3"] = f(-msl)
    shared["r_identb"] = np.eye(128, dtype=np.float32).astype(ml_dtypes.bfloat16)
    shared["r_bones"] = f(same.astype(np.float32))
    rm = np.ones((128, 512), np.float32)
    rm[:, ::64] = 0.0
    shared["r_resetm"] = rm
    mu_all = np.asarray(inputs["rwkv_mu"][0], np.float32)
    pt_ = np.zeros((128, 8, 8), np.float32)
    flat = lambda a: np.asarray(a, np.float32).reshape(-1)
    srcs = [mu_all[0:1024], mu_all[1024:2048], mu_all[2048:3072], flat(inputs["rwkv_w0"][0]), flat(inputs["rwkv_a0"][0]),
            flat(inputs["rwkv_k_k"][0]), flat(inputs["rwkv_k_a"][0]), flat(inputs["rwkv_r_k"][0])]
    for j_, a_ in enumerate(srcs):
        pt_[:, :, j_] = a_.reshape(8, 128).T
    shared["r_ptab"] = pt_
    lm = np.zeros((128, 4), np.float32)
    lm[:96, 0] = mu_all[3072:3168]
    lm[:96, 1] = mu_all[3168:3264]
    lm[:, 2] = mu_all[3264:3392]
    lm[:, 3] = mu_all[3392:3520]
    shared["r_lmu"] = lm
    for nm in ["rwkv_w_up", "rwkv_a_up", "rwkv_g_up"]:
        shared[nm] = f(inputs[nm][0])
    lw_ = flat(inputs["rwkv_lnx_w"][0]).reshape(8, 2, 64)
    lb_ = flat(inputs["rwkv_lnx_b"][0]).reshape(8, 2, 64)
    lnw_blk = np.zeros((8, 128, 128), np.float32)
    lnb_blk = np.zeros((8, 128, 128), np.float32)
    for hp_ in range(8):
        for h2 in range(2):
            lnw_blk[hp_, h2 * 64:(h2 + 1) * 64, h2 * 64:(h2 + 1) * 64] = lw_[hp_, h2][None, :]
            lnb_blk[hp_, h2 * 64:(h2 + 1) * 64, h2 * 64:(h2 + 1) * 64] = lb_[hp_, h2][None, :]
    shared["r_lnw"] = lnw_blk
    shared["r_lnb"] = lnb_blk
    n_ = np.arange(127)
    cs = n_[:, None] * 16
    ss = np.arange(32)[None, :] * 64
    overlap = np.clip(np.minimum(cs + 32, ss + 64) - np.maximum(cs, ss), 0, None) / 32.0
    percore = []
    for hf in range(2):
        off = 0 if hf == 1 else 16
        tokv = np.ones(2048, np.float32)
        if hf == 0:
            tokv[:1024] = 0.0
        vn = np.ones(127, np.float32)
        if hf == 0:
            vn[:64] = 0.0
        cc = np.zeros((128, 33), np.float32)
        cc[:127, 0] = vn
        cc[:127, 1:] = overlap * vn[:, None]
        dcm = np.full((128, 8, 128), BIG, np.float32)
        sk = np.zeros((128, 8, 32), np.float32)
        sb_ = np.zeros((128, 8, 32), np.float32)
        for qt in range(8):
            t = 1024 + qt * 128 + np.arange(128)
            dist = t[None, :] - (16 * n_[:, None] + 31)
            ok = (dist >= 0) & (vn[:, None] > 0)
            dcm[:127, qt, :] = np.where(ok, dist, BIG)
            cur = t // 64
            j = np.arange(32)[None, :]
            excl = (j > cur[:, None]) | (j < off)
            forced = (~excl) & ((j == off) | (j == cur[:, None]) | (j == cur[:, None] - 1))
            sk[:, qt, :] = np.where(excl | forced, 0.0, 1.0)
            sb_[:, qt, :] = np.where(excl, -1e30, np.where(forced, 1e30, 0.0))
        percore.append({"c_vtok": f(tokv.reshape(16, 128).T), "c_ccst": cc, "c_dc": dcm, "c_skeep": sk, "c_sbias": sb_})
    maps = []
    for c in range(8):
        b, hf = c // 2, c % 2
        if hf == 1:
            xs_ = x[b]
        else:
            xs_ = np.concatenate([np.zeros((NO, D), np.float32), x[b, :NO]], axis=0)
        m = dict(shared)
        m.update(percore[hf])
        m["xs"] = f(xs_)
        maps.append(m)
    return maps


_NC = {}


def kernel(**inputs):
    if "nc" not in _NC:
        _NC["nc"] = build()
    maps = host_inputs(inputs)
    res = run_bass_kernel_spmd(_NC["nc"], maps, core_ids=list(range(8)))
    out = np.zeros((4, 2048, D), np.float32)
    for c in range(8):
        b, hf = c // 2, c % 2
        out[b, hf * NO:(hf + 1) * NO] = res.results[c]["out"]
    return out
```

```python
import numpy as np
import concourse.bass as bass
import concourse.mybir as mybir
from concourse.bass_utils import run_bass_kernel_spmd

F32 = mybir.dt.float32
BF16 = mybir.dt.bfloat16
AF = mybir.ActivationFunctionType
ALU = mybir.AluOpType
AX = mybir.AxisListType

D = 2048
L = 2048
NO = 1024
KC = 16
DFF = 8192
IN_COLS = 10224
EPS = 1e-5


class Prog:
    def __init__(self, nc):
        self.nc = nc
        self.ops = []
        self.lastw = {}
        self.readers = {}
        self.bank_i = 0

    def _deps(self, idx, r, w):
        deps = set()
        for k in r:
            if k in self.lastw:
                deps.add(self.lastw[k])
        for k in w:
            if k in self.lastw:
                deps.add(self.lastw[k])
            deps.update(self.readers.get(k, ()))
        for k in r:
            self.readers.setdefault(k, []).append(idx)
        for k in w:
            self.lastw[k] = idx
            self.readers[k] = []
        deps.discard(idx)
        return deps

    def op(self, eng, fn, r=(), w=()):
        idx = len(self.ops)
        self.ops.append(dict(eng=eng, fn=fn, deps=self._deps(idx, r, w), dma=None, sig=False))

    def dma(self, out, in_, r=(), w=(), grp="d", q="sp"):
        idx = len(self.ops)
        self.ops.append(dict(eng=q, fn=(lambda e: e.dma_start(out=out, in_=in_)),
                             deps=self._deps(idx, r, w), dma=grp, sig=True))

    def barrier(self):
        last = {}
        for i, o in enumerate(self.ops):
            if o["fn"] is None:
                continue
            key = ("dma", o["dma"]) if o["dma"] is not None else ("eng", o["eng"])
            last[key] = i
        alld = set(last.values())
        for e in ["pe", "dve", "act", "pool", "sp"]:
            self.ops.append(dict(eng=e, fn=None, deps=set(alld), dma=None, sig=False))
        self.lastw = {}
        self.readers = {}

    def bank(self):
        b = self.bank_i
        self.bank_i = (b + 1) % 8
        return b

    def emit(self, stack):
        nc = self.nc
        ops = self.ops
        for o in ops:
            for p in o["deps"]:
                po = ops[p]
                if po["dma"] is not None or po["eng"] != o["eng"] or o["eng"] != "pe":
                    po["sig"] = True
        cnt = {}
        sems = {}
        for o in ops:
            if not o["sig"] or o["fn"] is None:
                continue
            key = ("dma", o["dma"]) if o["dma"] is not None else ("eng", o["eng"])
            if key not in sems:
                sems[key] = stack.enter_context(nc.semaphore("s_%s_%s" % key))
                cnt[key] = 0
            cnt[key] += 16 if o["dma"] is not None else 1
            o["sem"] = sems[key]
            o["val"] = cnt[key]
            o["skey"] = key
        block = stack.enter_context(nc.Block())

        def run(engname, e):
            waited = {}
            for o in ops:
                if o["eng"] != engname:
                    continue
                need = {}
                for p in o["deps"]:
                    po = ops[p]
                    if po["dma"] is None and po["eng"] == engname and engname == "pe":
                        continue
                    k = po["skey"]
                    if po["val"] > need.get(k, 0):
                        need[k] = po["val"]
                for k, v in need.items():
                    if v > waited.get(k, 0):
                        e.wait_ge(sems[k], v)
                        waited[k] = v
                if o["fn"] is None:
                    continue
                inst = o["fn"](e)
                if o["sig"]:
                    inst.then_inc(o["sem"], 16 if o["dma"] is not None else 1)

        @block.tensor
        def _(e):
            run("pe", e)

        @block.vector
        def _(e):
            run("dve", e)

        @block.scalar
        def _(e):
            run("act", e)

        @block.gpsimd
        def _(e):
            run("pool", e)

        @block.sync
        def _(e):
            run("sp", e)


class Arena:
    def __init__(self, t, nwords):
        self.t = t
        self.n = nwords
        self.off = 0
        self.marks = []

    def alloc(self, shape, dtype):
        free = int(np.prod(shape[1:]))
        words = free if dtype == F32 else (free + 1) // 2
        assert self.off + words <= self.n, ("arena overflow", self.off, words, self.n)
        v = self.t[:, self.off:self.off + words]
        self.off += words
        if dtype != F32:
            v = v.bitcast(dtype)
            if free % 2:
                v = v[:, 0:free]
        if len(shape) == 3:
            v = v.rearrange("p (a b) -> p a b", a=shape[1])
        elif len(shape) == 4:
            v = v.rearrange("p (a b c) -> p a b c", a=shape[1], b=shape[2])
        return v

    def mark(self):
        self.marks.append(self.off)

    def release(self):
        self.off = self.marks.pop()


def build(dbg=None):
    nc = bass.Bass("TRN2", target_bir_lowering=False)
    import contextlib
    stack = contextlib.ExitStack()

    def din(name, shape, dt=F32):
        if dbg == "A" and name not in ("xs", "ident", "gmixT"):
            return None
        if dbg in ("nsa", "rwkv") and name in ("w_out_rwkv", "w_out_nsa", "w_o", "mlp_w_up", "mlp_w_down", "gmlp_row", "gfin_row"):
            return None
        return nc.dram_tensor(name, list(shape), dt, kind="ExternalInput").ap()

    xs = din("xs", [L, D])
    w_in = din("w_in", [D, IN_COLS])
    w_out_rwkv = din("w_out_rwkv", [1024, D])
    w_out_nsa = din("w_out_nsa", [1024, D])
    w_o = din("w_o", [D, D])
    w_up = din("mlp_w_up", [D, DFF])
    w_down = din("mlp_w_down", [DFF, D])
    ident_d = din("ident", [128, 128])
    gmix_d = din("gmixT", [128, KC])
    gmlp_d = din("gmlp_row", [128, D])
    gfin_d = din("gfin_row", [128, D])
    c_d0 = din("c_d0", [128, 128]); c_dfull = din("c_dfull", [128, 128]); c_d4 = din("c_d4", [128, 128])
    c_dc = din("c_dc", [128, 8, 128]); c_bcol = din("c_bcol", [128, 256]); c_emat = din("c_emat", [128, 16, 128], BF16)
    c_skeep = din("c_skeep", [128, 8, 32]); c_sbias = din("c_sbias", [128, 8, 32]); c_vtok = din("c_vtok", [128, 16])
    c_ccst = din("c_ccst", [128, 33])
    c_w1k = din("cmp_w1_k", [2048, 256]); c_w2k = din("cmp_w2_k", [256, 64]); c_pek = din("c_pekT", [128, 32])
    c_w1v = din("cmp_w1_v", [2048, 256]); c_w2v = din("cmp_w2_v", [256, 64]); c_pev = din("c_pevT", [128, 32])
    r_m1 = din("r_m1", [128, 256]); r_m2 = din("r_m2", [128, 256]); r_m3 = din("r_m3", [128, 128])
    r_identb = din("r_identb", [128, 128], BF16); r_bones = din("r_bones", [128, 128]); r_resetm = din("r_resetm", [128, 512])
    r_ptab = din("r_ptab", [128, 8, 8]); r_lmu = din("r_lmu", [128, 4])
    r_wup = din("rwkv_w_up", [96, 1024]); r_aup = din("rwkv_a_up", [96, 1024]); r_gup = din("rwkv_g_up", [256, 1024])
    r_lnw = din("r_lnw", [8, 128, 128]); r_lnb = din("r_lnb", [8, 128, 128])
    yfake_d = din("yfake", [128, 2, 8, NO], BF16) if dbg == "tail" else None
    out_d = nc.dram_tensor("out", [NO, D], F32, kind="ExternalOutput").ap() if dbg not in ("A", "nsa", "rwkv") else None
    dbg_ya = nc.dram_tensor("dbg_ya", [128, 8, NO], BF16, kind="ExternalOutput").ap() if dbg == "rwkv" else None
    dbg_yb = nc.dram_tensor("dbg_yb", [128, 8, NO], BF16, kind="ExternalOutput").ap() if dbg == "nsa" else None

    NW = 48896
    arena_t = stack.enter_context(nc.sbuf_tensor("arena", [128, NW], F32))
    psum = stack.enter_context(nc.psum_tensor("ps", [128, 8, 512], F32))
    A = Arena(arena_t, NW)
    P = Prog(nc)

    ident = A.alloc([128, 128], F32)
    gmixT = A.alloc([128, KC], F32)
    P.dma(ident, ident_d, w=["ident"], grp="c0")
    P.dma(gmixT, gmix_d, w=["gmixT"], grp="c1")

    NST = 2
    wst = [A.alloc([128, KC, 128], F32) for _ in range(NST)]
    wbf = [A.alloc([128, KC, 128], BF16) for _ in range(NST)]
    wctr = [0]

    def wload(wap, r0, c0, ncols=128, nk=KC, cast_eng=None, segs=None):
        i = wctr[0]
        wctr[0] += 1
        s = i % NST
        if segs is None:
            segs = [(c0, ncols)]
        o0 = 0
        for (cc, nn) in segs:
            src = wap[r0:r0 + nk * 128, cc:cc + nn].rearrange("(k p) c -> p k c", p=128)
            P.dma(wst[s][:, 0:nk, o0:o0 + nn], src, w=[("wst", s)], grp="w%d" % s)
            o0 += nn
        ncols = o0
        eng = cast_eng or ("pool" if i % 2 == 0 else "act")
        o, i_ = wbf[s][:, 0:nk, 0:ncols], wst[s][:, 0:nk, 0:ncols]
        if eng == "act":
            P.op("act", lambda e, o=o, i_=i_: e.activation(out=o, in_=i_, func=AF.Copy), r=[("wst", s)], w=[("wbf", s)])
        else:
            P.op(eng, lambda e, o=o, i_=i_: e.tensor_copy(out=o, in_=i_), r=[("wst", s)], w=[("wbf", s)])
        return wbf[s], ("wbf", s)

    mix_raw = A.alloc([128, 8192], F32)
    mixT = mix_raw.bitcast(BF16).rearrange("p (a b) -> p a b", a=KC)
    xn_off = A.off
    xnT = A.alloc([128, KC, L], BF16)
    phase_off = A.off
    ysc = nc.dram_tensor("ysc", [128, 2, 8, NO], BF16, kind="Internal").ap()

    A.mark()
    xt = [A.alloc([128, D], F32) for _ in range(2)]
    junk = A.alloc([128, D], F32)
    ssq = A.alloc([128, 16], F32)
    rstd = A.alloc([128, 16], F32)
    for tt in range(16):
        s = tt % 2
        P.dma(xt[s], xs[tt * 128:(tt + 1) * 128, :], w=[("xt", s)], grp="x%d" % s)
        P.op("act", lambda e, s=s, tt=tt: e.activation(out=junk, in_=xt[s], func=AF.Square), r=[("xt", s)], w=["junk"])
        P.op("dve", lambda e, tt=tt: e.tensor_reduce(out=ssq[:, tt:tt + 1], in_=junk, axis=AX.X, op=ALU.add), r=["junk"], w=[("ssq", tt)])
        P.op("dve", lambda e, tt=tt: e.tensor_scalar(out=rstd[:, tt:tt + 1], in0=ssq[:, tt:tt + 1], scalar1=1.0 / D, scalar2=EPS,
                                                     op0=ALU.mult, op1=ALU.add), r=[("ssq", tt)], w=[("rstd", tt)])
        P.op("act", lambda e, tt=tt: e.activation(out=rstd[:, tt:tt + 1], in_=rstd[:, tt:tt + 1], func=AF.Sqrt), r=[("rstd", tt)], w=[("rstd", tt)])
        P.op("dve", lambda e, tt=tt: e.reciprocal(out=rstd[:, tt:tt + 1], in_=rstd[:, tt:tt + 1]), r=[("rstd", tt)], w=[("rstd", tt)])
        P.op("pool", lambda e, s=s, tt=tt: e.tensor_scalar(out=xt[s], in0=xt[s], scalar1=rstd[:, tt:tt + 1], scalar2=None,
                                                           op0=ALU.mult), r=[("xt", s), ("rstd", tt)], w=[("xt", s)])
        for kg in range(4):
            b = P.bank()
            for j in range(4):
                kc = kg * 4 + j
                P.op("pe", lambda e, s=s, b=b, j=j, kc=kc: e.transpose(out=psum[:, b, j * 128:(j + 1) * 128],
                                                                     in_=xt[s][:, kc * 128:(kc + 1) * 128], identity=ident),
                     r=[("xt", s), "ident"], w=[("ps", b)])
            P.op("dve", lambda e, b=b, kg=kg, tt=tt: e.tensor_tensor(
                out=xnT[:, kg * 4:(kg + 1) * 4, tt * 128:(tt + 1) * 128],
                in0=psum[:, b, :].rearrange("p (a c) -> p a c", a=4),
                in1=gmixT[:, kg * 4:(kg + 1) * 4].unsqueeze(2).broadcast_to([128, 4, 128]), op=ALU.mult),
                r=[("ps", b), "gmixT"], w=[("xnT", tt)])
    if dbg:
        dbg_xn = nc.dram_tensor("dbg_xn", [128, KC, L], BF16, kind="ExternalOutput").ap()
        for kc in range(KC):
            P.dma(dbg_xn[:, kc, :], xnT[:, kc, :], r=[("xnT", tt) for tt in range(16)], w=[("dbg_xn", kc)], grp="g0")
    if dbg == "A":
        P.op("sp", None, r=[("dbg_xn", kc) for kc in range(KC)])
        P.emit(stack)
        stack.close()
        return nc
    P.barrier()
    A.release()

    def fm_linear(wap, r0, c0, ncols, rhs_fn, rkeys, nk, toks, evac, segs=None):
        wt, wkey = wload(wap, r0, c0, ncols, nk, segs=segs)
        for (t0, n) in toks:
            b = P.bank()
            for k in range(nk):
                P.op("pe", lambda e, b=b, k=k, t0=t0, n=n: e.matmul(psum[0:ncols, b, 0:n], lhsT=wt[:, k, 0:ncols], rhs=rhs_fn(k, t0, n),
                                                                   start=(k == 0), stop=(k == nk - 1)),
                     r=[wkey] + list(rkeys), w=[("ps", b)])
            evac(b, t0, n)

    def tm_linear(wap, r0, c0, lhs_fn, lkeys, nk, ntt, evac, ncols=128):
        wt, wkey = wload(wap, r0, c0, ncols, nk)
        for tg in range(0, ntt, 4):
            b = P.bank()
            for j in range(4):
                tt = tg + j
                for k in range(nk):
                    P.op("pe", lambda e, b=b, k=k, tt=tt, j=j: e.matmul(psum[:, b, j * 128:j * 128 + ncols], lhsT=lhs_fn(k, tt), rhs=wt[:, k, 0:ncols],
                                                                       start=(k == 0), stop=(k == nk - 1)),
                         r=[wkey] + list(lkeys), w=[("ps", b)])
            evac(b, tg, 4)

    Q0, KCR, VCR, KS, VS, KW, VW, NGC = 3520, 4544, 4800, 5056, 5312, 5568, 5824, 6080
    SLOPES = [2.0 ** (-8.0 * (i + 1) / 16.0) for i in range(16)]
    BIG = 30000.0
    TOK4 = [(0, 512), (512, 512), (1024, 512), (1536, 512)]
    TOK2 = [(0, 512), (512, 512)]
    xn_all = lambda k, t0, n: xnT[:, k, t0:t0 + n]
    xn_own = lambda k, t0, n: xnT[:, k, NO + t0:NO + t0 + n]

    def nsa_phase():
        A.off = phase_off
        qT = mix_raw[:, 0:4096].bitcast(BF16).rearrange("p (a b) -> p a b", a=8)
        ksT = mix_raw[:, 4096:8192].bitcast(BF16).rearrange("p (a b) -> p a b", a=4)
        kwT = A.alloc([128, 4, L], BF16)
        vsx = A.alloc([128, 16, 4, 65], BF16)
        vwx = A.alloc([128, 16, 4, 65], BF16)
        d0 = A.alloc([128, 128], F32)
        dfull = A.alloc([128, 128], F32)
        d4 = A.alloc([128, 128], F32)
        dc = A.alloc([128, 8, 128], F32)
        bcol = A.alloc([128, 256], F32)
        emat = A.alloc([128, 16, 128], BF16)
        skeep = A.alloc([128, 8, 32], F32)
        sbias = A.alloc([128, 8, 32], F32)
        vtok = A.alloc([128, 16], F32)
        ccst = A.alloc([128, 33], F32)
        sg = A.alloc([128, 8, 48], F32)
        kcT = A.alloc([128, 4, 128], BF16)
        vcx = A.alloc([128, 4, 98], BF16)
        for (dst, src, nm) in [(d0, c_d0, "d0"), (dfull, c_dfull, "dfull"), (d4, c_d4, "d4"), (dc, c_dc, "dc"), (bcol, c_bcol, "bcol"),
                               (emat, c_emat, "emat"), (skeep, c_skeep, "skeep"), (sbias, c_sbias, "sbias"), (vtok, c_vtok, "vtok"),
                               (ccst, c_ccst, "ccst")]:
            P.dma(dst, src, w=[nm], grp="n_" + nm)

        for hp in range(8):
            def ev_q(b, t0, n, hp=hp):
                P.op("act", lambda e: e.activation(out=qT[:, hp, t0:t0 + n], in_=psum[:, b, 0:n], func=AF.Copy, scale=0.125),
                     r=[("ps", b)], w=["qT"])
            fm_linear(w_in, 0, Q0 + hp * 128, 128, xn_own, ["xnT"], KC, TOK2, ev_q)
        for (c0, dstT, nm) in [(KS, ksT, "ksT"), (KW, kwT, "kwT")]:
            for g in range(4):
                def ev_k(b, t0, n, g=g, dstT=dstT, nm=nm):
                    P.op("act", lambda e: e.activation(out=dstT[:, g, t0:t0 + n], in_=psum[:, b, 0:n], func=AF.Copy),
                         r=[("ps", b)], w=[nm])
                fm_linear(w_in, 0, 0, 128, xn_all, ["xnT"], KC, TOK4, ev_k, segs=[(c0 + g * 64, 64), (c0 + g * 64, 64)])
        for (c0, dstx, nm) in [(VS, vsx, "vsx"), (VW, vwx, "vwx")]:
            for half in range(2):
                def ev_v(b, tg, nt, half=half, dstx=dstx, nm=nm):
                    P.op("act", lambda e: e.activation(out=dstx[:, tg:tg + 4, 2 * half:2 * half + 2, 0:64],
                                                       in_=psum[:, b, :].rearrange("p (a g c) -> p a g c", a=4, g=2), func=AF.Copy),
                         r=[("ps", b)], w=[nm])
                tm_linear(w_in, 0, c0 + half * 128, lambda k, tt: xnT[:, k, tt * 128:(tt + 1) * 128], ["xnT"], KC, 16, ev_v)
            for g in range(4):
                P.op("dve", lambda e, g=g, dstx=dstx: e.tensor_copy(out=dstx[:, :, g, 64], in_=vtok), r=["vtok"], w=[nm])

        def ev_g(b, tg, nt):
            P.op("act", lambda e: e.activation(out=sg[:, tg:tg + 4, :], in_=psum[:, b, :].rearrange("p (a c) -> p a c", a=4)[:, :, 0:48],
                                               func=AF.Sigmoid), r=[("ps", b)], w=["sg"])
        tm_linear(w_in, 0, NGC, lambda k, tt: xnT[:, k, NO + tt * 128:NO + (tt + 1) * 128], ["xnT"], KC, 8, ev_g, ncols=48)

        A.mark()
        rawT = A.alloc([128, 2, L], BF16)
        peT32 = A.alloc([128, 32], F32)
        peT = A.alloc([128, 32], BF16)
        bh = A.alloc([128, 2], F32)
        hacc = A.alloc([128, 4, 2, 128], F32)
        gx = A.alloc([128, 128], F32)
        gx2 = A.alloc([128, 128], F32)
        gT = A.alloc([128, 4, 2, 128], BF16)
        w2s = A.alloc([128, 2, 64], F32)
        w2b = A.alloc([128, 2, 128], BF16)
        for which, (craw, w1_d, w2_d, pe_d) in enumerate([(KCR, c_w1k, c_w2k, c_pek), (VCR, c_w1v, c_w2v, c_pev)]):
            for tl in range(2):
                def ev_r(b, t0, n, tl=tl):
                    P.op("act", lambda e: e.activation(out=rawT[:, tl, t0:t0 + n], in_=psum[:, b, 0:n], func=AF.Copy),
                         r=[("ps", b)], w=["rawT"])
                fm_linear(w_in, 0, craw + tl * 128, 128, xn_all, ["xnT"], KC, TOK4, ev_r)
            P.dma(peT32, pe_d, w=["peT32"], grp="n_pe")
            P.op("dve", lambda e: e.tensor_copy(out=peT, in_=peT32), r=["peT32"], w=["peT"])
            P.dma(w2s, w2_d.rearrange("(k p) c -> p k c", p=128), w=["w2s"], grp="n_w2")
            P.op("dve", lambda e: e.tensor_copy(out=w2b[:, :, 0:64], in_=w2s), r=["w2s"], w=["w2b"])
            P.op("dve", lambda e: e.tensor_copy(out=w2b[:, :, 64:128], in_=w2s), r=["w2s"], w=["w2b"])
            for lg in range(4):
                i = wctr[0]
                wctr[0] += 1
                s_ = i % NST
                wv32 = wst[s_].rearrange("p a b -> p (a b)").rearrange("p (l c) -> p l c", l=8)
                wvb = wbf[s_].rearrange("p a b -> p (a b)").rearrange("p (l c) -> p l c", l=8)
                src = w1_d[lg * 512:(lg + 1) * 512, :].rearrange("(l d) c -> d l c", d=64)
                P.dma(wv32[0:64], src, w=[("wst", s_)], grp="w%d" % s_)
                P.dma(wv32[64:128], src, w=[("wst", s_)], grp="w%d" % s_)
                P.op("pool", lambda e, wvb=wvb, wv32=wv32: e.tensor_copy(out=wvb, in_=wv32), r=[("wst", s_)], w=[("wbf", s_)])
                for g in range(4):
                    p0 = (g % 2) * 64
                    b = P.bank()
                    for hc in range(2):
                        for l8 in range(8):
                            l = lg * 8 + l8
                            P.op("pe", lambda e, b=b, g=g, hc=hc, l8=l8, l=l, p0=p0, wvb=wvb: e.matmul(
                                psum[:, b, hc * 128:hc * 128 + 127], lhsT=wvb[p0:p0 + 64, l8, hc * 128:(hc + 1) * 128],
                                rhs=rawT[p0:p0 + 64, g // 2, l:l + 16 * 126 + 1:16], start=(l8 == 0), stop=(l8 == 7)),
                                r=[("wbf", s_), "rawT"], w=[("ps", b)])
                    hv = hacc[:, g, :, :]
                    pv_ = psum[:, b, 0:256].rearrange("p (a c) -> p a c", a=2)
                    if lg == 0:
                        P.op("act", lambda e, hv=hv, pv_=pv_: e.activation(out=hv[:, :, 0:127], in_=pv_[:, :, 0:127], func=AF.Copy), r=[("ps", b)], w=[("hacc", g)])
                    else:
                        P.op("dve", lambda e, hv=hv, pv_=pv_: e.tensor_tensor(out=hv[:, :, 0:127], in0=pv_[:, :, 0:127], in1=hv[:, :, 0:127], op=ALU.add),
                             r=[("ps", b), ("hacc", g)], w=[("hacc", g)])
                b = P.bank()
                for hc in range(2):
                    for l8 in range(8):
                        l = lg * 8 + l8
                        P.op("pe", lambda e, b=b, hc=hc, l8=l8, l=l, wvb=wvb: e.matmul(
                            psum[:, b, hc * 2:hc * 2 + 1], lhsT=wvb[0:64, l8, hc * 128:(hc + 1) * 128],
                            rhs=peT[0:64, l:l + 1], start=(l8 == 0), stop=(l8 == 7)),
                            r=[("wbf", s_), "peT"], w=[("ps", b)])
                pb_ = psum[:, b, 0:4].rearrange("p (a c) -> p a c", a=2)[:, :, 0]
                if lg == 0:
                    P.op("dve", lambda e, pb_=pb_: e.tensor_copy(out=bh, in_=pb_), r=[("ps", b)], w=["bh"])
                else:
                    P.op("dve", lambda e, pb_=pb_: e.tensor_tensor(out=bh, in0=pb_, in1=bh, op=ALU.add), r=[("ps", b), "bh"], w=["bh"])
            for g in range(4):
                for hc in range(2):
                    src = hacc[:, g, hc, 0:127]
                    P.op("act", lambda e, src=src, hc=hc: e.activation(out=gx[:, 0:127], in_=src, func=AF.Identity, bias=bh[:, hc:hc + 1]),
                         r=[("hacc", g), "bh"], w=["gx"])
                    P.op("dve", lambda e: e.tensor_tensor(out=gx2[:, 0:127], in0=gx[:, 0:127], in1=gx[:, 0:127], op=ALU.mult), r=["gx"], w=["gx2"])
                    P.op("dve", lambda e: e.tensor_scalar(out=gx2[:, 0:127], in0=gx2[:, 0:127], scalar1=0.044715, scalar2=1.0, op0=ALU.mult, op1=ALU.add),
                         r=["gx2"], w=["gx2"])
                    P.op("dve", lambda e: e.tensor_tensor(out=gx2[:, 0:127], in0=gx2[:, 0:127], in1=gx[:, 0:127], op=ALU.mult), r=["gx", "gx2"], w=["gx2"])
                    P.op("act", lambda e: e.activation(out=gx2[:, 0:127], in_=gx2[:, 0:127], func=AF.Tanh, scale=0.7978845608028654), r=["gx2"], w=["gx2"])
                    P.op("dve", lambda e: e.tensor_scalar(out=gx2[:, 0:127], in0=gx2[:, 0:127], scalar1=0.5, scalar2=0.5, op0=ALU.mult, op1=ALU.add),
                         r=["gx2"], w=["gx2"])
                    P.op("dve", lambda e, g=g, hc=hc: e.tensor_tensor(out=gT[:, g, hc, 0:127], in0=gx2[:, 0:127], in1=gx[:, 0:127], op=ALU.mult),
                         r=["gx", "gx2"], w=["gT"])
            for g in range(4):
                b = 7
                if which == 0:
                    for hc in range(2):
                        P.op("pe", lambda e, b=b, g=g, hc=hc: e.matmul(psum[:, b, 0:127], lhsT=w2b[:, hc, :], rhs=gT[:, g, hc, 0:127],
                                                                      start=(hc == 0), stop=(hc == 1)), r=["w2b", "gT"], w=[("ps", b)])
                    P.op("act", lambda e, b=b, g=g: e.activation(out=kcT[:, g, 0:127], in_=psum[:, b, 0:127], func=AF.Copy), r=[("ps", b)], w=["kcT"])
                else:
                    for hc in range(2):
                        P.op("pe", lambda e, b=b, g=g, hc=hc: e.matmul(psum[0:127, b, 0:64], lhsT=gT[:, g, hc, 0:127], rhs=w2b[:, hc, 0:64],
                                                                      start=(hc == 0), stop=(hc == 1)), r=["w2b", "gT"], w=[("ps", b)])
                    P.op("dve", lambda e, b=b, g=g: e.tensor_scalar(out=vcx[0:127, g, 0:64], in0=psum[0:127, b, 0:64], scalar1=ccst[0:127, 0:1], scalar2=None,
                                                                   op0=ALU.mult), r=[("ps", b), "ccst"], w=["vcx"])
                    P.op("dve", lambda e, g=g: e.tensor_copy(out=vcx[0:127, g, 64:97], in_=ccst[0:127, :]), r=["ccst"], w=["vcx"])
        P.barrier()
        A.release()

        ynsa = A.alloc([128, 1024], F32)
        ybt = A.alloc([128, 8, 128], BF16)
        impg = A.alloc([128, 4, 32], F32)
        sc = A.alloc([128, 32], F32)
        cmpt = A.alloc([128, 32, 32], F32)
        rk = A.alloc([128, 32], F32)
        t1 = A.alloc([128, 32], F32)
        selb = A.alloc([128, 4, 32], F32)
        selbT = A.alloc([128, 4, 128], BF16)
        sps = [A.alloc([128, 128], F32) for _ in range(4)]
        pts = [A.alloc([128, 128], BF16) for _ in range(4)]
        rv = A.alloc([128, 8], F32)
        sub = [0]
        acc_i = [0]

        def s_region():
            i = sub[0]
            sub[0] = (i + 1) % 4
            return i, 0

        def a_region():
            i = acc_i[0]
            acc_i[0] = (i + 1) % 3
            return 4 + i, 0
        vis = [0]

        pending = []
        LAG = 3

        def flush(n=0):
            while len(pending) > n:
                pending.pop(0)()

        def visit(h, qt, klhsT, nk_rows, extra_mm, dtile, bias_ap, vrhs, acc, first, last, kdeps, fin=None):
            g, p0 = h // 4, (h % 2) * 64
            sb, so = s_region()
            i3 = vis[0] % 4
            vis[0] += 1
            spt, ptt = sps[i3], pts[i3]
            skey = ("ps", sb)
            P.op("pe", lambda e: e.matmul(psum[0:nk_rows, sb, so:so + 128], lhsT=klhsT, rhs=qT[p0:p0 + 64, h // 2, qt * 128:(qt + 1) * 128],
                                          start=True, stop=(extra_mm is None)), r=["qT"] + kdeps, w=[skey])
            if extra_mm is not None:
                el, er = extra_mm
                P.op("pe", lambda e: e.matmul(psum[0:nk_rows, sb, so:so + 128], lhsT=el, rhs=er, start=False, stop=True),
                     r=["emat", ("selbT", g)], w=[skey])
            P.op("dve", lambda e: e.scalar_tensor_tensor(out=spt[0:nk_rows, :], in0=dtile, scalar=-SLOPES[h], in1=psum[0:nk_rows, sb, so:so + 128],
                                                         op0=ALU.mult, op1=ALU.add), r=[skey, "dtiles"], w=[("sp", i3)])
            if bias_ap is None:
                P.op("act", lambda e: e.activation(out=ptt[0:nk_rows, :], in_=spt[0:nk_rows, :], func=AF.Exp), r=[("sp", i3)], w=[("pt", i3)])
            else:
                P.op("act", lambda e: e.activation(out=ptt[0:nk_rows, :], in_=spt[0:nk_rows, :], func=AF.Exp, bias=bias_ap),
                     r=[("sp", i3), "bcol"], w=[("pt", i3)])
            ab, ao, ncol = acc

            def back():
                P.op("pe", lambda e: e.matmul(psum[:, ab, ao:ao + ncol], lhsT=ptt[0:nk_rows, :], rhs=vrhs, start=first, stop=last),
                     r=[("pt", i3)] + kdeps, w=[("ps", ab)])
                if fin is not None:
                    fin()
            pending.append(back)
            flush(LAG)

        def finish(h, qt, acc, br, first_branch):
            ab, ao, ncol = acc
            akey = ("ps", ab)
            c = vis[0] % 8
            P.op("dve", lambda e: e.tensor_scalar(out=rv[:, c:c + 1], in0=psum[:, ab, ao + 64:ao + 65], scalar1=1e-30, scalar2=None, op0=ALU.max),
                 r=[akey], w=["rv"])
            P.op("dve", lambda e: e.reciprocal(out=rv[:, c:c + 1], in_=rv[:, c:c + 1]), r=["rv"], w=["rv"])
            if br == 0:
                g = h // 4
                if h % 4 == 0:
                    P.op("dve", lambda e: e.tensor_scalar(out=impg[:, g, :], in0=psum[:, ab, ao + 65:ao + 97], scalar1=rv[:, c:c + 1], scalar2=None,
                                                          op0=ALU.mult), r=[akey, "rv"], w=[("impg", g)])
                else:
                    P.op("dve", lambda e: e.scalar_tensor_tensor(out=impg[:, g, :], in0=psum[:, ab, ao + 65:ao + 97], scalar=rv[:, c:c + 1],
                                                                 in1=impg[:, g, :], op0=ALU.mult, op1=ALU.add),
                         r=[akey, "rv", ("impg", g)], w=[("impg", g)])
            P.op("dve", lambda e: e.tensor_tensor(out=rv[:, c:c + 1], in0=rv[:, c:c + 1], in1=sg[:, qt, br * 16 + h:br * 16 + h + 1], op=ALU.mult),
                 r=["rv", "sg"], w=["rv"])
            yh = ynsa[:, h * 64:(h + 1) * 64]
            if first_branch:
                P.op("dve", lambda e: e.tensor_scalar(out=yh, in0=psum[:, ab, ao:ao + 64], scalar1=rv[:, c:c + 1], scalar2=None, op0=ALU.mult),
                     r=[akey, "rv"], w=[("ynsa", h)])
            else:
                P.op("dve", lambda e: e.scalar_tensor_tensor(out=yh, in0=psum[:, ab, ao:ao + 64], scalar=rv[:, c:c + 1], in1=yh,
                                                             op0=ALU.mult, op1=ALU.add), r=[akey, "rv", ("ynsa", h)], w=[("ynsa", h)])

        for qt in range(8):
            qb = 8 + qt
            for h in range(16):
                g, p0 = h // 4, (h % 2) * 64
                ab, ao = a_region()
                acc = (ab, ao, 97)
                visit(h, qt, kcT[p0:p0 + 64, g, 0:127], 127, None, dc[0:127, qt, :], None, vcx[0:127, g, 0:97], acc, True, True, ["kcT", "vcx"],
                      fin=(lambda h=h, acc=acc: finish(h, qt, acc, 0, True)))
            flush()
            for g in range(4):
                P.op("dve", lambda e, g=g, qt=qt: e.tensor_tensor(out=sc, in0=impg[:, g, :], in1=skeep[:, qt, :], op=ALU.mult), r=[("impg", g), "skeep"], w=["sc"])
                P.op("dve", lambda e, qt=qt: e.tensor_tensor(out=sc, in0=sc, in1=sbias[:, qt, :], op=ALU.add), r=["sc", "sbias"], w=["sc"])
                P.op("dve", lambda e: e.tensor_tensor(out=cmpt, in0=sc.unsqueeze(1).broadcast_to([128, 32, 32]),
                                                      in1=sc.unsqueeze(2).broadcast_to([128, 32, 32]), op=ALU.is_gt), r=["sc"], w=["cmpt"])
                P.op("dve", lambda e: e.tensor_reduce(out=rk, in_=cmpt, axis=AX.X, op=ALU.add), r=["cmpt"], w=["rk"])
                P.op("dve", lambda e: e.tensor_scalar(out=rk, in0=rk, scalar1=15.5, scalar2=None, op0=ALU.is_lt), r=["rk"], w=["rk"])
                P.op("dve", lambda e: e.tensor_scalar(out=t1, in0=sc, scalar1=-1e29, scalar2=None, op0=ALU.is_gt), r=["sc"], w=["t1"])
                P.op("dve", lambda e: e.tensor_tensor(out=rk, in0=rk, in1=t1, op=ALU.mult), r=["rk", "t1"], w=["rk"])
                P.op("dve", lambda e, g=g: e.tensor_scalar(out=selb[:, g, :], in0=rk, scalar1=-1.0, scalar2=BIG, op0=ALU.add, op1=ALU.mult),
                     r=["rk"], w=[("selb", g)])
                sb, so = s_region()
                P.op("pe", lambda e, g=g, sb=sb, so=so: e.transpose(out=psum[0:32, sb, so:so + 128], in_=selb[:, g, :], identity=ident),
                     r=[("selb", g), "ident"], w=[("ps", sb)])
                P.op("act", lambda e, g=g, sb=sb, so=so: e.activation(out=selbT[0:32, g, :], in_=psum[0:32, sb, so:so + 128], func=AF.Copy),
                     r=[("ps", sb)], w=[("selbT", g)])
            for h in range(16):
                g, p0 = h // 4, (h % 2) * 64
                ab, ao = a_region()
                acc = (ab, ao, 65)
                for kb in range(qb + 1):
                    rel = qb - kb
                    dt_ = d0 if rel == 0 else dfull
                    bias_ap = None if rel == 0 else bcol[:, h * 16 + rel:h * 16 + rel + 1]
                    visit(h, qt, ksT[p0:p0 + 64, g, kb * 128:(kb + 1) * 128], 128, (emat[0:32, kb, :], selbT[0:32, g, :]), dt_, bias_ap,
                          vsx[:, kb, g, :], acc, kb == 0, kb == qb, ["ksT", "vsx"],
                          fin=((lambda h=h, acc=acc: finish(h, qt, acc, 1, False)) if kb == qb else None))
            for h in range(16):
                g, p0 = h // 4, (h % 2) * 64
                ab, ao = a_region()
                acc = (ab, ao, 65)
                for kb in range(qb - 4, qb + 1):
                    rel = qb - kb
                    dt_ = d0 if rel == 0 else (d4 if rel == 4 else dfull)
                    bias_ap = None if rel == 0 else bcol[:, h * 16 + rel:h * 16 + rel + 1]
                    visit(h, qt, kwT[p0:p0 + 64, g, kb * 128:(kb + 1) * 128], 128, None, dt_, bias_ap,
                          vwx[:, kb, g, :], acc, kb == qb - 4, kb == qb, ["kwT", "vwx"],
                          fin=((lambda h=h, acc=acc: finish(h, qt, acc, 2, False)) if kb == qb else None))
            flush()
            for kg in range(2):
                b = 7
                for j in range(4):
                    kc = kg * 4 + j
                    P.op("pe", lambda e, b=b, j=j, kc=kc: e.transpose(out=psum[:, b, j * 128:(j + 1) * 128], in_=ynsa[:, kc * 128:(kc + 1) * 128], identity=ident),
                         r=[("ynsa", 2 * kc), ("ynsa", 2 * kc + 1), "ident"], w=[("ps", b)])
                P.op("act", lambda e, b=b, kg=kg: e.activation(out=ybt[:, kg * 4:(kg + 1) * 4, :], in_=psum[:, b, :].rearrange("p (a c) -> p a c", a=4), func=AF.Copy),
                     r=[("ps", b)], w=["ybt"])
            P.dma(ysc[:, 1, :, qt * 128:(qt + 1) * 128], ybt, r=["ybt"], w=["ysc"], grp="ysc")
            if dbg == "nsa":
                P.dma(dbg_yb[:, :, qt * 128:(qt + 1) * 128], ybt, r=["ybt"], w=["dbg_yb"], grp="dbgyb")
        P.barrier()

    GN_EPS = 64e-5

    def rwkv_phase():
        A.off = phase_off
        MA = Arena(mix_raw, 8192)
        twT = MA.alloc([128, L], BF16)
        alT = MA.alloc([128, L], BF16)
        sgT = MA.alloc([128, 2, NO], BF16)
        M1 = MA.alloc([128, 256], F32)
        M2 = MA.alloc([128, 256], F32)
        M3 = MA.alloc([128, 128], F32)
        identb = MA.alloc([128, 128], BF16)
        bones = MA.alloc([128, 128], F32)
        resetm = MA.alloc([128, 512], F32)
        ptab = MA.alloc([128, 8, 8], F32)
        omt = MA.alloc([128, 8, 8], F32)
        lmu = MA.alloc([128, 4], F32)
        lomm = MA.alloc([128, 4], F32)
        lnw = MA.alloc([128, 128], F32)
        lnb = MA.alloc([128, 128], F32)
        ybuf = MA.alloc([128, NO], BF16)
        wl32 = MA.alloc([128, 4, 128], F32)
        wlb = MA.alloc([128, 4, 128], BF16)
        carry = MA.alloc([128, 8], F32)
        st = MA.alloc([128, 8], F32)
        eC = MA.alloc([128, 8], F32)
        AG = MA.alloc([128, 2, 128], BF16)
        AU = MA.alloc([128, 2, 128], BF16)
        RbarT = MA.alloc([128, 128], BF16)
        Phi = MA.alloc([128, 128], F32)
        DeltaT = MA.alloc([128, 128], F32)
        Sst = MA.alloc([128, 128], F32)
        Sb = MA.alloc([128, 128], BF16)
        ysb = MA.alloc([128, 128], F32)
        ysq = MA.alloc([128, 128], F32)
        yn = MA.alloc([128, 128], F32)
        ytr = MA.alloc([128, 128], F32)
        t64 = MA.alloc([128, 64], F32)
        TT = [MA.alloc([128, 128], BF16) for _ in range(4)]
        rS = A.alloc([128, L], F32)
        kS = A.alloc([128, L], F32)
        vS = A.alloc([128, L], F32)
        raw = A.alloc([128, 514], F32)
        bon = raw[:, 0:512]
        lw = A.alloc([128, 512], F32)
        at = A.alloc([128, 512], F32)
        kk = A.alloc([128, 512], F32)
        k2 = A.alloc([128, 512], F32)
        cl = A.alloc([128, 512], F32)
        e1 = A.alloc([128, 512], F32)
        e2 = A.alloc([128, 512], F32)
        gq = A.alloc([128, 512], F32)
        ARblk = A.alloc([128, 8, 2, 128], BF16)
        Bblk = A.alloc([128, 8, 128], BF16)
        Kblk = A.alloc([128, 8, 128], BF16)
        B2blk = A.alloc([128, 8, 128], BF16)
        K2blk = A.alloc([128, 8, 128], BF16)
        Vblk = A.alloc([128, 8, 128], BF16)
        tm = [A.alloc([128, 4, 128], BF16) for _ in range(4)]
        LM1 = [A.alloc([128, 128], BF16) for _ in range(4)]
        LM2 = [A.alloc([128, 256], BF16) for _ in range(4)]
        W3 = [[A.alloc([128, 3, 128], BF16) for _ in range(2)] for _ in range(4)]

        for (dst, src, nm) in [(M1, r_m1, "M1"), (M2, r_m2, "M2"), (M3, r_m3, "M3"), (identb, r_identb, "identb"), (bones, r_bones, "bones"),
                               (resetm, r_resetm, "resetm"), (ptab, r_ptab, "ptab"), (lmu, r_lmu, "lmu")]:
            P.dma(dst, src, w=[nm], grp="r_" + nm)
        P.op("dve", lambda e: e.tensor_scalar(out=omt, in0=ptab, scalar1=-1.0, scalar2=1.0, op0=ALU.mult, op1=ALU.add), r=["ptab"], w=["omt"])
        P.op("dve", lambda e: e.tensor_scalar(out=lomm, in0=lmu, scalar1=-1.0, scalar2=1.0, op0=ALU.mult, op1=ALU.add), r=["lmu"], w=["lomm"])
        for blk_ in (ARblk, Bblk, Kblk, B2blk, K2blk, Vblk):
            P.op("pool", lambda e, blk_=blk_: e.memset(blk_, 0.0), w=["blk"])

        def shift_evac(b, t0, n, nrow, mu_col, omm_col, ci, dst):
            if t0 == 0:
                P.op("dve", lambda e: e.memset(raw[0:nrow, 0:1], 0.0), w=["raw"])
            else:
                P.op("dve", lambda e: e.tensor_copy(out=raw[0:nrow, 0:1], in_=carry[0:nrow, ci:ci + 1]), r=[("carry", ci)], w=["raw"])
            P.op("act", lambda e: e.activation(out=raw[0:nrow, 1:n + 1], in_=psum[0:nrow, b, 0:n], func=AF.Copy), r=[("ps", b)], w=["raw"])
            P.op("dve", lambda e: e.tensor_scalar(out=dst, in0=raw[0:nrow, 1:n + 1], scalar1=omm_col[0:nrow], scalar2=None, op0=ALU.mult),
                 r=["raw", "omt", "lomm"], w=["shiftdst"])
            P.op("dve", lambda e: e.scalar_tensor_tensor(out=dst, in0=raw[0:nrow, 0:n], scalar=mu_col[0:nrow], in1=dst,
                                                         op0=ALU.mult, op1=ALU.add), r=["raw", "shiftdst", "ptab", "lmu"], w=["shiftdst"])
            P.op("dve", lambda e: e.tensor_copy(out=carry[0:nrow, ci:ci + 1], in_=raw[0:nrow, n:n + 1]), r=["raw"], w=[("carry", ci)])

        for li, (c0, nrow) in enumerate([(3072, 96), (3168, 96), (3264, 128), (3392, 128)]):
            def ev_l(b, t0, n, li=li, nrow=nrow):
                shift_evac(b, t0, n, nrow, lmu[:, li:li + 1], lomm[:, li:li + 1], 3, e1[0:nrow, 0:n])
                if li == 0:
                    P.op("act", lambda e: e.activation(out=twT[0:nrow, t0:t0 + n], in_=e1[0:nrow, 0:n], func=AF.Tanh), r=["shiftdst"], w=["twT", "shiftdst"])
                elif li == 1:
                    P.op("act", lambda e: e.activation(out=alT[0:nrow, t0:t0 + n], in_=e1[0:nrow, 0:n], func=AF.Copy), r=["shiftdst"], w=["alT", "shiftdst"])
                elif t0 >= NO:
                    P.op("act", lambda e: e.activation(out=sgT[:, li - 2, t0 - NO:t0 - NO + n], in_=e1[0:nrow, 0:n], func=AF.Sigmoid),
                         r=["shiftdst"], w=["sgT", "shiftdst"])
            fm_linear(w_in, 0, c0, nrow, xn_all, ["xnT"], KC, TOK4, ev_l)

        def halves(fn):
            fn(0, 0)
            fn(64, 64)

        def do_pair(hp):
            pc = lambda j: ptab[:, hp, j:j + 1]
            oc = lambda j: omt[:, hp, j:j + 1]
            P.dma(wl32[0:96, 0, :], r_wup[:, hp * 128:(hp + 1) * 128], w=["wl32"], grp="r_wl")
            P.dma(wl32[0:96, 1, :], r_aup[:, hp * 128:(hp + 1) * 128], w=["wl32"], grp="r_wl")
            P.dma(wl32[:, 2:4, :], r_gup[:, hp * 128:(hp + 1) * 128].rearrange("(k p) c -> p k c", p=128), w=["wl32"], grp="r_wl")
            P.op("pool", lambda e: e.tensor_copy(out=wlb[0:96, 0:2, :], in_=wl32[0:96, 0:2, :]), r=["wl32"], w=["wlb"])
            P.op("pool", lambda e: e.tensor_copy(out=wlb[:, 2:4, :], in_=wl32[:, 2:4, :]), r=["wl32"], w=["wlb"])
            P.dma(lnw, r_lnw[hp], w=["lnw"], grp="r_lnw")
            P.dma(lnb, r_lnb[hp], w=["lnb"], grp="r_lnb")
            for qi, (c0, dstS) in enumerate([(0, rS), (1024, kS), (2048, vS)]):
                def ev_s(b, t0, n, qi=qi, dstS=dstS):
                    shift_evac(b, t0, n, 128, pc(qi), oc(qi), qi, dstS[:, t0:t0 + n])
                fm_linear(w_in, 0, c0 + hp * 128, 128, xn_all, ["xnT"], KC, TOK4, ev_s)
            P.op("pool", lambda e: e.memset(Sst, 0.0), w=["S"])
            P.op("pool", lambda e: e.memset(Sb, 0.0), w=["Sb"])
            def do_quarter(qtr):
                T0 = qtr * 512
                own = qtr >= 2
                sl = slice(T0, T0 + 512)
                b = P.bank()
                P.op("pe", lambda e, b=b: e.matmul(psum[:, b, :], lhsT=wlb[0:96, 0, :], rhs=twT[0:96, sl], start=True, stop=True), r=["wlb", "twT"], w=[("ps", b)])
                P.op("act", lambda e, b=b: e.activation(out=lw, in_=psum[:, b, :], func=AF.Sigmoid, bias=pc(3)), r=[("ps", b), "ptab"], w=["lw"])
                P.op("dve", lambda e: e.tensor_scalar(out=lw, in0=lw, scalar1=-0.6065306597126334, scalar2=None, op0=ALU.mult), r=["lw"], w=["lw"])
                b = P.bank()
                P.op("pe", lambda e, b=b: e.matmul(psum[:, b, :], lhsT=wlb[0:96, 1, :], rhs=alT[0:96, sl], start=True, stop=True), r=["wlb", "alT"], w=[("ps", b)])
                P.op("act", lambda e, b=b: e.activation(out=at, in_=psum[:, b, :], func=AF.Sigmoid, bias=pc(4)), r=[("ps", b), "ptab"], w=["at"])
                P.op("dve", lambda e: e.tensor_scalar(out=e1, in0=kS[:, sl], scalar1=pc(5), scalar2=None, op0=ALU.mult), r=["shiftdst", "ptab"], w=["e1"])
                P.op("act", lambda e: e.activation(out=e2, in_=e1, func=AF.Square), r=["e1"], w=["e2"])
                b = P.bank()
                P.op("pe", lambda e, b=b: e.matmul(psum[:, b, :], lhsT=bones, rhs=e2, start=True, stop=True), r=["bones", "e2"], w=[("ps", b)])
                P.op("dve", lambda e, b=b: e.tensor_scalar(out=e2, in0=psum[:, b, :], scalar1=1e-24, scalar2=None, op0=ALU.max), r=[("ps", b)], w=["e2"])
                P.op("act", lambda e: e.activation(out=e2, in_=e2, func=AF.Sqrt), r=["e2"], w=["e2"])
                P.op("dve", lambda e: e.reciprocal(out=e2, in_=e2), r=["e2"], w=["e2"])
                P.op("dve", lambda e: e.tensor_tensor(out=kk, in0=e1, in1=e2, op=ALU.mult), r=["e1", "e2"], w=["kk"])
                P.op("dve", lambda e: e.tensor_scalar(out=e1, in0=at, scalar1=pc(6), scalar2=oc(6), op0=ALU.mult, op1=ALU.add), r=["at", "ptab", "omt"], w=["e1"])
                P.op("dve", lambda e: e.tensor_tensor(out=k2, in0=kS[:, sl], in1=e1, op=ALU.mult), r=["e1", "shiftdst"], w=["k2"])
                if own:
                    P.op("dve", lambda e: e.tensor_tensor(out=e1, in0=rS[:, sl], in1=k2, op=ALU.mult), r=["k2", "shiftdst"], w=["e1"])
                    P.op("dve", lambda e: e.tensor_scalar(out=e1, in0=e1, scalar1=pc(7), scalar2=None, op0=ALU.mult), r=["e1", "ptab"], w=["e1"])
                    b = P.bank()
                    P.op("pe", lambda e, b=b: e.matmul(psum[:, b, :], lhsT=bones, rhs=e1, start=True, stop=True), r=["bones", "e1"], w=[("ps", b)])
                    P.op("dve", lambda e, b=b: e.tensor_tensor(out=bon, in0=psum[:, b, :], in1=vS[:, sl], op=ALU.mult), r=[("ps", b), "shiftdst"], w=["raw"])
                    b = P.bank()
                    for kt in range(2):
                        P.op("pe", lambda e, b=b, kt=kt: e.matmul(psum[:, b, :], lhsT=wlb[:, 2 + kt, :], rhs=sgT[:, kt, T0 - NO:T0 - NO + 512],
                                                                 start=(kt == 0), stop=(kt == 1)), r=["wlb", "sgT"], w=[("ps", b)])
                    P.op("act", lambda e, b=b: e.activation(out=gq, in_=psum[:, b, :], func=AF.Copy), r=[("ps", b)], w=["gq"])
                P.op("dve", lambda e: e.tensor_tensor_scan(out=cl, data0=resetm, data1=lw, initial=0.0, op0=ALU.mult, op1=ALU.add), r=["resetm", "lw"], w=["cl"])
                cl3 = cl.rearrange("p (c t) -> p c t", t=64)
                P.op("act", lambda e: e.activation(out=eC, in_=cl3[:, :, 63], func=AF.Exp), r=["cl"], w=["eC"])

                def v3(ap, p0):
                    return ap[p0:p0 + 64, :].rearrange("p (c t) -> p c t", t=64)
                P.op("act", lambda e: e.activation(out=e1, in_=cl, func=AF.Exp), r=["cl"], w=["e1"])
                halves(lambda p0, c0: P.op("dve", lambda e: e.tensor_tensor(out=ARblk[p0:p0 + 64, :, 1, c0:c0 + 64], in0=v3(rS[:, sl], p0), in1=v3(e1, p0), op=ALU.mult),
                                           r=["e1", "shiftdst"], w=["blk"]))
                P.op("act", lambda e: e.activation(out=e1, in_=cl, func=AF.Exp, scale=-1.0), r=["cl", "blk"], w=["e1"])
                P.op("dve", lambda e: e.tensor_tensor(out=e2, in0=kk, in1=at, op=ALU.mult), r=["kk", "at"], w=["e2"])
                halves(lambda p0, c0: P.op("dve", lambda e: e.tensor_tensor(out=Bblk[p0:p0 + 64, :, c0:c0 + 64], in0=v3(e2, p0), in1=v3(e1, p0), op=ALU.mult),
                                           r=["e1", "e2"], w=["blk"]))
                halves(lambda p0, c0: P.op("dve", lambda e: e.tensor_tensor(out=Kblk[p0:p0 + 64, :, c0:c0 + 64], in0=v3(k2, p0), in1=v3(e1, p0), op=ALU.mult),
                                           r=["e1", "k2"], w=["blk"]))
                P.op("dve", lambda e: e.tensor_tensor(out=e1, in0=cl, in1=lw, op=ALU.subtract), r=["cl", "lw", "blk"], w=["e1"])
                P.op("act", lambda e: e.activation(out=e1, in_=e1, func=AF.Exp), r=["e1"], w=["e1"])
                halves(lambda p0, c0: P.op("dve", lambda e: e.tensor_tensor(out=ARblk[p0:p0 + 64, :, 0, c0:c0 + 64], in0=v3(kk, p0), in1=v3(e1, p0), op=ALU.mult),
                                           r=["e1", "kk"], w=["blk"]))
                P.op("dve", lambda e: e.tensor_tensor(out=e1.rearrange("p (c t) -> p c t", t=64), in0=cl3[:, :, 63:64].broadcast_to([128, 8, 64]), in1=cl3,
                                                      op=ALU.subtract), r=["cl", "blk"], w=["e1"])
                P.op("act", lambda e: e.activation(out=e1, in_=e1, func=AF.Exp), r=["e1"], w=["e1"])
                halves(lambda p0, c0: P.op("dve", lambda e: e.tensor_tensor(out=B2blk[p0:p0 + 64, :, c0:c0 + 64], in0=v3(e2, p0), in1=v3(e1, p0), op=ALU.mult),
                                           r=["e1", "e2"], w=["blk"]))
                halves(lambda p0, c0: P.op("dve", lambda e: e.tensor_tensor(out=K2blk[p0:p0 + 64, :, c0:c0 + 64], in0=v3(k2, p0), in1=v3(e1, p0), op=ALU.mult),
                                           r=["e1", "k2"], w=["blk"]))
                halves(lambda p0, c0: P.op("pool", lambda e: e.tensor_copy(out=Vblk[p0:p0 + 64, :, c0:c0 + 64], in_=v3(vS[:, sl], p0)), r=["shiftdst"], w=["blk"]))

                for g4 in range(2):
                    for j in range(4):
                        cq = g4 * 4 + j
                        b = P.bank()
                        pb = psum[:, b, 0:256].bitcast(BF16)
                        for i, src in enumerate([ARblk[:, cq, 0, :], B2blk[:, cq, :], K2blk[:, cq, :], Vblk[:, cq, :]]):
                            P.op("pe", lambda e, i=i, src=src, pb=pb: e.transpose(out=pb[:, i * 128:(i + 1) * 128], in_=src, identity=identb),
                                 r=["blk", "identb"], w=[("ps", b)])
                        P.op("act", lambda e, j=j, pb=pb: e.activation(out=tm[j].rearrange("p a b -> p (a b)"), in_=pb, func=AF.Copy), r=[("ps", b)], w=[("tm", j)])
                        AR2 = ARblk[:, cq, :, :].rearrange("p a b -> p (a b)")
                        b = P.bank()
                        P.op("pe", lambda e, b=b, cq=cq, AR2=AR2: e.matmul(psum[:, b, 0:256], lhsT=Bblk[:, cq, :], rhs=AR2, start=True, stop=True), r=["blk"], w=[("ps", b)])
                        P.op("dve", lambda e, b=b, j=j: e.tensor_tensor(out=W3[j][0][:, 1, :], in0=psum[:, b, 0:128], in1=M1[:, 0:128], op=ALU.mult),
                             r=[("ps", b), "M1"], w=[("W3", j, 0)])
                        P.op("dve", lambda e, b=b, j=j: e.tensor_tensor(out=LM1[j], in0=psum[:, b, 128:256], in1=M1[:, 128:256], op=ALU.mult),
                             r=[("ps", b), "M1"], w=[("LM1", j)])
                        b = P.bank()
                        P.op("pe", lambda e, b=b, cq=cq, AR2=AR2: e.matmul(psum[:, b, 0:256], lhsT=Kblk[:, cq, :], rhs=AR2, start=True, stop=True), r=["blk"], w=[("ps", b)])
                        P.op("dve", lambda e, b=b, j=j: e.tensor_tensor(out=LM2[j], in0=psum[:, b, 0:256], in1=M2, op=ALU.mult), r=[("ps", b), "M2"], w=[("LM2", j)])
                        b = P.bank()
                        P.op("pe", lambda e, b=b, cq=cq: e.matmul(psum[:, b, 0:128], lhsT=ARblk[:, cq, 0, :], rhs=Bblk[:, cq, :], start=True, stop=True), r=["blk"], w=[("ps", b)])
                        P.op("dve", lambda e, b=b, j=j: e.tensor_tensor(out=W3[j][0][:, 2, :], in0=psum[:, b, 0:128], in1=M3, op=ALU.mult), r=[("ps", b), "M3"], w=[("W3", j, 0)])
                        P.op("pool", lambda e, j=j: e.tensor_copy(out=W3[j][0][:, 0, :], in_=identb), r=["identb"], w=[("W3", j, 0)])
                    for lvl in range(6):
                        last = lvl == 5
                        for j in range(4):
                            cur, nxt = W3[j][lvl % 2], W3[j][(lvl + 1) % 2]
                            b = P.bank()
                            nn = 128 if last else 256
                            P.op("pe", lambda e, b=b, cur=cur, nn=nn: e.matmul(psum[:, b, 0:nn], lhsT=cur[:, 2, :], rhs=cur[:, 0:nn // 128, :].rearrange("p a b -> p (a b)"),
                                                                              start=True, stop=True), r=[("W3", j, lvl % 2)], w=[("ps", b)])
                            if not last:
                                P.op("pe", lambda e, b=b, cur=cur: e.matmul(psum[:, b, 256:384], lhsT=cur[:, 1, :], rhs=cur[:, 2, :], start=True, stop=True),
                                     r=[("W3", j, lvl % 2)], w=[("ps", b)])
                            pdst = TT[j] if last else nxt[:, 0, :]
                            P.op("dve", lambda e, b=b, cur=cur, pdst=pdst: e.tensor_tensor(out=pdst, in0=psum[:, b, 0:128], in1=cur[:, 0, :], op=ALU.add),
                                 r=[("ps", b), ("W3", j, lvl % 2)], w=[("TT", j) if last else ("W3", j, (lvl + 1) % 2)])
                            if not last:
                                P.op("act", lambda e, b=b, nxt=nxt: e.activation(out=nxt[:, 1:3, :].rearrange("p a b -> p (a b)"), in_=psum[:, b, 128:384], func=AF.Copy),
                                     r=[("ps", b)], w=[("W3", j, (lvl + 1) % 2)])
                    for j in range(4):
                        cq = g4 * 4 + j
                        cg = qtr * 8 + cq
                        b = P.bank()
                        P.op("pe", lambda e, b=b, j=j: e.matmul(psum[:, b, 0:128], lhsT=LM2[j][:, 0:128], rhs=tm[j][:, 3, :], start=True, stop=True),
                             r=[("LM2", j), ("tm", j)], w=[("ps", b)])
                        P.op("act", lambda e, b=b: e.activation(out=AG[:, 1, :], in_=psum[:, b, 0:128], func=AF.Copy, scale=-1.0), r=[("ps", b)], w=["AG"])
                        P.op("pool", lambda e, j=j: e.tensor_copy(out=AG[:, 0, :], in_=tm[j][:, 0, :]), r=[("tm", j)], w=["AG"])
                        b = P.bank()
                        P.op("pe", lambda e, b=b, j=j: e.matmul(psum[:, b, 0:256], lhsT=TT[j], rhs=AG.rearrange("p a b -> p (a b)"), start=True, stop=True),
                             r=[("TT", j), "AG"], w=[("ps", b)])
                        P.op("act", lambda e, b=b: e.activation(out=AU.rearrange("p a b -> p (a b)"), in_=psum[:, b, 0:256], func=AF.Copy), r=[("ps", b)], w=["AU"])
                        if cg >= 16:
                            b = P.bank()
                            P.op("pe", lambda e, b=b, j=j: e.matmul(psum[:, b, 0:128], lhsT=AU[:, 0, :], rhs=LM1[j], start=True, stop=True),
                                 r=["AU", ("LM1", j)], w=[("ps", b)])
                            P.op("dve", lambda e, b=b, cq=cq: e.tensor_tensor(out=RbarT, in0=ARblk[:, cq, 1, :], in1=psum[:, b, 0:128], op=ALU.subtract),
                                 r=[("ps", b), "blk"], w=["RbarT"])
                        b = P.bank()
                        P.op("pe", lambda e, b=b, j=j: e.matmul(psum[:, b, 0:128], lhsT=AU[:, 0, :], rhs=tm[j][:, 1, :], start=True, stop=True),
                             r=["AU", ("tm", j)], w=[("ps", b)])
                        P.op("dve", lambda e, b=b, cq=cq: e.scalar_tensor_tensor(out=Phi, in0=ident, scalar=eC[:, cq:cq + 1], in1=psum[:, b, 0:128],
                                                                                op0=ALU.mult, op1=ALU.subtract), r=[("ps", b), "eC", "ident"], w=["Phi"])
                        b = P.bank()
                        P.op("pe", lambda e, b=b, j=j: e.matmul(psum[:, b, 0:128], lhsT=tm[j][:, 1, :], rhs=AU[:, 1, :], start=True, stop=False),
                             r=["AU", ("tm", j)], w=[("ps", b)])
                        P.op("pe", lambda e, b=b, j=j: e.matmul(psum[:, b, 0:128], lhsT=tm[j][:, 2, :], rhs=tm[j][:, 3, :], start=False, stop=True),
                             r=[("tm", j)], w=[("ps", b)])
                        P.op("act", lambda e, b=b: e.activation(out=DeltaT, in_=psum[:, b, 0:128], func=AF.Copy), r=[("ps", b)], w=["DeltaT"])
                        if cg >= 16:
                            to = cg * 64 - NO
                            b = P.bank()
                            P.op("pe", lambda e, b=b, j=j: e.matmul(psum[:, b, 0:128], lhsT=LM1[j], rhs=AU[:, 1, :], start=True, stop=False),
                                 r=["AU", ("LM1", j)], w=[("ps", b)])
                            P.op("pe", lambda e, b=b, j=j: e.matmul(psum[:, b, 0:128], lhsT=LM2[j][:, 128:256], rhs=tm[j][:, 3, :], start=False, stop=False),
                                 r=[("LM2", j), ("tm", j)], w=[("ps", b)])
                            P.op("pe", lambda e, b=b: e.matmul(psum[:, b, 0:128], lhsT=RbarT, rhs=Sb, start=False, stop=True), r=["RbarT", "Sb"], w=[("ps", b)])
                            P.op("act", lambda e, b=b: e.activation(out=ysb, in_=psum[:, b, 0:128], func=AF.Copy), r=[("ps", b)], w=["ysb"])
                            P.op("dve", lambda e: e.tensor_reduce(out=st[:, 0:1], in_=ysb, axis=AX.X, op=ALU.add), r=["ysb"], w=["st"])
                            P.op("act", lambda e: e.activation(out=ysq, in_=ysb, func=AF.Square), r=["ysb"], w=["ysq"])
                            P.op("dve", lambda e: e.tensor_reduce(out=st[:, 1:2], in_=ysq, axis=AX.X, op=ALU.add), r=["ysq"], w=["st"])
                            P.op("dve", lambda e: e.tensor_scalar(out=st[:, 2:3], in0=st[:, 0:1], scalar1=1.0 / 64, scalar2=None, op0=ALU.mult), r=["st"], w=["st"])
                            P.op("dve", lambda e: e.tensor_tensor(out=st[:, 3:4], in0=st[:, 2:3], in1=st[:, 2:3], op=ALU.mult), r=["st"], w=["st"])
                            P.op("dve", lambda e: e.scalar_tensor_tensor(out=st[:, 4:5], in0=st[:, 1:2], scalar=1.0 / 64, in1=st[:, 3:4], op0=ALU.mult, op1=ALU.subtract),
                                 r=["st"], w=["st"])
                            P.op("dve", lambda e: e.tensor_scalar(out=st[:, 4:5], in0=st[:, 4:5], scalar1=GN_EPS, scalar2=None, op0=ALU.add), r=["st"], w=["st"])
                            P.op("act", lambda e: e.activation(out=st[:, 4:5], in_=st[:, 4:5], func=AF.Sqrt), r=["st"], w=["st"])
                            P.op("dve", lambda e: e.reciprocal(out=st[:, 5:6], in_=st[:, 4:5]), r=["st"], w=["st"])
                            P.op("dve", lambda e: e.tensor_scalar(out=yn, in0=ysb, scalar1=st[:, 2:3], scalar2=st[:, 5:6], op0=ALU.subtract, op1=ALU.mult),
                                 r=["ysb", "st"], w=["yn"])
                            P.op("dve", lambda e: e.tensor_tensor(out=yn, in0=yn, in1=lnw, op=ALU.mult), r=["yn", "lnw"], w=["yn"])
                            P.op("dve", lambda e: e.tensor_tensor(out=yn, in0=yn, in1=lnb, op=ALU.add), r=["yn", "lnb"], w=["yn"])
                            b = P.bank()
                            P.op("pe", lambda e, b=b: e.transpose(out=psum[:, b, 0:128], in_=yn, identity=ident), r=["yn", "ident"], w=[("ps", b)])
                            P.op("act", lambda e, b=b: e.activation(out=ytr, in_=psum[:, b, 0:128], func=AF.Copy), r=[("ps", b)], w=["ytr"])
                            P.op("dve", lambda e: e.tensor_tensor(out=t64, in0=ytr[:, 0:64], in1=ytr[:, 64:128], op=ALU.add), r=["ytr"], w=["t64"])
                            P.op("dve", lambda e, cq=cq: e.tensor_tensor(out=t64, in0=t64, in1=bon[:, cq * 64:(cq + 1) * 64], op=ALU.add), r=["t64", "raw"], w=["t64"])
                            P.op("dve", lambda e, cq=cq, to=to: e.tensor_tensor(out=ybuf[:, to:to + 64], in0=t64, in1=gq[:, cq * 64:(cq + 1) * 64], op=ALU.mult),
                                 r=["t64", "gq"], w=["ybuf"])
                        b = P.bank()
                        P.op("pe", lambda e, b=b: e.matmul(psum[:, b, 0:128], lhsT=Phi, rhs=Sst, start=True, stop=True), r=["Phi", "S"], w=[("ps", b)])
                        P.op("dve", lambda e, b=b: e.tensor_tensor(out=Sst, in0=psum[:, b, 0:128], in1=DeltaT, op=ALU.add), r=[("ps", b), "DeltaT"], w=["S"])
                        P.op("act", lambda e: e.activation(out=Sb, in_=Sst, func=AF.Copy), r=["S"], w=["Sb"])
            for qtr in range(4):
                do_quarter(qtr)
            P.dma(ysc[:, 0, hp, :], ybuf, r=["ybuf"], w=["ysc"], grp="ysc")
            if dbg == "rwkv":
                P.dma(dbg_ya[:, hp, :], ybuf, r=["ybuf"], w=["dbg_ya"], grp="dbgya")
        for hp in range(8):
            do_pair(hp)
        P.barrier()

    if dbg not in ("tail", "nsa"):
        rwkv_phase()
    if dbg == "rwkv":
        P.op("sp", None, r=["dbg_ya"])
        P.emit(stack)
        stack.close()
        return nc
    if dbg not in ("tail", "rwkv"):
        nsa_phase()
    if dbg == "nsa":
        P.op("sp", None, r=["dbg_yb"])
        P.emit(stack)
        stack.close()
        return nc

    A.off = phase_off
    yT = A.alloc([128, 2, 8, NO], BF16)
    for wh in range(2):
        for kc in range(8):
            P.dma(yT[:, wh, kc, :], (yfake_d if dbg == "tail" else ysc)[:, wh, kc, :], r=["ysc"], w=["yT"], grp="c2")

    TOK2 = [(0, 512), (512, 512)]
    A.mark()
    tga = A.alloc([128, NO], F32)
    tgb = A.alloc([128, NO], F32)
    tpa = A.alloc([128, NO], F32)
    GA0 = 3520 + 1024 + 6 * 256 + 48
    for dc in range(KC):
        def ev_sig(dst, key):
            def f(b, t0, n):
                P.op("act", lambda e: e.activation(out=dst[:, t0:t0 + n], in_=psum[:, b, 0:n], func=AF.Sigmoid),
                     r=[("ps", b)], w=[key])
            return f
        xn_own = lambda k, t0, n: xnT[:, k, NO + t0:NO + t0 + n]
        fm_linear(w_in, 0, GA0 + dc * 128, 128, xn_own, ["xnT"], KC, TOK2, ev_sig(tga, "tga"))
        fm_linear(w_in, 0, GA0 + D + dc * 128, 128, xn_own, ["xnT"], KC, TOK2, ev_sig(tgb, "tgb"))

        def ev_pa(b, t0, n):
            P.op("dve", lambda e: e.tensor_tensor(out=tpa[:, t0:t0 + n], in0=psum[:, b, 0:n], in1=tga[:, t0:t0 + n], op=ALU.mult),
                 r=[("ps", b), "tga"], w=["tpa"])
        fm_linear(w_out_rwkv, 0, dc * 128, 128, lambda k, t0, n: yT[:, 0, k, t0:t0 + n], ["yT"], 8, TOK2, ev_pa)

        def ev_pb(b, t0, n, dc=dc):
            P.op("dve", lambda e: e.tensor_tensor(out=tgb[:, t0:t0 + n], in0=psum[:, b, 0:n], in1=tgb[:, t0:t0 + n], op=ALU.mult),
                 r=[("ps", b), "tgb"], w=["tgb"])
            P.op("pool", lambda e: e.tensor_tensor(out=mixT[:, dc, t0:t0 + n], in0=tgb[:, t0:t0 + n], in1=tpa[:, t0:t0 + n], op=ALU.add),
                 r=["tgb", "tpa"], w=[("mixT", dc)])
        fm_linear(w_out_nsa, 0, dc * 128, 128, lambda k, t0, n: yT[:, 1, k, t0:t0 + n], ["yT"], 8, TOK2, ev_pb)
    P.barrier()
    A.release()
    mix_keys = [("mixT", dc) for dc in range(KC)]
    A.off = xn_off
    h = A.alloc([128, 8, D], F32)
    for tt in range(8):
        P.dma(h[:, tt, :], xs[NO + tt * 128:NO + (tt + 1) * 128, :], w=[("h", tt)], grp="h%d" % tt)
    for dc in range(KC):
        def ev_h(b, tg, nt, dc=dc):
            for j in range(nt):
                tt = tg + j
                P.op("dve", lambda e, tt=tt, j=j: e.tensor_tensor(out=h[:, tt, dc * 128:(dc + 1) * 128], in0=psum[:, b, j * 128:(j + 1) * 128],
                                                                  in1=h[:, tt, dc * 128:(dc + 1) * 128], op=ALU.add),
                     r=[("ps", b), ("h", tt)], w=[("h", tt)])
        tm_linear(w_o, 0, dc * 128, lambda k, tt: mixT[:, k, tt * 128:(tt + 1) * 128], mix_keys, KC, 8, ev_h)

    if dbg:
        dbg_mix = nc.dram_tensor("dbg_mix", [128, KC, NO], BF16, kind="ExternalOutput").ap()
        for kc in range(KC):
            P.dma(dbg_mix[:, kc, :], mixT[:, kc, :], r=mix_keys, w=[("dbg_mix", kc)], grp="g1")
        dbg_h = nc.dram_tensor("dbg_h", [128, 8, D], F32, kind="ExternalOutput").ap()
        for tt in range(8):
            P.dma(dbg_h[:, tt, :], h[:, tt, :], r=[("h", tt)], w=[("dbg_h", tt)], grp="g2")
    grow = A.alloc([128, D], F32)
    hnT = A.alloc([128, KC, NO], BF16)
    hn = A.alloc([128, D], F32)
    jraw = A.alloc([128, 1024], F32)
    junk2 = jraw.bitcast(BF16)
    ss2 = A.alloc([128, 16], F32)
    rs2 = A.alloc([128, 16], F32)
    P.dma(grow, gmlp_d, w=["grow"], grp="c3")

    def rms_tile(tt, col, src, gkey, dst_fn):
        P.op("act", lambda e: e.activation(out=hn, in_=src, func=AF.Square), r=[("h", tt)], w=["hn"])
        P.op("dve", lambda e: e.tensor_reduce(out=ss2[:, col:col + 1], in_=hn, axis=AX.X, op=ALU.add), r=["hn"], w=[("ss2", col)])
        P.op("dve", lambda e: e.tensor_scalar(out=rs2[:, col:col + 1], in0=ss2[:, col:col + 1], scalar1=1.0 / D, scalar2=EPS,
                                              op0=ALU.mult, op1=ALU.add), r=[("ss2", col)], w=[("rs2", col)])
        P.op("act", lambda e: e.activation(out=rs2[:, col:col + 1], in_=rs2[:, col:col + 1], func=AF.Sqrt), r=[("rs2", col)], w=[("rs2", col)])
        P.op("dve", lambda e: e.reciprocal(out=rs2[:, col:col + 1], in_=rs2[:, col:col + 1]), r=[("rs2", col)], w=[("rs2", col)])
        dst_fn()

    for tt in range(8):
        def mk(tt=tt):
            P.op("dve", lambda e: e.scalar_tensor_tensor(out=hn, in0=h[:, tt, :], scalar=rs2[:, tt:tt + 1], in1=grow,
                                                         op0=ALU.mult, op1=ALU.mult),
                 r=[("h", tt), ("rs2", tt), "grow"], w=["hn"])
            for kg in range(4):
                b = P.bank()
                for j in range(4):
                    kc = kg * 4 + j
                    P.op("pe", lambda e, b=b, j=j, kc=kc: e.transpose(out=psum[:, b, j * 128:(j + 1) * 128],
                                                                     in_=hn[:, kc * 128:(kc + 1) * 128], identity=ident),
                         r=["hn", "ident"], w=[("ps", b)])
                P.op("act", lambda e, b=b, kg=kg: e.activation(out=hnT[:, kg * 4:(kg + 1) * 4, tt * 128:(tt + 1) * 128],
                                                              in_=psum[:, b, :].rearrange("p (a c) -> p a c", a=4), func=AF.Copy),
                     r=[("ps", b)], w=[("hnT", tt)])
        rms_tile(tt, tt, h[:, tt, :], "grow", mk)
    hn_keys = [("hnT", tt) for tt in range(8)]

    P.barrier()
    aT = mixT
    for g in range(4):
        for fl in range(KC):
            def ev_a(b, t0, n, fl=fl):
                sl = (t0 // 512) % 2
                tmp = jraw[:, sl * 512:sl * 512 + n]
                P.op("act", lambda e: e.activation(out=tmp, in_=psum[:, b, 0:n], func=AF.Relu), r=[("ps", b)], w=[("jr", sl)])
                P.op("pool", lambda e: e.tensor_tensor(out=aT[:, fl, t0:t0 + n], in0=tmp, in1=tmp, op=ALU.mult),
                     r=[("jr", sl)], w=[("aT", fl)])
            fm_linear(w_up, 0, (g * KC + fl) * 128, 128, lambda k, t0, n: hnT[:, k, t0:t0 + n], hn_keys, KC, TOK2, ev_a)
        a_keys = [("aT", fl) for fl in range(KC)]
        for dc in range(KC):
            def ev_h2(b, tg, nt, dc=dc):
                for j in range(nt):
                    tt = tg + j
                    P.op("dve", lambda e, tt=tt, j=j: e.tensor_tensor(out=h[:, tt, dc * 128:(dc + 1) * 128], in0=psum[:, b, j * 128:(j + 1) * 128],
                                                                      in1=h[:, tt, dc * 128:(dc + 1) * 128], op=ALU.add),
                         r=[("ps", b), ("h", tt)], w=[("h", tt)])
            tm_linear(w_down, g * 2048, dc * 128, lambda k, tt: aT[:, k, tt * 128:(tt + 1) * 128], a_keys, KC, 8, ev_h2)

    P.dma(grow, gfin_d, r=[], w=["grow"], grp="c3")
    for tt in range(8):
        def mk(tt=tt):
            P.op("dve", lambda e: e.scalar_tensor_tensor(out=h[:, tt, :], in0=h[:, tt, :], scalar=rs2[:, 8 + tt:9 + tt], in1=grow,
                                                         op0=ALU.mult, op1=ALU.mult),
                 r=[("h", tt), ("rs2", 8 + tt), "grow"], w=[("h", tt)])
            P.dma(out_d[tt * 128:(tt + 1) * 128, :], h[:, tt, :], r=[("h", tt)], w=[("out", tt)], grp="o%d" % (tt % 2))
        rms_tile(tt, 8 + tt, h[:, tt, :], "grow", mk)
    P.op("sp", None, r=[("out", tt) for tt in range(8)])
    P.emit(stack)
    stack.close()
    return nc


def host_inputs(inputs, dbg=None):
    f = lambda a: np.ascontiguousarray(np.asarray(a, dtype=np.float32))
    x = f(inputs["x"])
    shared = {
        "w_in": f(inputs["w_in"][0]),
        "w_out_rwkv": f(inputs["w_out_rwkv"][0]),
        "w_out_nsa": f(inputs["w_out_nsa"][0]),
        "w_o": f(inputs["w_o"][0]),
        "mlp_w_up": f(inputs["mlp_w_up"][0]),
        "mlp_w_down": f(inputs["mlp_w_down"][0]),
        "ident": np.eye(128, dtype=np.float32),
        "gmixT": f(np.asarray(inputs["norm_mix"][0]).reshape(KC, 128).T),
        "gmlp_row": f(np.broadcast_to(np.asarray(inputs["norm_mlp"][0])[None, :], (128, D))),
        "gfin_row": f(np.broadcast_to(np.asarray(inputs["norm_final"])[None, :], (128, D))),
    }
    BIG = 30000.0
    kk_, qq_ = np.meshgrid(np.arange(128), np.arange(128), indexing="ij")
    dq = (qq_ - kk_).astype(np.float32)
    shared["c_dfull"] = f(dq)
    shared["c_d0"] = f(np.where(qq_ >= kk_, dq, BIG))
    shared["c_d4"] = f(np.where(qq_ < kk_, dq, BIG))
    slopes = 2.0 ** (-8.0 * np.arange(1, 17) / 16.0)
    bc = np.zeros((128, 256), np.float32)
    for h_ in range(16):
        for rel in range(16):
            bc[:, h_ * 16 + rel] = -slopes[h_] * 128.0 * rel
    shared["c_bcol"] = bc
    em = np.zeros((128, 16, 128), np.float32)
    for kb in range(16):
        for k_ in range(128):
            em[2 * kb + k_ // 64, kb, k_] = 1.0
    import ml_dtypes
    shared["c_emat"] = em.astype(ml_dtypes.bfloat16)
    for nm in ["cmp_w1_k", "cmp_w2_k", "cmp_w1_v", "cmp_w2_v"]:
        shared[nm] = f(inputs[nm][0])
    for nm, src in [("c_pekT", "cmp_pe_k"), ("c_pevT", "cmp_pe_v")]:
        pt = np.zeros((128, 32), np.float32)
        pt[0:64, :] = np.asarray(inputs[src][0]).T
        shared[nm] = pt
    hh_ = np.arange(128) // 64
    same = (hh_[:, None] == hh_[None, :])
    rr_, cc_ = np.meshgrid(np.arange(128) % 64, np.arange(128) % 64, indexing="ij")
    msu = (same & (rr_ < cc_)).astype(np.float32)
    mu_ = (same & (rr_ <= cc_)).astype(np.float32)
    msl = (same & (cc_ < rr_)).astype(np.float32)
    shared["r_m1"] = f(np.concatenate([-msu, mu_], axis=1))
    shared["r_m2"] = f(np.concatenate([msu, mu_], axis=1))
    shared["r_m3"] = f(-msl)
    shared["r_identb"] = np.eye(128, dtype=np.float32).astype(ml_dtypes.bfloat16)
    shared["r_bones"] = f(same.astype(np.float32))
    rm = np.ones((128, 512), np.float32)
    rm[:, ::64] = 0.0
    shared["r_resetm"] = rm
    mu_all = np.asarray(inputs["rwkv_mu"][0], np.float32)
    pt_ = np.zeros((128, 8, 8), np.float32)
    flat = lambda a: np.asarray(a, np.float32).reshape(-1)
    srcs = [mu_all[0:1024], mu_all[1024:2048], mu_all[2048:3072], flat(inputs["rwkv_w0"][0]), flat(inputs["rwkv_a0"][0]),
            flat(inputs["rwkv_k_k"][0]), flat(inputs["rwkv_k_a"][0]), flat(inputs["rwkv_r_k"][0])]
    for j_, a_ in enumerate(srcs):
        pt_[:, :, j_] = a_.reshape(8, 128).T
    shared["r_ptab"] = pt_
    lm = np.zeros((128, 4), np.float32)
    lm[:96, 0] = mu_all[3072:3168]
    lm[:96, 1] = mu_all[3168:3264]
    lm[:, 2] = mu_all[3264:3392]
    lm[:, 3] = mu_all[3392:3520]
    shared["r_lmu"] = lm
    for nm in ["rwkv_w_up", "rwkv_a_up", "rwkv_g_up"]:
        shared[nm] = f(inputs[nm][0])
    lw_ = flat(inputs["rwkv_lnx_w"][0]).reshape(8, 2, 64)
    lb_ = flat(inputs["rwkv_lnx_b"][0]).reshape(8, 2, 64)
    lnw_blk = np.zeros((8, 128, 128), np.float32)
    lnb_blk = np.zeros((8, 128, 128), np.float32)
    for hp_ in range(8):
        for h2 in range(2):
            lnw_blk[hp_, h2 * 64:(h2 + 1) * 64, h2 * 64:(h2 + 1) * 64] = lw_[hp_, h2][None, :]
            lnb_blk[hp_, h2 * 64:(h2 + 1) * 64, h2 * 64:(h2 + 1) * 64] = lb_[hp_, h2][None, :]
    shared["r_lnw"] = lnw_blk
    shared["r_lnb"] = lnb_blk
    n_ = np.arange(127)
    cs = n_[:, None] * 16
    ss = np.arange(32)[None, :] * 64
    overlap = np.clip(np.minimum(cs + 32, ss + 64) - np.maximum(cs, ss), 0, None) / 32.0
    percore = []
    for hf in range(2):
        off = 0 if hf == 1 else 16
        tokv = np.ones(2048, np.float32)
        if hf == 0:
            tokv[:1024] = 0.0
        vn = np.ones(127, np.float32)
        if hf == 0:
            vn[:64] = 0.0
        cc = np.zeros((128, 33), np.float32)
        cc[:127, 0] = vn
        cc[:127, 1:] = overlap * vn[:, None]
        dcm = np.full((128, 8, 128), BIG, np.float32)
        sk = np.zeros((128, 8, 32), np.float32)
        sb_ = np.zeros((128, 8, 32), np.float32)
        for qt in range(8):
            t = 1024 + qt * 128 + np.arange(128)
            dist = t[None, :] - (16 * n_[:, None] + 31)
            ok = (dist >= 0) & (vn[:, None] > 0)
            dcm[:127, qt, :] = np.where(ok, dist, BIG)
            cur = t // 64
            j = np.arange(32)[None, :]
            excl = (j > cur[:, None]) | (j < off)
            forced = (~excl) & ((j == off) | (j == cur[:, None]) | (j == cur[:, None] - 1))
            sk[:, qt, :] = np.where(excl | forced, 0.0, 1.0)
            sb_[:, qt, :] = np.where(excl, -1e30, np.where(forced, 1e30, 0.0))
        percore.append({"c_vtok": f(tokv.reshape(16, 128).T), "c_ccst": cc, "c_dc": dcm, "c_skeep": sk, "c_sbias": sb_})
    maps = []
    for c in range(8):
        b, hf = c // 2, c % 2
        if hf == 1:
            xs_ = x[b]
        else:
            xs_ = np.concatenate([np.zeros((NO, D), np.float32), x[b, :NO]], axis=0)
        m = dict(shared)
        m.update(percore[hf])
        m["xs"] = f(xs_)
        maps.append(m)
    return maps


_NC = {}


def kernel(**inputs):
    if "nc" not in _NC:
        _NC["nc"] = build()
    maps = host_inputs(inputs)
    res = run_bass_kernel_spmd(_NC["nc"], maps, core_ids=list(range(8)))
    out = np.zeros((4, 2048, D), np.float32)
    for c in range(8):
        b, hf = c // 2, c % 2
        out[b, hf * NO:(hf + 1) * NO] = res.results[c]["out"]
    return out
```
